# Optimizing a Trainium2 kernel written in Bass

```python
import math
import jax
import jax.numpy as jnp
from jax import lax
import numpy as np

D_MODEL = 1024
BATCH = 4
SEQ = 8192
DEPTH = 2

NSA_HEADS = 8
NSA_KV_GROUPS = 2
NSA_HEAD_DIM = 64
NSA_CMP_LEN = 32
NSA_CMP_STRIDE = 16
NSA_SEL_LEN = 64
NSA_TOP_N = 16
NSA_WINDOW = 512
NSA_PHI_HIDDEN = 128
NSA_QBLOCK = 128
FORCE_BONUS = 1.0e4
POOL_GROUPS = 4
POOL_GROUP_DIM = 128
POOL_WINDOWS = (2, 4, 8, 16)
DN_HEADS = 4
DN_HEAD_DIM = 128
DN_CONV = 4
DN_CHUNK = 64
D_FF = 4 * D_MODEL
N_BRANCH = 3
EPS = 1e-6
NEG = -1e30

NSA_Q_W = NSA_HEADS * NSA_HEAD_DIM
NSA_KV_W = NSA_KV_GROUPS * NSA_HEAD_DIM
POOL_W = POOL_GROUPS * POOL_GROUP_DIM
DN_W = DN_HEADS * DN_HEAD_DIM
IN_SIZES = (NSA_Q_W, NSA_KV_W, NSA_KV_W, NSA_KV_W, NSA_KV_W, NSA_KV_W, NSA_KV_W,
            3 * NSA_HEADS, POOL_W, 3 * DN_W, DN_W, DN_HEADS, DN_HEADS, N_BRANCH * D_MODEL)
N_IN = sum(IN_SIZES)

kernel_name = "hybrid_nsa_pool_deltanet_block"


def rmsnorm(x, g):
    xf = x.astype(jnp.float32)
    y = xf * lax.rsqrt(jnp.mean(xf * xf, axis=-1, keepdims=True) + EPS)
    return (y * g.astype(jnp.float32)).astype(x.dtype)


def l2norm(t):
    return t * lax.rsqrt(jnp.sum(t * t, axis=-1, keepdims=True) + EPS)


def nsa_compress(k, pos, w1, w2):
    B, S, G, dh = k.shape
    nc = (S - NSA_CMP_LEN) // NSA_CMP_STRIDE + 1
    idx = jnp.arange(nc)[:, None] * NSA_CMP_STRIDE + jnp.arange(NSA_CMP_LEN)[None, :]
    blk = k[:, idx] + pos[None, None, :, None, :]
    blk = blk.transpose(0, 1, 3, 2, 4).reshape(B, nc, G, NSA_CMP_LEN * dh)
    return jax.nn.silu(blk @ w1) @ w2


def nsa_attention(q, kc, vc, ks, vs, kw, vw, gates, slopes):
    B, S, H, dh = q.shape
    G = NSA_KV_GROUPS
    R = H // G
    Q = NSA_QBLOCK
    nc = kc.shape[1]
    nsel = S // NSA_SEL_LEN
    top_n = min(NSA_TOP_N, nsel)
    f32 = jnp.float32
    q = q.reshape(B, S, G, R, dh) * (dh ** -0.5)
    gates = gates.reshape(B, S, G, R, 3)
    m = slopes.reshape(G, R)[None, :, :, None, None]
    c_start = jnp.arange(nc) * NSA_CMP_STRIDE
    c_end = c_start + NSA_CMP_LEN - 1
    s_start = jnp.arange(nsel) * NSA_SEL_LEN
    blk_id = jnp.arange(nsel)
    overlap = (jnp.clip(jnp.minimum(s_start[:, None] + NSA_SEL_LEN, c_start[None, :] + NSA_CMP_LEN)
                        - jnp.maximum(s_start[:, None], c_start[None, :]), 0)
               / NSA_CMP_STRIDE).astype(f32)
    ks_t = ks.transpose(0, 2, 1, 3)
    vs_t = vs.transpose(0, 2, 1, 3)
    pad = ((0, 0), (NSA_WINDOW, 0), (0, 0), (0, 0))
    kw_p = jnp.pad(kw, pad)
    vw_p = jnp.pad(vw, pad)
    gather = jax.vmap(jax.vmap(lambda a, i: a[i]))

    def block(qb):
        t0 = qb * Q
        t = t0 + jnp.arange(Q)
        qq = lax.dynamic_slice_in_dim(q, t0, Q, axis=1)
        gb = lax.dynamic_slice_in_dim(gates, t0, Q, axis=1)
        vis = c_end[None, :] <= t[:, None]
        dist = (t[:, None] - c_end[None, :]).astype(f32)
        lg = jnp.einsum('bqgrd,bngd->bgrqn', qq, kc).astype(f32) - m * dist
        p_c = jnp.where(vis, jax.nn.softmax(jnp.where(vis, lg, NEG), axis=-1), 0.0)
        o_c = jnp.einsum('bgrqn,bngd->bqgrd', p_c.astype(vc.dtype), vc)
        imp = jnp.einsum('bgrqn,jn->bgqj', p_c, overlap)
        valid = s_start[None, :] <= t[:, None]
        forced = (blk_id[None, :] == 0) | (blk_id[None, :] == (t // NSA_SEL_LEN)[:, None])
        score = jnp.where(valid, imp + FORCE_BONUS * forced, NEG)
        _, idx = lax.top_k(score, top_n)
        pos = (idx[..., None] * NSA_SEL_LEN + jnp.arange(NSA_SEL_LEN)).reshape(B, G, Q * top_n * NSA_SEL_LEN)
        k_sel = gather(ks_t, pos).reshape(B, G, Q, top_n * NSA_SEL_LEN, dh)
        v_sel = gather(vs_t, pos).reshape(B, G, Q, top_n * NSA_SEL_LEN, dh)
        pos = pos.reshape(B, G, Q, top_n * NSA_SEL_LEN)
        dsel = t[None, None, :, None] - pos
        lg = jnp.einsum('bqgrd,bgqkd->bgrqk', qq, k_sel).astype(f32) - m * dsel[:, :, None].astype(f32)
        p_s = jax.nn.softmax(jnp.where((dsel >= 0)[:, :, None], lg, NEG), axis=-1)
        o_s = jnp.einsum('bgrqk,bgqkd->bqgrd', p_s.astype(v_sel.dtype), v_sel)
        kwin = lax.dynamic_slice_in_dim(kw_p, t0, NSA_WINDOW + Q, axis=1)
        vwin = lax.dynamic_slice_in_dim(vw_p, t0, NSA_WINDOW + Q, axis=1)
        s = t0 - NSA_WINDOW + jnp.arange(NSA_WINDOW + Q)
        dw = t[:, None] - s[None, :]
        okw = (dw >= 0) & (dw < NSA_WINDOW) & (s[None, :] >= 0)
        lg = jnp.einsum('bqgrd,bkgd->bgrqk', qq, kwin).astype(f32) - m * dw.astype(f32)
        p_w = jax.nn.softmax(jnp.where(okw, lg, NEG), axis=-1)
        o_w = jnp.einsum('bgrqk,bkgd->bqgrd', p_w.astype(vwin.dtype), vwin)
        o = gb[..., 0:1] * o_c + gb[..., 1:2] * o_s + gb[..., 2:3] * o_w
        return o.reshape(B, Q, H * dh)

    out = lax.map(block, jnp.arange(S // Q))
    return out.transpose(1, 0, 2, 3).reshape(B, S, H * dh)


def multiscale_pool(p, w, scale):
    B, S, _ = p.shape
    f32 = jnp.float32
    pg = p.reshape(B, S, POOL_GROUPS, POOL_GROUP_DIM).astype(f32)
    cs = jnp.pad(jnp.cumsum(pg, axis=1), ((0, 0), (1, 0), (0, 0), (0, 0)))
    win = jnp.array(POOL_WINDOWS)
    t = jnp.arange(S)
    lo = jnp.maximum(t[:, None] + 1 - win[None, :], 0)
    cnt = jnp.minimum(t[:, None] + 1, win[None, :]).astype(f32)
    lower = cs[:, lo, jnp.arange(POOL_GROUPS)[None, :]]
    y = ((cs[:, 1:] - lower) / cnt[None, :, :, None] - pg).astype(p.dtype)
    y = jnp.einsum('bsgc,gcd->bsgd', y, w).reshape(B, S, POOL_W)
    return y * scale


def causal_dwconv(x, w):
    K, C = w.shape
    return lax.conv_general_dilated(x, w[:, None, :], window_strides=(1,), padding=[(K - 1, 0)],
                                    dimension_numbers=('NWC', 'WIO', 'NWC'), feature_group_count=C)


def gated_deltanet(qkv, z, beta_logit, a_logit, conv_w, A_log, dt_bias, norm_g):
    B, S, _ = qkv.shape
    H, dk, C = DN_HEADS, DN_HEAD_DIM, DN_CHUNK
    nC = S // C
    f32 = jnp.float32
    qkv = jax.nn.silu(causal_dwconv(qkv, conv_w)).astype(f32)
    q, k, v = jnp.split(qkv, 3, axis=-1)
    q = l2norm(q.reshape(B, S, H, dk)) * (dk ** -0.5)
    k = l2norm(k.reshape(B, S, H, dk))
    v = v.reshape(B, S, H, dk)
    beta = jax.nn.sigmoid(beta_logit.astype(f32))
    g = -jnp.exp(A_log.astype(f32)) * jax.nn.softplus(a_logit.astype(f32) + dt_bias.astype(f32))

    def chunks(a):
        return jnp.moveaxis(a.reshape(B, nC, C, H, *a.shape[3:]), 3, 1)

    q, k, v, beta, g = chunks(q), chunks(k), chunks(v), chunks(beta), chunks(g)
    gc = jnp.cumsum(g, axis=-1)
    tril = jnp.tril(jnp.ones((C, C), bool))
    strict = jnp.tril(jnp.ones((C, C), bool), -1)
    diff = gc[..., :, None] - gc[..., None, :]
    Lmask = jnp.where(tril, jnp.exp(jnp.where(tril, diff, 0.0)), 0.0)
    kb = k * beta[..., None]
    vb = v * beta[..., None]
    N = jnp.where(strict, jnp.einsum('bhnid,bhnjd->bhnij', kb, k) * Lmask, 0.0)
    A = N + jnp.eye(C, dtype=f32)
    rhs = jnp.concatenate([vb, kb * jnp.exp(gc)[..., None]], axis=-1)
    sol = lax.linalg.triangular_solve(A, rhs, left_side=True, lower=True)
    u, w = sol[..., :dk], sol[..., dk:]
    Aqk = jnp.einsum('bhnid,bhnjd->bhnij', q, k) * Lmask
    glast = gc[..., -1]
    q_e = q * jnp.exp(gc)[..., None]
    k_e = k * jnp.exp(glast[..., None] - gc)[..., None]

    def step(state, xs):
        q_c, k_c, u_c, w_c, a_c, d_c = xs
        v_new = u_c - w_c @ state
        o = q_c @ state + a_c @ v_new
        state = state * d_c[..., None, None] + jnp.swapaxes(k_c, -1, -2) @ v_new
        return state, o

    xs = tuple(jnp.moveaxis(a, 2, 0) for a in (q_e, k_e, u, w, Aqk, jnp.exp(glast)))
    _, o = lax.scan(step, jnp.zeros((B, H, dk, dk), f32), xs)
    o = o.transpose(1, 0, 3, 2, 4).reshape(B, S, H, dk)
    o = rmsnorm(o, norm_g) * jax.nn.silu(z.reshape(B, S, H, dk).astype(f32))
    return o.reshape(B, S, H * dk).astype(z.dtype)


def hybrid_mixer(h, w_in, phi_k1, phi_k2, phi_v1, phi_v2, pos_k, pos_v, pool_w, pool_scale,
                 dn_conv_w, dn_A_log, dn_dt_bias, dn_norm_g, w_bn, w_bp, w_bd, w_out, slopes):
    B, S, _ = h.shape
    proj = h @ w_in
    points = np.cumsum(IN_SIZES)[:-1].tolist()
    (nq, nkc, nvc, nks, nvs, nkw, nvw, ngate, pin, dqkv, dz, dbeta, da, mg) = jnp.split(proj, points, axis=-1)
    G = NSA_KV_GROUPS
    rs = lambda a, n: a.reshape(B, S, n, -1)
    kc = nsa_compress(rs(nkc, G), pos_k, phi_k1, phi_k2)
    vc = nsa_compress(rs(nvc, G), pos_v, phi_v1, phi_v2)
    y_a = nsa_attention(rs(nq, NSA_HEADS), kc, vc, rs(nks, G), rs(nvs, G), rs(nkw, G), rs(nvw, G),
                        jax.nn.sigmoid(ngate).reshape(B, S, NSA_HEADS, 3), slopes)
    y_b = multiscale_pool(pin, pool_w, pool_scale)
    y_c = gated_deltanet(dqkv, dz, dbeta, da, dn_conv_w, dn_A_log, dn_dt_bias, dn_norm_g)
    g_a, g_b, g_c = jnp.split(jax.nn.sigmoid(mg), 3, axis=-1)
    merged = g_a * (y_a @ w_bn) + g_b * (y_b @ w_bp) + g_c * (y_c @ w_bd)
    return merged @ w_out


def setup_inputs(seed: int = 0) -> dict:
    key = jax.random.key(seed)
    ks = jax.random.split(key, 32)
    f32 = jnp.float32
    L = DEPTH
    nrm = lambda k, shape, s: jax.random.normal(k, shape, f32) * s
    dt = jnp.exp(jax.random.uniform(ks[20], (L, DN_HEADS), f32, math.log(1e-3), math.log(1e-1)))
    return {
        "x": nrm(ks[0], (BATCH, SEQ, D_MODEL), 1.0),
        "c": nrm(ks[1], (BATCH, D_MODEL), 1.0),
        "ada_w": nrm(ks[2], (L, D_MODEL, 6 * D_MODEL), 0.25 * D_MODEL ** -0.5),
        "ada_b": nrm(ks[3], (L, 6 * D_MODEL), 0.01),
        "norm1_g": 1.0 + nrm(ks[4], (L, D_MODEL), 0.02),
        "norm2_g": 1.0 + nrm(ks[5], (L, D_MODEL), 0.02),
        "w_in": nrm(ks[6], (L, D_MODEL, N_IN), D_MODEL ** -0.5),
        "phi_k1": nrm(ks[7], (L, NSA_CMP_LEN * NSA_HEAD_DIM, NSA_PHI_HIDDEN), (NSA_CMP_LEN * NSA_HEAD_DIM) ** -0.5),
        "phi_k2": nrm(ks[8], (L, NSA_PHI_HIDDEN, NSA_HEAD_DIM), NSA_PHI_HIDDEN ** -0.5),
        "phi_v1": nrm(ks[9], (L, NSA_CMP_LEN * NSA_HEAD_DIM, NSA_PHI_HIDDEN), (NSA_CMP_LEN * NSA_HEAD_DIM) ** -0.5),
        "phi_v2": nrm(ks[10], (L, NSA_PHI_HIDDEN, NSA_HEAD_DIM), NSA_PHI_HIDDEN ** -0.5),
        "pos_k": nrm(ks[11], (L, NSA_CMP_LEN, NSA_HEAD_DIM), 0.1),
        "pos_v": nrm(ks[12], (L, NSA_CMP_LEN, NSA_HEAD_DIM), 0.1),
        "pool_w": nrm(ks[13], (L, POOL_GROUPS, POOL_GROUP_DIM, POOL_GROUP_DIM), POOL_GROUP_DIM ** -0.5),
        "pool_scale": 1.0 + nrm(ks[14], (L, POOL_W), 0.02),
        "dn_conv_w": nrm(ks[15], (L, DN_CONV, 3 * DN_W), DN_CONV ** -0.5),
        "dn_A_log": jnp.log(jax.random.uniform(ks[16], (L, DN_HEADS), f32, 1.0, 16.0)),
        "dn_dt_bias": dt + jnp.log(-jnp.expm1(-dt)),
        "dn_norm_g": 1.0 + nrm(ks[17], (L, DN_HEAD_DIM), 0.02),
        "w_branch_nsa": nrm(ks[18], (L, NSA_Q_W, D_MODEL), NSA_Q_W ** -0.5),
        "w_branch_pool": nrm(ks[19], (L, POOL_W, D_MODEL), POOL_W ** -0.5),
        "w_branch_dn": nrm(ks[21], (L, DN_W, D_MODEL), DN_W ** -0.5),
        "w_out": nrm(ks[22], (L, D_MODEL, D_MODEL), D_MODEL ** -0.5),
        "mlp_w1": nrm(ks[23], (L, D_MODEL, D_FF), D_MODEL ** -0.5),
        "mlp_w2": nrm(ks[24], (L, D_FF, D_MODEL), D_FF ** -0.5),
        "final_g": 1.0 + nrm(ks[25], (D_MODEL,), 0.02),
    }


def reference(x, c, ada_w, ada_b, norm1_g, norm2_g, w_in, phi_k1, phi_k2, phi_v1, phi_v2, pos_k, pos_v,
              pool_w, pool_scale, dn_conv_w, dn_A_log, dn_dt_bias, dn_norm_g, w_branch_nsa, w_branch_pool,
              w_branch_dn, w_out, mlp_w1, mlp_w2, final_g):
    slopes = 2.0 ** (-(8.0 / NSA_HEADS) * (jnp.arange(NSA_HEADS, dtype=jnp.float32) + 1.0))
    cond = jax.nn.silu(c)
    for l in range(DEPTH):
        mod = (cond @ ada_w[l] + ada_b[l])[:, None, :]
        sh1, sc1, gt1, sh2, sc2, gt2 = jnp.split(mod, 6, axis=-1)
        h = rmsnorm(x, norm1_g[l]) * (1.0 + sc1) + sh1
        y = hybrid_mixer(h, w_in[l], phi_k1[l], phi_k2[l], phi_v1[l], phi_v2[l], pos_k[l], pos_v[l],
                         pool_w[l], pool_scale[l], dn_conv_w[l], dn_A_log[l], dn_dt_bias[l], dn_norm_g[l],
                         w_branch_nsa[l], w_branch_pool[l], w_branch_dn[l], w_out[l], slopes)
        x = x + gt1 * y
        h = rmsnorm(x, norm2_g[l]) * (1.0 + sc2) + sh2
        x = x + gt2 * (jnp.square(jax.nn.relu(h @ mlp_w1[l])) @ mlp_w2[l])
    return rmsnorm(x, final_g)
```

```python
import numpy as np
import ml_dtypes
import concourse.bass as bass
import concourse.mybir as mybir
from concourse.bass_utils import run_bass_kernel_spmd

F32 = mybir.dt.float32
BF16 = mybir.dt.bfloat16
I32 = mybir.dt.int32
F32R = mybir.dt.float32r
AF = mybir.ActivationFunctionType
ALU = mybir.AluOpType
AX = mybir.AxisListType

D_MODEL = 1024
IN_SIZES = (512, 128, 128, 128, 128, 128, 128, 24, 512, 1536, 512, 4, 4, 3072)
BATCH = 4
SEQ = 8192
DEPTH = 2
N_CORES = 8
TOK = SEQ // 2
EPS = 1e-6


class Buf:
    __slots__ = ("name", "w", "r")

    def __init__(self, name=""):
        self.name = name
        self.w = {}
        self.r = {}


class KB:
    def __init__(self, nc, n_dma_sems=60, n_sw=12):
        self.nc = nc
        self.E = {"pe": nc.tensor, "dve": nc.vector, "act": nc.scalar, "pool": nc.gpsimd, "sp": nc.sync}
        self.sems = []
        self.semidx = {}
        for e in ("pe", "dve", "act", "pool"):
            self.semidx[e] = len(self.sems)
            self.sems.append(nc.alloc_semaphore(name="sem_" + e))
        self.cnt = {e: 0 for e in self.semidx}
        self.seen = {e: {} for e in self.E}
        self.dslots = {"hw": [], "sw": []}
        for kind in ("hw", "sw"):
            for i in range(n_dma_sems if kind == "hw" else n_sw):
                self.dslots[kind].append([len(self.sems), 0])
                self.sems.append(nc.alloc_semaphore(name=f"dsem_{kind}{i}"))
        self.dnext = {"hw": 0, "sw": 0}
        self.n_inst = 0
        self.n_wait = 0
        self._banks = None
        self._bank_next = 0

    def _deps(self, reads, writes):
        deps = {}
        for b in reads:
            for si, v in b.w.items():
                if v > deps.get(si, 0):
                    deps[si] = v
        for b in writes:
            for si, v in b.w.items():
                if v > deps.get(si, 0):
                    deps[si] = v
            for si, v in b.r.items():
                if v > deps.get(si, 0):
                    deps[si] = v
        return deps

    def _wait(self, e, deps):
        eng = self.E[e]
        seen = self.seen[e]
        pe_si = self.semidx["pe"]
        for si, v in deps.items():
            if e == "pe" and si == pe_si:
                continue
            if seen.get(si, 0) >= v:
                continue
            eng.wait_ge(self.sems[si], v)
            seen[si] = v
            self.n_wait += 1

    def _mark(self, tok, reads, writes):
        si, v = tok
        for b in reads:
            if v > b.r.get(si, 0):
                b.r[si] = v
        for b in writes:
            if v > b.w.get(si, 0):
                b.w[si] = v

    def op(self, e, fn, reads=(), writes=()):
        self._wait(e, self._deps(reads, writes))
        inst = fn(self.E[e])
        self.cnt[e] += 1
        si = self.semidx[e]
        inst.then_inc(self.sems[si], 1)
        self._mark((si, self.cnt[e]), reads, writes)
        self.n_inst += 1
        return inst

    def dma(self, q, out, in_, reads=(), writes=(), **kw):
        deps = self._deps(reads, writes)
        kind = "sw" if q == "pool" else "hw"
        slot = self.dslots[kind][self.dnext[kind]]
        self.dnext[kind] = (self.dnext[kind] + 1) % len(self.dslots[kind])
        si, target = slot
        if target > 0 and target > deps.get(si, 0):
            deps[si] = target
        self._wait(q, deps)
        inst = self.E[q].dma_start(out=out, in_=in_, **kw)
        inst.then_inc(self.sems[si], 16)
        slot[1] = target + 16
        self._mark((si, target + 16), reads, writes)
        self.n_inst += 1
        return inst

    def finish(self, q="sp"):
        deps = {}
        for kind in ("hw", "sw"):
            for si, target in self.dslots[kind]:
                if target > 0:
                    deps[si] = target
        for e, c in self.cnt.items():
            if c > 0:
                deps[self.semidx[e]] = c
        self._wait(q, deps)

    def open_scope(self, tag):
        from contextlib import ExitStack
        self._scope = ExitStack()
        self._tag = tag
        return self._scope

    def T(self, name, shape, dtype):
        scope = getattr(self, "_scope", None)
        full = f"{getattr(self, '_tag', '')}{name}"
        if scope is None:
            return self.nc.alloc_sbuf_tensor(full, shape, dtype)
        return scope.enter_context(self.nc.sbuf_tensor(full, shape, dtype))

    def S(self, name, shape, dtype):
        return self.nc.sbuf_tensor(f"{getattr(self, '_tag', '')}{name}", shape, dtype)

    def close_scope(self):
        self.barrier()
        self._scope.close()
        self._scope = None
        self._tag = ""

    def fill_reg(self, val):
        regs = self.__dict__.setdefault("_fill_regs", {})
        if val not in regs:
            regs[val] = self.nc.gpsimd.to_reg(val)
        return regs[val]

    def barrier(self):
        for q in self.E:
            self.finish(q)

    def init_banks(self):
        if self._banks is not None:
            return
        self._banks = []
        for i in range(8):
            t = self.nc.alloc_psum_tensor(f"bank{i}", [128, 512], F32)
            self._banks.append((t, Buf(f"bank{i}")))

    def bank(self):
        t, b = self._banks[self._bank_next]
        self._bank_next = (self._bank_next + 1) % 8
        return t, b


def _begin(nc, kb, tag):
    own = nc is None
    if own:
        nc = bass.Bass("TRN2", target_bir_lowering=False)
        kb = KB(nc)
        kb.init_banks()
    else:
        kb.open_scope(tag)
    return nc, kb, own


def _end(kb, own):
    if own:
        kb.finish("sp")
    else:
        kb.close_scope()


def _dram(nc, io, name, shape, dtype, kind):
    if io is not None:
        return io[name]
    return nc.dram_tensor(name, shape, dtype, kind=kind).ap()


def _bf16(a):
    return np.ascontiguousarray(a).astype(ml_dtypes.bfloat16)


class NormCtx:
    pass


def emit_identity(kb, nc, dtype=BF16, name="ident"):
    it = kb.T(name + "_i", [128, 128], I32)
    idt = kb.T(name, [128, 128], dtype)
    b = Buf(name)
    kb.op("pool", lambda e: e.iota(it[:], pattern=[[1, 128]], base=0, channel_multiplier=-1), writes=[b])
    kb.op("dve", lambda e: e.tensor_scalar(out=idt[:], in0=it[:], scalar1=0, scalar2=None, op0=ALU.is_equal),
          reads=[b], writes=[b])
    return idt, b


def emit_mod(kb, nc, c_ap, adaw_ap, adab_ap, n_vec, name):
    ncol = n_vec * 1024
    cT = kb.T(name + "_cT", [128, 8], F32)
    sT = kb.T(name + "_sT", [128, 8], F32)
    bT = kb.T(name + "_bT", [128, n_vec * 8], F32)
    modT = kb.T(name + "_modT", [128, n_vec * 8], F32)
    bc, bb, bm = Buf(), Buf(), Buf()
    with nc.allow_non_contiguous_dma(reason="tiny transposed vector loads"):
        kb.dma("sp", cT[:], c_ap.rearrange("(k p) -> p k", p=128), writes=[bc])
        kb.dma("sp", bT[:], adab_ap[0:ncol].rearrange("(j p) -> p j", p=128), writes=[bb])
    kb.op("act", lambda e: e.activation(out=sT[:], in_=cT[:], func=AF.Silu), reads=[bc], writes=[bc])
    ps, pb = kb.bank()
    ngrp = ncol // 256
    with kb.S(name + "_w0", [128, 8, 256], F32) as wt0, kb.S(name + "_w1", [128, 8, 256], F32) as wt1:
        wts = [wt0, wt1]
        wb = [Buf(), Buf()]
        for g in range(ngrp):
            wt, wbuf = wts[g % 2], wb[g % 2]
            kb.dma("sp", wt[:], adaw_ap[:, g * 256:(g + 1) * 256].rearrange("(k p) n -> p k n", p=128), writes=[wbuf])
            for j in range(2):
                col = g * 2 + j
                for k in range(8):
                    kb.op("pe", lambda e, wt=wt, j=j, k=k, col=col: e.matmul(
                        ps[:, col:col + 1], lhsT=wt[:, k, j * 128:(j + 1) * 128], rhs=sT[:, k:k + 1],
                        start=(k == 0), stop=(k == 7)), reads=[wbuf, bc], writes=[pb])
        kb.op("dve", lambda e: e.tensor_tensor(out=modT[:], in0=ps[:, 0:n_vec * 8], in1=bT[:], op=ALU.add),
              reads=[bb], writes=[bm, pb])
        kb.barrier()
    return modT, bm


def norm_slots(kb, n, name="ns"):
    return [dict(xn=kb.T(f"{name}_xn{i}", [128, D_MODEL], BF16), ss=kb.T(f"{name}_ss{i}", [128, 1], F32),
                 rstd=kb.T(f"{name}_rstd{i}", [128, 1], F32), b=Buf()) for i in range(n)]


def emit_norm_pre(kb, x_tile, xb, sl):
    xn, ss, rstd, sb = sl["xn"], sl["ss"], sl["rstd"], sl["b"]
    kb.op("act", lambda e: e.activation(out=xn[:], in_=x_tile, func=AF.Square, accum_out=ss[:]), reads=[xb], writes=[sb])
    kb.op("dve", lambda e: e.tensor_scalar(out=ss[:], in0=ss[:], scalar1=1.0 / D_MODEL, scalar2=EPS, op0=ALU.mult, op1=ALU.add), writes=[sb])
    kb.op("act", lambda e: e.activation(out=ss[:], in_=ss[:], func=AF.Sqrt), writes=[sb])
    kb.op("dve", lambda e: e.reciprocal(out=rstd[:], in_=ss[:]), writes=[sb])
    kb.op("dve", lambda e: e.tensor_scalar(out=xn[:], in0=x_tile, scalar1=rstd[:, 0:1], scalar2=None, op0=ALU.mult), reads=[xb], writes=[sb])


def emit_norm_post(kb, sl, hT_dst, hT_buf, A_ap, B_ap, ab_buf, ident, ident_buf):
    xn, sb = sl["xn"], sl["b"]
    ps, pb = kb.bank()
    psb = ps[:].bitcast(BF16)
    for k in range(8):
        kb.op("pe", lambda e, k=k: e.transpose(psb[:, k * 128:(k + 1) * 128], xn[:, k * 128:(k + 1) * 128], ident[:]),
              reads=[sb, ident_buf], writes=[pb])
    for k in range(8):
        kb.op("act", lambda e, k=k: e.activation(out=hT_dst[:, k, :], in_=psb[:, k * 128:(k + 1) * 128], func=AF.Identity,
                                                 scale=A_ap[:, k:k + 1], bias=B_ap[:, k:k + 1]), reads=[ab_buf], writes=[hT_buf, pb])


def emit_mod_AB(kb, nc, modT, bm, g_ap, sh_idx, sc_idx, name):
    gT = kb.T(name + "_gT", [128, 8], F32)
    A = kb.T(name + "_A", [128, 8], F32)
    Bt = kb.T(name + "_B", [128, 8], F32)
    b = Buf()
    with nc.allow_non_contiguous_dma(reason="tiny transposed vector loads"):
        kb.dma("sp", gT[:], g_ap.rearrange("(k p) -> p k", p=128), writes=[b])
    kb.op("dve", lambda e: e.scalar_tensor_tensor(out=A[:], in0=modT[:, sc_idx * 8:(sc_idx + 1) * 8], scalar=1.0,
                                                  in1=gT[:], op0=ALU.add, op1=ALU.mult),
          reads=[bm, b], writes=[b])
    kb.op("dve", lambda e: e.tensor_copy(out=Bt[:], in_=modT[:, sh_idx * 8:(sh_idx + 1) * 8]), reads=[bm], writes=[b])
    return A, Bt, b


A_FM_BF = 8
A_FM_F32 = 16
A_FM = A_FM_BF + A_FM_F32
A_TM0 = A_FM * 128
A_TM1 = A_TM0 + 288
A_NCOL = A_TM1 + 512


def build_phaseA(ntok=TOK, nc=None, kb=None, io=None, tag=""):
    nc, kb, own = _begin(nc, kb, tag)
    x = _dram(nc, io, "x", [ntok, D_MODEL], F32, "ExternalInput")
    cvec = _dram(nc, io, "c", [D_MODEL], F32, "ExternalInput")
    adaw = _dram(nc, io, "adaw", [D_MODEL, 2048], F32, "ExternalInput")
    adab = _dram(nc, io, "adab", [2048], F32, "ExternalInput")
    g1 = _dram(nc, io, "g1", [D_MODEL], F32, "ExternalInput")
    w = _dram(nc, io, "w", [D_MODEL, A_NCOL], F32, "ExternalInput")
    o_fm_bf = _dram(nc, io, "fm_bf", [A_FM_BF * 128, ntok], BF16, "ExternalOutput")
    o_fm_f32 = _dram(nc, io, "fm_f32", [A_FM_F32 * 128, ntok], F32, "ExternalOutput")
    o_tm_bf = _dram(nc, io, "tm_bf", [ntok, 256], BF16, "ExternalOutput")
    o_tm_f32 = _dram(nc, io, "tm_f32", [ntok, 544], F32, "ExternalOutput")

    ident, identb = emit_identity(kb, nc)
    modT, bm = emit_mod(kb, nc, cvec, adaw, adab, 2, "mod")
    A, Bt, abb = emit_mod_AB(kb, nc, modT, bm, g1, 0, 1, "n1")

    wsb = kb.T("wsb", [128, 8, A_NCOL], BF16)
    wbufs = [Buf(f"w{i}") for i in range((A_NCOL + 1023) // 1024)]
    for bi, c0 in enumerate(range(0, A_NCOL, 1024)):
        c1 = min(A_NCOL, c0 + 1024)
        for k in range(8):
            kb.dma("pool", wsb[:, k, c0:c1], w[k * 128:(k + 1) * 128, c0:c1], writes=[wbufs[bi]])

    def wdep(c0, c1):
        return [wbufs[i] for i in range(c0 // 1024, (c1 - 1) // 1024 + 1)]

    NB = ntok // 512
    xts = [kb.T(f"xt{i}", [128, D_MODEL], F32) for i in range(4)]
    xbs = [Buf() for _ in range(4)]
    hTs = [kb.T(f"hT{i}", [128, 8, 512], BF16) for i in range(2)]
    hbs = [Buf(), Buf()]
    nsl = norm_slots(kb, 4)

    def norm_pre(blk):
        for t in range(4):
            ti = blk * 4 + t
            kb.dma("sp", xts[t][:], x[ti * 128:(ti + 1) * 128, :], writes=[xbs[t]])
            emit_norm_pre(kb, xts[t][:], xbs[t], nsl[t])

    def norm_post(blk, t):
        emit_norm_post(kb, nsl[t], hTs[blk % 2][:, :, t * 128:(t + 1) * 128], hbs[blk % 2], A, Bt, abb, ident, identb)

    norm_pre(0)
    for t in range(4):
        norm_post(0, t)
    st_bf = [kb.T(f"stbf{i}", [128, A_FM_BF, 512], BF16) for i in range(2)]
    st_f32 = [kb.T(f"stf{i}", [128, 8, 512], F32) for i in range(2)]
    stfb = [Buf(), Buf()]
    st_tb = [kb.T(f"sttb{i}", [128, 4, 256], BF16) for i in range(2)]
    st_tf = [kb.T(f"sttf{i}", [128, 4, 544], F32) for i in range(2)]
    stb = [[Buf() for _ in range(4)] for _ in range(2)]
    ev = 0
    for blk in range(NB):
        hT, hb = hTs[blk % 2], hbs[blk % 2]
        nxt = blk + 1 < NB
        if nxt:
            norm_pre(blk + 1)
        sb_, stb_, stf_ = st_bf[blk % 2], st_tb[blk % 2], st_tf[blk % 2]
        b_bf, _unused, b_tb, b_tf = stb[blk % 2]
        for c in range(A_FM):
            ps, pb = kb.bank()
            for k in range(8):
                kb.op("pe", lambda e, c=c, k=k, ps=ps: e.matmul(ps[:], lhsT=wsb[:, k, c * 128:(c + 1) * 128], rhs=hT[:, k, :],
                                                         start=(k == 0), stop=(k == 7)), reads=wdep(c * 128, (c + 1) * 128) + [hb], writes=[pb])
            eng = "act" if ev % 2 == 0 else "dve"
            ev += 1
            if c < A_FM_BF:
                dst, db = sb_[:, c, :], b_bf
                scale = 0.125 if c < 4 else 1.0
            else:
                hf = (c - A_FM_BF) // 8
                dst, db = st_f32[hf][:, (c - A_FM_BF) % 8, :], stfb[hf]
                scale = 1.0
            if eng == "act":
                kb.op("act", lambda e, dst=dst, ps=ps, scale=scale: e.activation(out=dst, in_=ps[:], func=AF.Copy, scale=scale),
                      writes=[db, pb])
            else:
                kb.op("dve", lambda e, dst=dst, ps=ps, scale=scale: e.tensor_scalar(out=dst, in0=ps[:], scalar1=scale, scalar2=None,
                                                                           op0=ALU.mult), writes=[db, pb])
            if c >= A_FM_BF and (c - A_FM_BF) % 8 == 7:
                hf = (c - A_FM_BF) // 8
                kb.dma("sp", o_fm_f32[hf * 1024:(hf + 1) * 1024, blk * 512:(blk + 1) * 512].rearrange("(c p) t -> p c t", p=128),
                       st_f32[hf][:], reads=[stfb[hf]])
            if nxt and c % 6 == 5:
                norm_post(blk + 1, c // 6)
        kb.dma("sp", o_fm_bf[:, blk * 512:(blk + 1) * 512].rearrange("(c p) t -> p c t", p=128), sb_[:], reads=[b_bf])
        for t in range(4):
            ps, pb = kb.bank()
            for k in range(8):
                kb.op("pe", lambda e, k=k, t=t, ps=ps: e.matmul(ps[:, 0:288], lhsT=hT[:, k, t * 128:(t + 1) * 128], rhs=wsb[:, k, A_TM0:A_TM1],
                                                         start=(k == 0), stop=(k == 7)), reads=wdep(A_TM0, A_TM1) + [hb], writes=[pb])
            kb.op("dve", lambda e, t=t, ps=ps: e.tensor_copy(out=stb_[:, t, :], in_=ps[:, 0:256]), writes=[b_tb, pb])
            kb.op("act", lambda e, t=t, ps=ps: e.activation(out=stf_[:, t, 0:24], in_=ps[:, 256:280], func=AF.Sigmoid),
                  writes=[b_tf, pb])
            kb.op("dve", lambda e, t=t, ps=ps: e.tensor_copy(out=stf_[:, t, 24:32], in_=ps[:, 280:288]), writes=[b_tf, pb])
            ps2, pb2 = kb.bank()
            for k in range(8):
                kb.op("pe", lambda e, k=k, t=t, ps2=ps2: e.matmul(ps2[:], lhsT=hT[:, k, t * 128:(t + 1) * 128], rhs=wsb[:, k, A_TM1:A_NCOL],
                                                           start=(k == 0), stop=(k == 7)), reads=wdep(A_TM1, A_NCOL) + [hb], writes=[pb2])
            kb.op("act", lambda e, t=t, ps2=ps2: e.activation(out=stf_[:, t, 32:544], in_=ps2[:], func=AF.Copy), writes=[b_tf, pb2])
        kb.dma("sp", o_tm_bf[blk * 512:(blk + 1) * 512, :].rearrange("(t p) c -> p t c", p=128), stb_[:], reads=[b_tb])
        kb.dma("sp", o_tm_f32[blk * 512:(blk + 1) * 512, :].rearrange("(t p) c -> p t c", p=128), stf_[:], reads=[b_tf])
    _end(kb, own)
    return nc, kb


def phaseA_weight_perm():
    off = {}
    o = 0
    for nme, n in (("nq", 512), ("kc", 128), ("vc", 128), ("ks", 128), ("vs", 128), ("kw", 128), ("vw", 128),
                   ("gate", 24), ("pool", 512), ("dqkv", 1536), ("dz", 512), ("dbeta", 4), ("da", 4), ("mg", 3072)):
        off[nme] = (o, n)
        o += n
    order = ["nq", "kc", "vc", "ks", "kw", "pool", "dqkv", "vs", "vw", "gate", "dbeta", "da", "dz"]
    idx = np.concatenate([np.arange(off[n][0], off[n][0] + off[n][1]) for n in order])
    assert idx.size == A_NCOL
    return idx, off


def emit_pos_rows(kb, nc, S, NCP, slopes_ap, nheads, put_c, put_k, put_q):
    slp = kb.T("slp", [1, nheads], F32)
    b_sl = Buf()
    kb.dma("sp", slp[:], slopes_ap, writes=[b_sl])
    CH = min(S, 2048)
    with kb.S("rowf", [1, 2, CH], F32) as rowf, kb.S("rowb", [1, 4, CH], BF16) as rowb, \
            kb.S("crow", [1, 2, NCP], F32) as crow, kb.S("crowb", [1, 2, NCP], BF16) as crowb, \
            kb.S("qrow", [1, 4, CH], BF16) as qrow:
        b_rf, b_rb, b_cb, b_qr = Buf(), Buf(), Buf(), Buf()
        kb.op("pool", lambda e: e.iota(crow[:, 0, :], pattern=[[128, NCP // 8], [0, 8]], base=0, channel_multiplier=0,
                                       allow_small_or_imprecise_dtypes=True), writes=[b_cb])
        kb.op("pool", lambda e: e.iota(crow[:, 1, :], pattern=[[0, NCP // 8], [16, 8]], base=31, channel_multiplier=0,
                                       allow_small_or_imprecise_dtypes=True), writes=[b_cb])
        kb.op("dve", lambda e: e.tensor_copy(out=crowb[:], in_=crow[:]), writes=[b_cb])
        for r in range(2):
            put_c(2 + r, crowb[:, r, :], [b_cb])
        kb.op("dve", lambda e: e.memset(crowb[:], 1.0), writes=[b_cb])
        for r in range(2):
            put_c(r, crowb[:, r, :], [b_cb])
        for c in range(S // CH):
            c0 = c * CH
            kb.op("pool", lambda e, c0=c0: e.iota(rowf[:, 0, :], pattern=[[128, CH // 128], [0, 128]], base=c0, channel_multiplier=0,
                                           allow_small_or_imprecise_dtypes=True), writes=[b_rf])
            kb.op("pool", lambda e: e.iota(rowf[:, 1, :], pattern=[[0, CH // 128], [1, 128]], base=0, channel_multiplier=0,
                                           allow_small_or_imprecise_dtypes=True), writes=[b_rf])
            kb.op("dve", lambda e: e.memset(rowb[:, 0:2, :], 1.0), writes=[b_rb])
            kb.op("dve", lambda e: e.tensor_copy(out=rowb[:, 2:4, :], in_=rowf[:, :, :]), reads=[b_rf], writes=[b_rb])
            for r in range(4):
                put_k(r, c0, CH, rowb[:, r, :], [b_rb])
            for h in range(nheads):
                kb.op("dve", lambda e, h=h: e.tensor_scalar(out=qrow[:, 0:2, :], in0=rowf[:, :, :], scalar1=slp[0:1, h:h + 1],
                                                     scalar2=-1.0, op0=ALU.mult, op1=ALU.mult),
                      reads=[b_rf, b_sl], writes=[b_qr])
                kb.op("dve", lambda e, h=h: e.tensor_scalar(out=qrow[:, 2:4, :], in0=rowb[:, 0:2, :], scalar1=slp[0:1, h:h + 1],
                                                     scalar2=None, op0=ALU.mult), reads=[b_rb, b_sl], writes=[b_qr])
                for r in range(4):
                    put_q(h, r, c0, CH, qrow[:, r, :], [b_qr])
        kb.barrier()


def build_nsa_rows(S, nc, kb, io, tag="rows_"):
    nc, kb, own = _begin(nc, kb, tag)
    rk, rq, rc = io["rows_k"], io["rows_q"], io["rows_c"]
    emit_pos_rows(kb, nc, S, S // 16, io["slopes"], 8,
                  lambda r, src, rd: kb.dma("sp", rc[r:r + 1, :], src, reads=rd),
                  lambda r, c0, CH, src, rd: kb.dma("sp", rk[r:r + 1, c0:c0 + CH], src, reads=rd),
                  lambda h, r, c0, CH, src, rd: kb.dma("sp", rq[h, r:r + 1, c0:c0 + CH], src, reads=rd))
    _end(kb, own)


NEG_BIG = -30000.0


def build_nsa(S=SEQ, nc=None, kb=None, io=None, tag=""):
    NQB = S // 512
    NT = S // 128
    NCP = S // 16
    NC_REAL = NCP - 1
    NNT = NCP // 128
    NSEL = S // 64
    VCW = 65 + NSEL
    nc, kb, own = _begin(nc, kb, tag)
    qT_d = _dram(nc, io, "qT", [4, 64, S], BF16, "ExternalInput")
    kcT_d = _dram(nc, io, "kcT", [64, S], BF16, "ExternalInput")
    vcT_d = _dram(nc, io, "vcT", [64, S], BF16, "ExternalInput")
    ksT_d = _dram(nc, io, "ksT", [64, S], BF16, "ExternalInput")
    kwT_d = _dram(nc, io, "kwT", [64, S], BF16, "ExternalInput")
    vs_d = _dram(nc, io, "vs", [S, 64], BF16, "ExternalInput")
    vw_d = _dram(nc, io, "vw", [S, 64], BF16, "ExternalInput")
    gate_d = _dram(nc, io, "gate", [S, 12], F32, "ExternalInput")
    phk1_d = _dram(nc, io, "phk1", [2048, 128], F32, "ExternalInput")
    phv1_d = _dram(nc, io, "phv1", [2048, 128], F32, "ExternalInput")
    phk2_d = _dram(nc, io, "phk2", [128, 64], F32, "ExternalInput")
    phv2_d = _dram(nc, io, "phv2", [128, 64], F32, "ExternalInput")
    posk_d = _dram(nc, io, "posk", [32, 64], F32, "ExternalInput")
    posv_d = _dram(nc, io, "posv", [32, 64], F32, "ExternalInput")
    slopes_d = _dram(nc, io, "slopes", [1, 4], F32, "ExternalInput")
    yaT_d = _dram(nc, io, "yaT", [256, S], BF16, "ExternalOutput")

    banks = kb._banks
    ident, identb = emit_identity(kb, nc)
    FILL0 = kb.fill_reg(0.0)
    FILLNEG = kb.fill_reg(-1e30)

    qTa = kb.T("qTa", [68, 4, S], BF16)
    ksTa = kb.T("ksTa", [68, S], BF16)
    kwTa = kb.T("kwTa", [68, S], BF16)
    kcTa = kb.T("kcTa", [68, NCP], BF16)
    VS = kb.T("VS", [128, NT, 65], BF16)
    VW = kb.T("VW", [128, NT, 65], BF16)
    VCa = kb.T("VCa", [128, NNT, VCW], BF16)
    Kind = kb.T("Kind", [128, S], BF16)
    gates = kb.T("gates", [128, NT, 12], F32)
    b_q, b_ks, b_kw, b_kc, b_vs, b_vw, b_vc, b_kind, b_g = (Buf() for _ in range(9))

    kb.dma("sp", qTa[0:64, :, :], qT_d.rearrange("h d s -> d h s"), writes=[b_q])
    kb.dma("sp", ksTa[0:64, :], ksT_d, writes=[b_ks])
    kb.dma("sp", kwTa[0:64, :], kwT_d, writes=[b_kw])
    kb.dma("sp", VS[:, :, 0:64], vs_d.rearrange("(t p) d -> p t d", p=128), writes=[b_vs])
    kb.dma("sp", VW[:, :, 0:64], vw_d.rearrange("(t p) d -> p t d", p=128), writes=[b_vw])
    kb.dma("sp", gates[:], gate_d.rearrange("(t p) c -> p t c", p=128), writes=[b_g])
    kb.op("dve", lambda e: e.memset(VS[:, :, 64:65], 1.0), writes=[b_vs])
    kb.op("dve", lambda e: e.memset(VW[:, :, 64:65], 1.0), writes=[b_vw])
    kb.op("dve", lambda e: e.memset(VCa[:, :, 64:65], 1.0), writes=[b_vc])

    if io is not None and "rows_k" in io:
        kb.dma("sp", ksTa[64:68, :], io["rows_k"], writes=[b_ks])
        kb.dma("sp", kwTa[64:68, :], io["rows_k"], writes=[b_kw])
        kb.dma("sp", qTa[64:68, :, :], io["rows_q"].rearrange("h r s -> r h s"), writes=[b_q])
        kb.dma("sp", kcTa[64:68, :], io["rows_c"], writes=[b_kc])
    else:
        def put_c(r, src, rd):
            kb.dma("sp", kcTa[64 + r:65 + r, :], src, reads=rd, writes=[b_kc])

        def put_k(r, c0, CH, src, rd):
            kb.dma("sp", ksTa[64 + r:65 + r, c0:c0 + CH], src, reads=rd, writes=[b_ks])
            kb.dma("sp", kwTa[64 + r:65 + r, c0:c0 + CH], src, reads=rd, writes=[b_kw])

        def put_q(h, r, c0, CH, src, rd):
            kb.dma("sp", qTa[64 + r:65 + r, h, c0:c0 + CH], src, reads=rd, writes=[b_q])

        emit_pos_rows(kb, nc, S, NCP, slopes_d, 4, put_c, put_k, put_q)

    kb.op("dve", lambda e: e.memset(Kind[:], 1.0), writes=[b_kind])
    kb.op("pool", lambda e: e.affine_select(out=Kind[:], in_=Kind[:], pattern=[[1, S]], compare_op=ALU.is_ge, fill=FILL0,
                                            base=0, channel_multiplier=-64), writes=[b_kind])
    kb.op("pool", lambda e: e.affine_select(out=Kind[:], in_=Kind[:], pattern=[[-1, S]], compare_op=ALU.is_ge, fill=FILL0,
                                            base=63, channel_multiplier=64), writes=[b_kind])
    with kb.S("ovA", [128, NSEL], BF16) as ovA, kb.S("ovB", [128, NSEL], BF16) as ovB:
        b_ov = Buf()
        for nt in range(NNT):
            kb.op("dve", lambda e: e.memset(ovA[:], 1.0), writes=[b_ov])
            kb.op("dve", lambda e: e.memset(ovB[:], 1.0), writes=[b_ov])
            for (t_, sgn, off) in ((ovA, 1, 1), (ovA, -1, 3), (ovB, 1, 0), (ovB, -1, 2)):
                kb.op("pool", lambda e, t_=t_, sgn=sgn, off=off, nt=nt: e.affine_select(
                    out=t_[:], in_=t_[:], pattern=[[-4 * sgn, NSEL]], compare_op=ALU.is_ge, fill=FILL0,
                    base=sgn * 128 * nt + off, channel_multiplier=sgn), writes=[b_ov])
            kb.op("dve", lambda e, nt=nt: e.tensor_tensor(out=VCa[:, nt, 65:VCW], in0=ovA[:], in1=ovB[:], op=ALU.add),
                  reads=[b_ov], writes=[b_vc])
        kb.barrier()

    cps, cpb = banks[7]
    with kb.S("cin", [64, S], BF16) as cin, kb.S("w1", [64, 32, 128], BF16) as w1, \
            kb.S("w2", [128, 64], BF16) as w2, kb.S("posT", [64, 32], BF16) as posT, \
            kb.S("hidT", [128, NCP], BF16) as hidT, kb.S("cvec", [128, 1], F32) as cvec:
        b_cin, b_w1, b_w2, b_pos, b_hid, b_cv = (Buf() for _ in range(6))
        for which in ("k", "v"):
            src, p1, p2, pp = (kcT_d, phk1_d, phk2_d, posk_d) if which == "k" else (vcT_d, phv1_d, phv2_d, posv_d)
            kb.dma("sp", cin[:], src, writes=[b_cin])
            kb.dma("pool", w1[:], p1.rearrange("(l d) h -> d l h", d=64), writes=[b_w1])
            kb.dma("pool", w2[:], p2, writes=[b_w2])
            with nc.allow_non_contiguous_dma(reason="tiny transposed load"):
                kb.dma("pool", posT[:], pp.rearrange("l d -> d l"), writes=[b_pos])
            cview = cin[:].rearrange("d (n s) -> d n s", s=16)
            for l in range(32):
                rhs = cview[:, 0:NC_REAL, l] if l < 16 else cview[:, 1:NC_REAL + 1, l - 16]
                kb.op("pe", lambda e, l=l, rhs=rhs: e.matmul(cps[:, 0:NC_REAL], lhsT=w1[:, l, :], rhs=rhs, start=(l == 0), stop=(l == 31)),
                      reads=[b_cin, b_w1], writes=[cpb])
            ps2, pb2 = banks[6]
            for l in range(32):
                kb.op("pe", lambda e, l=l: e.matmul(ps2[:, 0:1], lhsT=w1[:, l, :], rhs=posT[:, l:l + 1], start=(l == 0), stop=(l == 31)),
                      reads=[b_pos, b_w1], writes=[pb2])
            kb.op("dve", lambda e: e.tensor_copy(out=cvec[:], in_=ps2[:, 0:1]), writes=[b_cv, pb2])
            kb.op("dve", lambda e: e.memset(hidT[:, NC_REAL:NCP], 0.0), writes=[b_hid])
            kb.op("act", lambda e: e.activation(out=hidT[:, 0:NC_REAL], in_=cps[:, 0:NC_REAL], func=AF.Silu, bias=cvec[:, 0:1]),
                  reads=[b_cv], writes=[b_hid, cpb])
            if which == "k":
                kb.op("pe", lambda e: e.matmul(ps2[0:64, 0:NCP], lhsT=w2[:], rhs=hidT[:], start=True, stop=True),
                      reads=[b_w2, b_hid], writes=[pb2])
                kb.op("dve", lambda e: e.tensor_copy(out=kcTa[0:64, :], in_=ps2[0:64, 0:NCP]), writes=[b_kc, pb2])
            else:
                for nt in range(NNT):
                    kb.op("pe", lambda e, nt=nt: e.matmul(ps2[:, nt * 64:(nt + 1) * 64], lhsT=hidT[:, nt * 128:(nt + 1) * 128], rhs=w2[:],
                                                   start=True, stop=True), reads=[b_w2, b_hid], writes=[pb2])
                kb.op("dve", lambda e: e.tensor_copy(out=VCa[:, :, 0:64], in_=ps2[:, 0:NNT * 64].rearrange("p (n d) -> p n d", d=64)),
                      writes=[b_vc, pb2])
        kb.barrier()

    NPT = 12
    pTs = [kb.T(f"pT{i}", [128, 512], BF16) for i in range(NPT)]
    pTb = [Buf() for _ in range(NPT)]
    pti = [0]
    ya = kb.T("ya", [128, 4, 256], F32)
    yab = kb.T("yab", [128, 4, 256], BF16)
    imp = kb.T("imp", [128, 4, NSEL], F32)
    score = kb.T("score", [128, NSEL], F32)
    score2 = kb.T("score2", [128, NSEL], F32)
    m8 = kb.T("m8", [128, 16], F32)
    mneg = kb.T("mneg", [128, 128], BF16)
    mnegT = kb.T("mnegT", [128, 512], BF16)
    den = kb.T("den", [128, 4], F32)
    scl = kb.T("scl", [128, 4], F32)
    yst = [kb.T(f"yst{i}", [128, 2, 512], BF16) for i in range(2)]
    b_ya, b_yab, b_imp, b_sc, b_mn, b_mnT, b_den = (Buf() for _ in range(7))
    b_yst = [Buf(), Buf()]
    kb.op("dve", lambda e: e.memset(mneg[:], 0.0), writes=[b_mn])
    sb_i = [0]

    SB = [(0, 1, 7)]

    def s_bank():
        sb = SB[0]
        i = sb_i[0] % len(sb)
        sb_i[0] = (i + 1) % len(sb)
        return banks[sb[i]]

    LAG = 5
    pend = []

    def push_pv(fn):
        pend.append(fn)
        if len(pend) > LAG:
            pend.pop(0)()

    def flush_pv():
        while pend:
            pend.pop(0)()

    def next_pT():
        i = pti[0]
        pti[0] = (i + 1) % NPT
        return pTs[i], pTb[i]

    tmpfs = [kb.T(f"tmpf{i}", [128, 512], F32) for i in range(2)]
    tmpfb = [Buf(), Buf()]
    tfi = [0]

    def emit_exp(ps, pb, pT, ptb, c0, c1, clamp):
        if clamp:
            i = tfi[0]
            tfi[0] ^= 1
            tf, tfb = tmpfs[i], tmpfb[i]
            kb.op("dve", lambda e: e.tensor_scalar(out=tf[:, c0:c1], in0=ps[:, c0:c1], scalar1=60.0, scalar2=None, op0=ALU.min),
                  writes=[tfb, pb])
            kb.op("act", lambda e: e.activation(out=pT[:, c0:c1], in_=tf[:, c0:c1], func=AF.Exp), reads=[tfb], writes=[ptb])
        else:
            kb.op("act", lambda e: e.activation(out=pT[:, c0:c1], in_=ps[:, c0:c1], func=AF.Exp), writes=[ptb, pb])

    def evac_branch(ps, pb, width, h, qb, br, first):
        accs = ps
        for i, (a, ab) in enumerate(accs):
            kb.op("dve", lambda e, a=a, i=i: e.tensor_scalar(out=den[:, i:i + 1], in0=a[:, 64:65], scalar1=1e-30, scalar2=None,
                                                             op0=ALU.max), writes=[b_den, ab])
        kb.op("dve", lambda e: e.reciprocal(out=den[:], in_=den[:]), writes=[b_den])
        kb.op("dve", lambda e: e.tensor_tensor(out=scl[:], in0=den[:], in1=gates[:, qb * 4:(qb + 1) * 4, h * 3 + br], op=ALU.mult),
              reads=[b_g], writes=[b_den])
        for i, (a, ab) in enumerate(accs):
            dst = ya[:, i, h * 64:(h + 1) * 64]
            if first:
                kb.op("dve", lambda e, a=a, i=i, dst=dst: e.tensor_scalar(out=dst, in0=a[:, 0:64], scalar1=scl[:, i:i + 1], scalar2=None,
                                                                 op0=ALU.mult), reads=[b_den], writes=[b_ya, ab])
            else:
                kb.op("dve", lambda e, a=a, i=i, dst=dst: e.scalar_tensor_tensor(out=dst, in0=a[:, 0:64], scalar=scl[:, i:i + 1], in1=dst,
                                                                        op0=ALU.mult, op1=ALU.add), reads=[b_den], writes=[b_ya, ab])

    for qb in range(NQB):
        q0 = qb * 512
        SB[0] = (0, 1, 7)
        nts = list(range(0, min(NNT, (512 * qb + 480) // 2048 + 1)))
        for h in range(4):
            bx, bxb = banks[2]
            by, byb = banks[3]
            accs = [(bx[:, 0:VCW], bxb), (bx[:, VCW:2 * VCW], bxb), (by[:, 0:VCW], byb), (by[:, VCW:2 * VCW], byb)] if 2 * VCW <= 512 else None
            for ni, nt in enumerate(nts):
                ps, pb = s_bank()
                kb.op("pe", lambda e, ps=ps, nt=nt, h=h: e.matmul(ps[:], lhsT=kcTa[:, nt * 128:(nt + 1) * 128], rhs=qTa[:, h, q0:q0 + 512],
                                                           start=True, stop=True), reads=[b_kc, b_q], writes=[pb])
                pT, ptb = next_pT()
                emit_exp(ps, pb, pT, ptb, 0, 512, qb - 4 * nt <= 4)
                if qb - 4 * nt <= 4:
                    kb.op("pool", lambda e, pT=pT, nt=nt: e.affine_select(out=pT[:], in_=pT[:], pattern=[[1, 512]], compare_op=ALU.is_ge,
                                                                   fill=FILL0, base=512 * qb - 2048 * nt - 31, channel_multiplier=-16),
                          writes=[ptb])
                def pv_c(accs=accs, pT=pT, ptb=ptb, nt=nt, ni=ni, last=(ni == len(nts) - 1)):
                    for qs in range(4):
                        a, ab = accs[qs]
                        kb.op("pe", lambda e, a=a, qs=qs: e.matmul(
                            a, lhsT=pT[:, qs * 128:(qs + 1) * 128], rhs=VCa[:, nt, :], start=(ni == 0 and qs % 2 == 0),
                            stop=last, skip_group_check=True), reads=[ptb, b_vc], writes=[ab])
                push_pv(pv_c)
            def ev_c(accs=accs, h=h):
                evac_branch(accs, None, VCW, h, qb, 0, True)
                for qs in range(4):
                    a, ab = accs[qs]
                    if h == 0:
                        kb.op("dve", lambda e, a=a, qs=qs: e.tensor_scalar(out=imp[:, qs, :], in0=a[:, 65:VCW], scalar1=den[:, qs:qs + 1], scalar2=None,
                                                                    op0=ALU.mult), reads=[b_den], writes=[b_imp, ab])
                    else:
                        kb.op("dve", lambda e, a=a, qs=qs: e.scalar_tensor_tensor(out=imp[:, qs, :], in0=a[:, 65:VCW], scalar=den[:, qs:qs + 1],
                                                                           in1=imp[:, qs, :], op0=ALU.mult, op1=ALU.add),
                              reads=[b_den], writes=[b_imp, ab])
            push_pv(ev_c)
        flush_pv()
        tp, tpb = banks[6]
        tpv = tp[:].bitcast(BF16)
        for qs in range(4):
            T = 4 * qb + qs
            kb.op("pool", lambda e, qs=qs, T=T: e.affine_select(out=score[:], in_=imp[:, qs, :], pattern=[[-64, NSEL]], compare_op=ALU.is_ge,
                                                        fill=FILLNEG, base=128 * T, channel_multiplier=1), reads=[b_imp], writes=[b_sc])
            kb.op("dve", lambda e: e.tensor_scalar(out=score[:, 0:1], in0=score[:, 0:1], scalar1=1e4, scalar2=None, op0=ALU.add), writes=[b_sc])
            kb.op("dve", lambda e, T=T: e.tensor_scalar(out=score[0:64, 2 * T:2 * T + 1], in0=score[0:64, 2 * T:2 * T + 1], scalar1=1e4,
                                                  scalar2=None, op0=ALU.add), writes=[b_sc])
            kb.op("dve", lambda e, T=T: e.tensor_scalar(out=score[64:128, 2 * T + 1:2 * T + 2], in0=score[64:128, 2 * T + 1:2 * T + 2],
                                                  scalar1=1e4, scalar2=None, op0=ALU.add), writes=[b_sc])
            kb.op("dve", lambda e: e.max(out=m8[:, 0:8], in_=score[:]), writes=[b_sc])
            kb.op("dve", lambda e: e.match_replace(out=score2[:], in_to_replace=m8[:, 0:8], in_values=score[:], imm_value=-3e38), writes=[b_sc])
            kb.op("dve", lambda e: e.max(out=m8[:, 8:16], in_=score2[:]), writes=[b_sc])
            kb.op("dve", lambda e: e.tensor_scalar(out=mneg[:, 0:NSEL], in0=score[:], scalar1=m8[:, 15:16], scalar2=NEG_BIG,
                                                   op0=ALU.is_lt, op1=ALU.mult), reads=[b_sc], writes=[b_mn])
            kb.op("pe", lambda e, qs=qs: e.transpose(tpv[:, qs * 128:(qs + 1) * 128], mneg[:], ident[:]), reads=[b_mn, identb], writes=[tpb])
        kb.op("act", lambda e: e.activation(out=mnegT[:], in_=tpv[:, 0:512], func=AF.Copy), writes=[b_mnT, tpb])
        SB[0] = (0, 1, 7, 2, 3)
        for h in range(4):
            sbk, sbb = banks[4]
            wbk, wbb = banks[5]
            acc_s = [(sbk[:, i * 65:(i + 1) * 65], sbb) for i in range(4)]
            acc_w = [(wbk[:, i * 65:(i + 1) * 65], wbb) for i in range(4)]
            first_s = True
            for kt in range(0, 4 * qb + 4):
                a_ = max(0, kt - 4 * qb)
                c0 = 128 * a_
                ps, pb = s_bank()
                kb.op("pe", lambda e, ps=ps, kt=kt, h=h, c0=c0: e.matmul(ps[:, c0:512], lhsT=ksTa[:, kt * 128:(kt + 1) * 128],
                                                                  rhs=qTa[:, h, q0 + c0:q0 + 512], start=True, stop=False),
                      reads=[b_ks, b_q], writes=[pb])
                kb.op("pe", lambda e, ps=ps, kt=kt, c0=c0: e.matmul(ps[:, c0:512], lhsT=Kind[:, kt * 128:(kt + 1) * 128], rhs=mnegT[:, c0:512],
                                                             start=False, stop=True), reads=[b_kind, b_mnT], writes=[pb])
                pT, ptb = next_pT()
                emit_exp(ps, pb, pT, ptb, c0, 512, False)
                if kt >= 4 * qb:
                    kb.op("pool", lambda e, pT=pT, c0=c0: e.affine_select(out=pT[:, c0:c0 + 128], in_=pT[:, c0:c0 + 128], pattern=[[1, 128]],
                                                                   compare_op=ALU.is_ge, fill=FILL0, base=0, channel_multiplier=-1), writes=[ptb])
                def pv_s(acc_s=acc_s, pT=pT, ptb=ptb, kt=kt, a_=a_, fs=first_s):
                    for qs in range(a_, 4):
                        a, ab = acc_s[qs]
                        kb.op("pe", lambda e, a=a, qs=qs, st_=(fs and qs == a_): e.matmul(
                            a, lhsT=pT[:, qs * 128:(qs + 1) * 128], rhs=VS[:, kt, :], start=st_, stop=(kt == 4 * qb + qs),
                            skip_group_check=True), reads=[ptb, b_vs], writes=[ab])
                push_pv(pv_s)
                first_s = False
            first_w = True
            for kt in range(max(0, 4 * qb - 4), 4 * qb + 4):
                qlo = max(0, kt - 4 * qb)
                qhi = min(3, kt + 4 - 4 * qb)
                c0, c1 = 128 * qlo, 128 * (qhi + 1)
                ps, pb = s_bank()
                kb.op("pe", lambda e, ps=ps, kt=kt, h=h, c0=c0, c1=c1: e.matmul(ps[:, c0:c1], lhsT=kwTa[:, kt * 128:(kt + 1) * 128],
                                                                         rhs=qTa[:, h, q0 + c0:q0 + c1], start=True, stop=True),
                      reads=[b_kw, b_q], writes=[pb])
                pT, ptb = next_pT()
                emit_exp(ps, pb, pT, ptb, c0, c1, False)
                if kt >= 4 * qb:
                    kb.op("pool", lambda e, pT=pT, c0=c0: e.affine_select(out=pT[:, c0:c0 + 128], in_=pT[:, c0:c0 + 128], pattern=[[1, 128]],
                                                                   compare_op=ALU.is_ge, fill=FILL0, base=0, channel_multiplier=-1), writes=[ptb])
                if kt + 4 - 4 * qb <= 3:
                    ce = 128 * (kt + 4 - 4 * qb)
                    kb.op("pool", lambda e, pT=pT, ce=ce: e.affine_select(out=pT[:, ce:ce + 128], in_=pT[:, ce:ce + 128], pattern=[[-1, 128]],
                                                                   compare_op=ALU.is_ge, fill=FILL0, base=-1, channel_multiplier=1), writes=[ptb])
                def pv_w(acc_w=acc_w, pT=pT, ptb=ptb, kt=kt, qlo=qlo, qhi=qhi, fw=first_w):
                    for qs in range(qlo, qhi + 1):
                        a, ab = acc_w[qs]
                        kb.op("pe", lambda e, a=a, qs=qs, st_=(fw and qs == qlo): e.matmul(
                            a, lhsT=pT[:, qs * 128:(qs + 1) * 128], rhs=VW[:, kt, :], start=st_, stop=(kt == 4 * qb + qs),
                            skip_group_check=True), reads=[ptb, b_vw], writes=[ab])
                push_pv(pv_w)
                first_w = False
            def ev_sw(acc_s=acc_s, acc_w=acc_w, h=h):
                evac_branch(acc_s, None, 65, h, qb, 1, False)
                evac_branch(acc_w, None, 65, h, qb, 2, False)
            push_pv(ev_sw)
        flush_pv()
        kb.op("act", lambda e: e.activation(out=yab[:], in_=ya[:], func=AF.Copy), reads=[b_ya], writes=[b_yab])
        yst_, ystb = yst[qb % 2], b_yst[qb % 2]
        for f in range(2):
            tp, tpb = banks[6]
            tpv = tp[:].bitcast(BF16)
            for qs in range(4):
                kb.op("pe", lambda e, qs=qs, f=f, tpv=tpv: e.transpose(tpv[:, qs * 128:(qs + 1) * 128], yab[:, qs, f * 128:(f + 1) * 128], ident[:]),
                      reads=[b_yab, identb], writes=[tpb])
            kb.op("dve", lambda e, f=f, tpv=tpv, yst_=yst_: e.tensor_copy(out=yst_[:, f, :], in_=tpv[:, 0:512]), writes=[ystb, tpb])
        kb.dma("sp", yaT_d[:, q0:q0 + 512].rearrange("(f p) t -> p f t", p=128), yst_[:], reads=[ystb])
    _end(kb, own)
    return nc, kb


def build_dn(S=SEQ, nc=None, kb=None, io=None, tag=""):
    PREP_SLICE = 8
    NBL = S // 512
    NCH = S // 64
    nc, kb, own = _begin(nc, kb, tag)
    x_d = _dram(nc, io, "xq", [3, 256, S], F32, "ExternalInput")
    cw_d = _dram(nc, io, "convw", [4, 3, 256], F32, "ExternalInput")
    z_d = _dram(nc, io, "z", [S, 256], F32, "ExternalInput")
    bl_d = _dram(nc, io, "blog", [S, 2], F32, "ExternalInput")
    al_d = _dram(nc, io, "alog", [S, 2], F32, "ExternalInput")
    Alog_d = _dram(nc, io, "Alog", [1, 2], F32, "ExternalInput")
    dtb_d = _dram(nc, io, "dtb", [1, 2], F32, "ExternalInput")
    ng_d = _dram(nc, io, "ng", [1, 128], F32, "ExternalInput")
    yc_d = _dram(nc, io, "ycT", [256, S], BF16, "ExternalOutput")

    banks = kb._banks
    V = lambda fn, reads=(), writes=(): kb.op("dve", fn, reads, writes)
    A = lambda fn, reads=(), writes=(): kb.op("act", fn, reads, writes)
    P = lambda fn, reads=(), writes=(): kb.op("pe", fn, reads, writes)
    G = lambda fn, reads=(), writes=(): kb.op("pool", fn, reads, writes)
    FILL0 = kb.fill_reg(0.0)

    identF, b_idF = emit_identity(kb, nc, F32, "identF")
    identB, b_idB = emit_identity(kb, nc, BF16, "identB")
    cst = Buf("consts")
    U = kb.T("U", [64, 64], F32)
    negU = kb.T("negU", [64, 64], F32)
    ones64 = kb.T("ones64", [64, 128], F32)
    neg64 = kb.T("neg64", [64, 64], F32)
    ones128 = kb.T("ones128", [128, 128], F32)
    mask_sl = kb.T("mask_sl", [64, 8, 64], F32)
    mask_ui = kb.T("mask_ui", [64, 8, 64], F32)
    V(lambda e: e.memset(U[:], 1.0), writes=[cst])
    G(lambda e: e.affine_select(out=U[:], in_=U[:], pattern=[[1, 64]], compare_op=ALU.is_ge, fill=FILL0, base=0,
                                channel_multiplier=-1), writes=[cst])
    V(lambda e: e.tensor_scalar(out=negU[:], in0=U[:], scalar1=-1.0, scalar2=None, op0=ALU.mult), writes=[cst])
    V(lambda e: e.memset(ones64[:], 1.0), writes=[cst])
    V(lambda e: e.memset(neg64[:], -1.0), writes=[cst])
    V(lambda e: e.memset(ones128[:], 1.0), writes=[cst])
    V(lambda e: e.memset(mask_sl[:], 1.0), writes=[cst])
    V(lambda e: e.memset(mask_ui[:], 1.0), writes=[cst])
    G(lambda e: e.affine_select(out=mask_sl[:], in_=mask_sl[:], pattern=[[0, 8], [-1, 64]], compare_op=ALU.is_ge, fill=FILL0,
                                base=-1, channel_multiplier=1), writes=[cst])
    G(lambda e: e.affine_select(out=mask_ui[:], in_=mask_ui[:], pattern=[[0, 8], [1, 64]], compare_op=ALU.is_ge, fill=FILL0,
                                base=0, channel_multiplier=-1), writes=[cst])
    ones128r = kb.T("ones128r", [128, 128], F32)
    identR = kb.T("identR", [64, 64], F32)
    V(lambda e: e.tensor_copy(out=ones128r[:].bitcast(F32R), in_=ones128[:]), writes=[cst])
    V(lambda e: e.tensor_copy(out=identR[:].bitcast(F32R), in_=identF[0:64, 0:64]), reads=[b_idF], writes=[cst])
    U8 = U[:].unsqueeze(1).to_broadcast([64, 8, 64])
    I8 = identF[0:64, 0:64].unsqueeze(1).to_broadcast([64, 8, 64])

    cw = kb.T("cw", [128, 6, 4], F32)
    ngt = kb.T("ngt", [64, 128], F32)
    An = kb.T("An", [64, 2], F32)
    dtb = kb.T("dtb_sb", [64, 2], F32)
    b_par = Buf("par")
    with nc.allow_non_contiguous_dma(reason="tiny parameter loads"):
        for k in range(4):
            for w_ in range(3):
                kb.dma("sp", cw[:, 2 * w_:2 * w_ + 2, k], cw_d[k, w_].rearrange("(t p) -> p t", p=128), writes=[b_par])
        kb.dma("sp", ngt[:], ng_d.to_broadcast([64, 128]), writes=[b_par])
        kb.dma("sp", An[:], Alog_d.to_broadcast([64, 2]), writes=[b_par])
        kb.dma("sp", dtb[:], dtb_d.to_broadcast([64, 2]), writes=[b_par])
    A(lambda e: e.activation(out=An[:], in_=An[:], func=AF.Exp), writes=[b_par])
    V(lambda e: e.tensor_scalar(out=An[:], in0=An[:], scalar1=-1.0, scalar2=None, op0=ALU.mult), writes=[b_par])

    beta = kb.T("beta", [64, NCH, 2], F32)
    gl = kb.T("gl", [64, NCH, 2], F32)
    eA = kb.T("eA", [64, NCH, 2], F32)
    eB = kb.T("eB", [64, NCH, 2], F32)
    bA = kb.T("bA", [64, NCH, 2], F32)
    dec = kb.T("dec", [128, NCH, 2], F32)
    b_gt = Buf("gates")
    kb.dma("sp", beta[:], bl_d.rearrange("(c i) h -> i c h", i=64), writes=[b_gt])
    kb.dma("sp", gl[:], al_d.rearrange("(c i) h -> i c h", i=64), writes=[b_gt])
    A(lambda e: e.activation(out=beta[:], in_=beta[:], func=AF.Sigmoid), writes=[b_gt])
    V(lambda e: e.tensor_tensor(out=gl[:], in0=gl[:], in1=dtb[:].unsqueeze(1).to_broadcast([64, NCH, 2]), op=ALU.add),
      reads=[b_par], writes=[b_gt])
    A(lambda e: e.activation(out=gl[:], in_=gl[:], func=AF.Exp), writes=[b_gt])
    A(lambda e: e.activation(out=gl[:], in_=gl[:], func=AF.Ln, bias=1.0), writes=[b_gt])
    V(lambda e: e.tensor_tensor(out=gl[:], in0=gl[:], in1=An[:].unsqueeze(1).to_broadcast([64, NCH, 2]), op=ALU.mult),
      reads=[b_par], writes=[b_gt])
    glf = gl[:].rearrange("p c h -> p (c h)")
    NG = NCH * 2
    for c0 in range(0, NG, 512):
        c1 = min(NG, c0 + 512)
        ps, pb = banks[0]
        P(lambda e: e.matmul(ps[0:64, 0:c1 - c0], lhsT=U[:], rhs=glf[:, c0:c1], start=True, stop=True), reads=[cst, b_gt], writes=[pb])
        ps2, pb2 = banks[1]
        P(lambda e: e.matmul(ps2[0:64, 0:c1 - c0], lhsT=ones64[:, 0:64], rhs=glf[:, c0:c1], start=True, stop=True), reads=[cst, b_gt], writes=[pb2])
        ps3, pb3 = banks[2]
        P(lambda e: e.matmul(ps3[:, 0:c1 - c0], lhsT=ones64[:, :], rhs=glf[:, c0:c1], start=True, stop=True), reads=[cst, b_gt], writes=[pb3])
        eAf = eA[:].rearrange("p c h -> p (c h)")
        eBf = eB[:].rearrange("p c h -> p (c h)")
        decf = dec[:].rearrange("p c h -> p (c h)")
        A(lambda e: e.activation(out=eAf[:, c0:c1], in_=ps[0:64, 0:c1 - c0], func=AF.Exp), writes=[b_gt, pb])
        V(lambda e: e.tensor_copy(out=eBf[:, c0:c1], in_=ps[0:64, 0:c1 - c0]), writes=[b_gt, pb])
        V(lambda e: e.tensor_tensor(out=eBf[:, c0:c1], in0=ps2[0:64, 0:c1 - c0], in1=eBf[:, c0:c1], op=ALU.subtract), writes=[b_gt, pb2])
        A(lambda e: e.activation(out=eBf[:, c0:c1], in_=eBf[:, c0:c1], func=AF.Exp), writes=[b_gt])
        A(lambda e: e.activation(out=decf[:, c0:c1], in_=ps3[:, 0:c1 - c0], func=AF.Exp), writes=[b_gt, pb3])
    V(lambda e: e.tensor_tensor(out=bA[:], in0=beta[:], in1=eA[:], op=ALU.mult), writes=[b_gt])

    xin = [kb.T("xin0", [128, 6, 515], F32)] * 2
    b_xin = [Buf()] * 2
    cs = [kb.T(f"cs{i}", [128, 6, 512], F32) for i in range(2)]
    b_cs = [Buf(), Buf()]
    acc = kb.T("acc", [128, 512], F32)
    accp = kb.T("accp", [128, 512], F32)
    b_acc, b_accp = Buf(), Buf()
    sqt = kb.T("sqt", [128, 512], F32)
    rt = kb.T("rt", [128, 512], F32)
    b_sq, b_rt = Buf(), Buf()
    tmps = []
    for h in range(2):
        tmps.append(dict(
            G1=kb.T(f"G1_{h}", [64, 8, 64], F32), G2=kb.T(f"G2_{h}", [64, 8, 64], F32),
            Ls=kb.T(f"Ls_{h}", [64, 8, 64], F32), Lt=kb.T(f"Lt_{h}", [64, 8, 64], F32),
            KKL=kb.T(f"KKL_{h}", [64, 8, 64], F32), dgb=kb.T(f"dgb_{h}", [64, 8, 64], F32),
            nM0=kb.T(f"nM0_{h}", [64, 8, 64], F32), b_nM0=Buf(),
            Nk=[kb.T(f"Nk{i}_{h}", [64, 8, 64], F32) for i in range(2)],
            Mk=[kb.T(f"Mk{i}_{h}", [64, 8, 64], F32) for i in range(6)],
            b_G=Buf(), b_Ls=Buf(), b_Lt=Buf(), b_KKL=Buf(), b_dgb=Buf(), b_Nk=[Buf(), Buf()], b_Mk=[Buf() for _ in range(6)]))
    sets = []
    for i in range(4):
        st = dict(
            k_e=kb.T(f"k_e{i}", [64, 8, 128], F32), R=kb.T(f"R{i}", [64, 8, 256], F32),
            AqkT=kb.T(f"AqkT{i}", [64, 8, 64], F32), wT=kb.T(f"wT{i}", [128, 8, 64], F32),
            o_all=(kb.T(f"o_all{i}", [64, 8, 128], F32) if i < 2 else None),
            b_ke=Buf(), b_R=[Buf() for _ in range(4)], b_Aqk=Buf(), b_wT=Buf(), b_o=Buf())
        if i >= 2:
            st["o_all"], st["b_o"] = sets[i - 2]["o_all"], sets[i - 2]["b_o"]
        sets.append(st)
    Sst = [kb.T(f"Sst{h}", [128, 128], F32) for h in range(2)]
    b_S = [Buf(), Buf()]
    for h in range(2):
        V(lambda e, h=h: e.tensor_scalar(out=Sst[h][:].bitcast(F32R), in0=ones128[:], scalar1=0.0, scalar2=None, op0=ALU.mult), reads=[cst], writes=[b_S[h]])
    vnew = [kb.T(f"vnew{h}", [64, 128], F32) for h in range(2)]
    o1 = [kb.T(f"o1{h}", [64, 128], F32) for h in range(2)]
    b_vn = [Buf(), Buf()]
    b_o1 = [Buf(), Buf()]
    zt = kb.T("zt", [64, 8, 128], F32)
    ysq = kb.T("ysq", [64, 8, 128], F32)
    ybf = kb.T("ybf", [64, 8, 128], BF16)
    ssn = kb.T("ssn", [64, 8], F32)
    yst = [kb.T(f"ycst{i}", [128, 512], BF16) for i in range(2)]
    b_zt, b_ysq, b_ybf, b_ssn = Buf(), Buf(), Buf(), Buf()
    b_yst = [Buf(), Buf()]
    osti = [0]

    def bc(ap2, n):
        return ap2.unsqueeze(2).to_broadcast([64, 8, n])

    def prep_gen(blk):
        par = blk % 2
        xi, bxi = xin[par], b_xin[par]
        c_, bcs = cs[par], b_cs[par]
        for w_ in range(3):
            if blk == 0:
                kb.dma("sp", xi[:, 2 * w_:2 * w_ + 2, 3:515], x_d[w_, :, 0:512].rearrange("(t p) s -> p t s", p=128), writes=[bxi])
            else:
                kb.dma("sp", xi[:, 2 * w_:2 * w_ + 2, :], x_d[w_, :, blk * 512 - 3:blk * 512 + 512].rearrange("(t p) s -> p t s", p=128),
                       writes=[bxi])
        if blk == 0:
            V(lambda e: e.memset(xi[:, :, 0:3], 0.0), writes=[bxi])
        for t in range(6):
            eng, ac, bac = ("dve", acc, b_acc) if t % 2 == 0 else ("dve", accp, b_accp)
            kb.op(eng, lambda e, t=t, ac=ac: e.tensor_scalar(out=ac[:], in0=xi[:, t, 3:515], scalar1=cw[:, t, 3:4], scalar2=None, op0=ALU.mult),
                  reads=[bxi, b_par], writes=[bac])
            for k in (2, 1, 0):
                kb.op("dve", lambda e, t=t, k=k, ac=ac: e.scalar_tensor_tensor(out=ac[:], in0=xi[:, t, k:k + 512], scalar=cw[:, t, k:k + 1], in1=ac[:],
                                                                   op0=ALU.mult, op1=ALU.add), reads=[bxi, b_par], writes=[bac])
            A(lambda e, t=t, ac=ac: e.activation(out=c_[:, t, :].bitcast(F32R), in_=ac[:], func=AF.Silu), reads=[bac], writes=[bcs])
            yield
        for t in range(4):
            A(lambda e, t=t: e.activation(out=sqt[:].bitcast(F32R), in_=c_[:, t, :], func=AF.Square), reads=[bcs], writes=[b_sq])
            ps, pb = banks[2]
            P(lambda e: e.matmul(ps[:], lhsT=ones128r[:].bitcast(F32R), rhs=sqt[:].bitcast(F32R), start=True, stop=True), reads=[cst, b_sq], writes=[pb])
            A(lambda e: e.activation(out=rt[:], in_=ps[:], func=AF.Ln, bias=EPS), writes=[b_rt, pb])
            A(lambda e: e.activation(out=rt[:], in_=rt[:], func=AF.Exp, scale=-0.5), writes=[b_rt])
            sc = (128.0 ** -0.5) if t < 2 else 1.0
            V(lambda e, t=t, sc=sc: e.scalar_tensor_tensor(out=c_[:, t, :].bitcast(F32R), in0=c_[:, t, :], scalar=sc, in1=rt[:], op0=ALU.mult, op1=ALU.mult),
              reads=[b_rt], writes=[bcs])
            yield
        cs8 = slice(blk * 8, blk * 8 + 8)
        cx = []
        for h in range(2):
            st = sets[par * 2 + h]
            d_ = dict(tmps[h])
            d_.update(h=h, st=st, qT=c_[:, 0 + h, :], kT=c_[:, 2 + h, :], vT=c_[:, 4 + h, :],
                      g_b=gl[:, cs8, h], be_b=beta[:, cs8, h], eB_b=eB[:, cs8, h], bA_b=bA[:, cs8, h],
                      bN=banks[0 + 2 * h], bM=banks[1 + 2 * h])
            cx.append(d_)
        fl = lambda t: t[:].rearrange("p c j -> p (c j)")
        yield
        for X in cx:
            st = X["st"]
            k_e, R, bR = st["k_e"], st["R"], st["b_R"]
            for half in range(2):
                bk, bkb = banks[4 + half]
                hs = slice(half * 4, half * 4 + 4)
                bRh = [bR[2 * half], bR[2 * half + 1]]
                for c4 in range(4):
                    c = half * 4 + c4
                    P(lambda e, c=c, c4=c4, bk=bk, X=X: e.transpose(bk[0:64, c4 * 128:(c4 + 1) * 128], X["kT"][:, c * 64:(c + 1) * 64], identF[:]),
                      reads=[bcs, b_idF], writes=[bkb])
                bkv = bk[0:64, :].rearrange("p (c d) -> p c d", d=128)
                V(lambda e, bkv=bkv, hs=hs, X=X, k_e=k_e: e.tensor_tensor(out=k_e[:, hs, :].bitcast(F32R), in0=bkv, in1=X["eB_b"][:, hs].unsqueeze(2).to_broadcast([64, 4, 128]),
                                                                 op=ALU.mult), reads=[b_gt], writes=[st["b_ke"], bkb])
                V(lambda e, bkv=bkv, hs=hs, X=X, R=R: e.tensor_tensor(out=R[:, hs, 128:256].bitcast(F32R), in0=bkv, in1=X["bA_b"][:, hs].unsqueeze(2).to_broadcast([64, 4, 128]),
                                                               op=ALU.mult), reads=[b_gt], writes=bRh + [bkb])
                for c4 in range(4):
                    c = half * 4 + c4
                    P(lambda e, c=c, c4=c4, bk=bk, X=X: e.transpose(bk[0:64, c4 * 128:(c4 + 1) * 128], X["vT"][:, c * 64:(c + 1) * 64], identF[:]),
                      reads=[bcs, b_idF], writes=[bkb])
                V(lambda e, bkv=bkv, hs=hs, X=X, R=R: e.tensor_tensor(out=R[:, hs, 0:128].bitcast(F32R), in0=bkv, in1=X["be_b"][:, hs].unsqueeze(2).to_broadcast([64, 4, 128]),
                                                               op=ALU.mult), reads=[b_gt], writes=bRh + [bkb])
        yield
        for X in cx:
            (b0, b0b), (b1, b1b) = X["bN"], X["bM"]
            for c in range(8):
                P(lambda e, c=c, X=X, b0=b0: e.matmul(b0[0:64, c * 64:(c + 1) * 64], lhsT=X["kT"][:, c * 64:(c + 1) * 64], rhs=X["kT"][:, c * 64:(c + 1) * 64],
                                               start=True, stop=True), reads=[bcs], writes=[b0b])
            for c in range(8):
                P(lambda e, c=c, X=X, b1=b1: e.matmul(b1[0:64, c * 64:(c + 1) * 64], lhsT=X["kT"][:, c * 64:(c + 1) * 64], rhs=X["qT"][:, c * 64:(c + 1) * 64],
                                               start=True, stop=True), reads=[bcs], writes=[b1b])
        yield
        for X in cx:
            V(lambda e, X=X: e.tensor_copy(out=X["G1"][:], in_=bc(X["g_b"], 64)), reads=[b_gt], writes=[X["b_G"]])
            V(lambda e, X=X: e.tensor_tensor(out=X["G2"][:], in0=X["G1"][:], in1=U8, op=ALU.mult), reads=[cst], writes=[X["b_G"]])
        yield
        for X in cx:
            (b2, b2b), (b3, b3b) = banks[4], banks[5]
            G1f, G2f, Lsf, Ltf = fl(X["G1"]), fl(X["G2"]), fl(X["Ls"]), fl(X["Lt"])
            P(lambda e, G1f=G1f: e.matmul(b2[0:64, :], lhsT=U[:], rhs=G1f, start=True, stop=False), reads=[cst, X["b_G"]], writes=[b2b])
            P(lambda e, G2f=G2f: e.matmul(b2[0:64, :], lhsT=neg64[:], rhs=G2f, start=False, stop=True), reads=[cst, X["b_G"]], writes=[b2b])
            P(lambda e, G2f=G2f: e.matmul(b3[0:64, :], lhsT=ones64[:, 0:64], rhs=G2f, start=True, stop=False), reads=[cst, X["b_G"]], writes=[b3b])
            P(lambda e, G1f=G1f: e.matmul(b3[0:64, :], lhsT=negU[:], rhs=G1f, start=False, stop=True), reads=[cst, X["b_G"]], writes=[b3b])
            V(lambda e, Lsf=Lsf: e.tensor_scalar(out=Lsf, in0=b2[0:64, :], scalar1=0.0, scalar2=None, op0=ALU.min), writes=[X["b_Ls"], b2b])
            A(lambda e, Lsf=Lsf: e.activation(out=Lsf, in_=Lsf, func=AF.Exp), writes=[X["b_Ls"]])
            V(lambda e, X=X: e.tensor_tensor(out=X["Ls"][:], in0=X["Ls"][:], in1=mask_sl[:], op=ALU.mult), reads=[cst], writes=[X["b_Ls"]])
            V(lambda e, Ltf=Ltf: e.tensor_scalar(out=Ltf, in0=b3[0:64, :], scalar1=0.0, scalar2=None, op0=ALU.min), writes=[X["b_Lt"], b3b])
            A(lambda e, Ltf=Ltf: e.activation(out=Ltf, in_=Ltf, func=AF.Exp), writes=[X["b_Lt"]])
            V(lambda e, X=X: e.tensor_tensor(out=X["Lt"][:], in0=X["Lt"][:], in1=mask_ui[:], op=ALU.mult), reads=[cst], writes=[X["b_Lt"]])
        yield
        for X in cx:
            st = X["st"]
            (b0, b0b), (b1, b1b) = X["bN"], X["bM"]
            V(lambda e, X=X, b0=b0: e.tensor_tensor(out=fl(X["KKL"]), in0=b0[0:64, :], in1=fl(X["Ls"]), op=ALU.mult), reads=[X["b_Ls"]], writes=[X["b_KKL"], b0b])
            V(lambda e, X=X: e.tensor_tensor(out=X["Nk"][0][:].bitcast(F32R), in0=X["KKL"][:], in1=bc(X["be_b"], 64), op=ALU.mult),
              reads=[X["b_KKL"], b_gt], writes=[X["b_Nk"][0]])
            V(lambda e, X=X, b1=b1, st=st: e.tensor_tensor(out=fl(st["AqkT"]).bitcast(F32R), in0=b1[0:64, :], in1=fl(X["Lt"]), op=ALU.mult),
              reads=[X["b_Lt"]], writes=[st["b_Aqk"], b1b])
            V(lambda e, X=X: e.tensor_tensor(out=X["dgb"][:], in0=I8, in1=bc(X["be_b"], 64), op=ALU.mult), reads=[b_idF, b_gt], writes=[X["b_dgb"]])
        yield
        for X in cx:
            b1, b1b = X["bM"]
            for c in range(8):
                P(lambda e, c=c, X=X, b1=b1: e.matmul(b1[0:64, c * 64:(c + 1) * 64], lhsT=X["KKL"][:, c, :], rhs=X["dgb"][:, c, :], start=True, stop=True),
                  reads=[X["b_KKL"], X["b_dgb"]], writes=[b1b])
        yield
        for X in cx:
            b1, b1b = X["bM"]
            V(lambda e, X=X, b1=b1: e.tensor_copy(out=fl(X["Mk"][0]).bitcast(F32R), in_=b1[0:64, :]), writes=[X["b_Mk"][0], b1b])
            V(lambda e, X=X, b1=b1: e.tensor_scalar(out=fl(X["nM0"]).bitcast(F32R), in0=b1[0:64, :], scalar1=-1.0, scalar2=None, op0=ALU.mult),
              writes=[X["b_nM0"], b1b])
        for lev in range(1, 6):
            yield
            for X in cx:
                (b0, b0b), (b1, b1b) = X["bN"], X["bM"]
                Np, bNp = X["Nk"][(lev - 1) % 2], X["b_Nk"][(lev - 1) % 2]
                Mp, bMp = X["Mk"][lev - 1], X["b_Mk"][lev - 1]
                if lev < 5:
                    for c in range(8):
                        P(lambda e, c=c, Mp=Mp, Np=Np, b0=b0: e.matmul(b0[0:64, c * 64:(c + 1) * 64], lhsT=Mp[:, c, :].bitcast(F32R), rhs=Np[:, c, :].bitcast(F32R), start=True, stop=True),
                          reads=[bMp, bNp], writes=[b0b])
                for c in range(8):
                    P(lambda e, c=c, Mp=Mp, Np=Np, b1=b1: e.matmul(b1[0:64, c * 64:(c + 1) * 64], lhsT=Np[:, c, :].bitcast(F32R), rhs=Mp[:, c, :].bitcast(F32R), start=True, stop=True),
                      reads=[bMp, bNp], writes=[b1b])
            yield
            for X in cx:
                (b0, b0b), (b1, b1b) = X["bN"], X["bM"]
                Nn, bNn = X["Nk"][lev % 2], X["b_Nk"][lev % 2]
                if lev < 5:
                    A(lambda e, Nn=Nn, b0=b0: e.activation(out=fl(Nn).bitcast(F32R), in_=b0[0:64, :], func=AF.Copy), writes=[bNn, b0b])
                V(lambda e, X=X, lev=lev, b1=b1: e.tensor_copy(out=fl(X["Mk"][lev]).bitcast(F32R), in_=b1[0:64, :]), writes=[X["b_Mk"][lev], b1b])
        ai = 0
        for lev in (5, 4, 3, 2, 1, 0):
            for pr in range(4):
                yield
                for X in cx:
                    st = X["st"]
                    R, bRp = st["R"], st["b_R"][pr]
                    bk, bkb = banks[4 + ai % 2]
                    ai += 1
                    Ml, bMl = (X["Mk"][lev], X["b_Mk"][lev]) if lev > 0 else (X["nM0"], X["b_nM0"])
                    for c2 in range(2):
                        c = pr * 2 + c2
                        P(lambda e, c=c, c2=c2, bk=bk, R=R: e.matmul(bk[0:64, c2 * 256:(c2 + 1) * 256], lhsT=identR[:].bitcast(F32R), rhs=R[:, c, :].bitcast(F32R),
                                                              start=True, stop=False), reads=[cst, bRp], writes=[bkb])
                        P(lambda e, c=c, c2=c2, bk=bk, Ml=Ml, R=R: e.matmul(bk[0:64, c2 * 256:(c2 + 1) * 256], lhsT=Ml[:, c, :].bitcast(F32R), rhs=R[:, c, :].bitcast(F32R),
                                                                     start=False, stop=True), reads=[bMl, bRp], writes=[bkb])
                    Rv = R[:, pr * 2:pr * 2 + 2, :].rearrange("p c d -> p (c d)")
                    if ai % 3 == 0:
                        V(lambda e, Rv=Rv, bk=bk: e.tensor_copy(out=Rv.bitcast(F32R), in_=bk[0:64, :]), writes=[bRp, bkb])
                    else:
                        A(lambda e, Rv=Rv, bk=bk: e.activation(out=Rv.bitcast(F32R), in_=bk[0:64, :], func=AF.Copy), writes=[bRp, bkb])
        yield
        for X in cx:
            st = X["st"]
            b2, b2b = X["bN"]
            for c in range(8):
                P(lambda e, c=c, st=st, b2=b2: e.transpose(b2[:, c * 64:(c + 1) * 64], st["R"][:, c, 128:256], identF[0:64, 0:64]),
                  reads=[st["b_R"][c // 2], b_idF], writes=[b2b])
            A(lambda e, st=st, b2=b2: e.activation(out=st["wT"][:].rearrange("p c i -> p (c i)").bitcast(F32R), in_=b2[:, :], func=AF.Copy), writes=[st["b_wT"], b2b])
    def scan_step(blk, c):
        par = blk % 2
        c_, bcs = cs[par], b_cs[par]
        if True:
            n = blk * 8 + c
            for h in range(2):
                st = sets[par * 2 + h]
                qT = c_[:, 0 + h, :]
                sb, sbb = banks[6 + h]
                P(lambda e, c=c, st=st, h=h, sb=sb: e.matmul(sb[0:64, 0:128], lhsT=st["wT"][:, c, :].bitcast(F32R), rhs=Sst[h][:].bitcast(F32R), start=True, stop=True),
                  reads=[st["b_wT"], b_S[h]], writes=[sbb])
                P(lambda e, c=c, qT=qT, h=h, sb=sb: e.matmul(sb[0:64, 128:256], lhsT=qT[:, c * 64:(c + 1) * 64].bitcast(F32R), rhs=Sst[h][:].bitcast(F32R), start=True, stop=True),
                  reads=[bcs, b_S[h]], writes=[sbb])
                V(lambda e, c=c, st=st, h=h, sb=sb: e.tensor_tensor(out=vnew[h][:].bitcast(F32R), in0=st["R"][:, c, 0:128], in1=sb[0:64, 0:128], op=ALU.subtract),
                  reads=[st["b_R"][c // 2]], writes=[b_vn[h], sbb])
                V(lambda e, n=n, h=h, sb=sb: e.tensor_scalar(out=o1[h][:], in0=sb[0:64, 128:256], scalar1=eA[:, n, h:h + 1], scalar2=None, op0=ALU.mult),
                  reads=[b_gt], writes=[b_o1[h], sbb])
                P(lambda e, c=c, st=st, h=h, sb=sb: e.matmul(sb[0:64, 256:384], lhsT=st["AqkT"][:, c, :].bitcast(F32R), rhs=vnew[h][:].bitcast(F32R), start=True, stop=True),
                  reads=[st["b_Aqk"], b_vn[h]], writes=[sbb])
                P(lambda e, c=c, st=st, h=h, sb=sb: e.matmul(sb[:, 384:512], lhsT=st["k_e"][:, c, :].bitcast(F32R), rhs=vnew[h][:].bitcast(F32R), start=True, stop=True),
                  reads=[st["b_ke"], b_vn[h]], writes=[sbb])
                V(lambda e, c=c, st=st, h=h, sb=sb: e.tensor_tensor(out=st["o_all"][:, c, :], in0=o1[h][:], in1=sb[0:64, 256:384], op=ALU.add),
                  reads=[b_o1[h]], writes=[st["b_o"], sbb])
                V(lambda e, n=n, h=h, sb=sb: e.scalar_tensor_tensor(out=Sst[h][:].bitcast(F32R), in0=Sst[h][:], scalar=dec[:, n, h:h + 1], in1=sb[:, 384:512],
                                                             op0=ALU.mult, op1=ALU.add), reads=[b_gt], writes=[b_S[h], sbb])
    def output_stage(blk):
        par = blk % 2
        for h in range(2):
            st = sets[par * 2 + h]
            o_all = st["o_all"]
            kb.dma("sp", zt[:], z_d[blk * 512:(blk + 1) * 512, h * 128:(h + 1) * 128].rearrange("(c i) d -> i c d", i=64), writes=[b_zt])
            A(lambda e: e.activation(out=zt[:], in_=zt[:], func=AF.Silu), writes=[b_zt])
            V(lambda e, o_all=o_all: e.tensor_tensor(out=ysq[:], in0=o_all[:], in1=o_all[:], op=ALU.mult), reads=[st["b_o"]], writes=[b_ysq])
            V(lambda e: e.tensor_reduce(out=ssn[:], in_=ysq[:], axis=AX.X, op=ALU.add), reads=[b_ysq], writes=[b_ssn])
            V(lambda e: e.tensor_scalar(out=ssn[:], in0=ssn[:], scalar1=1.0 / 128.0, scalar2=EPS, op0=ALU.mult, op1=ALU.add), writes=[b_ssn])
            A(lambda e: e.activation(out=ssn[:], in_=ssn[:], func=AF.Sqrt), writes=[b_ssn])
            V(lambda e: e.reciprocal(out=ssn[:], in_=ssn[:]), writes=[b_ssn])
            V(lambda e, o_all=o_all: e.tensor_tensor(out=ysq[:], in0=o_all[:], in1=bc(ssn[:, :], 128), op=ALU.mult), reads=[st["b_o"], b_ssn], writes=[b_ysq])
            V(lambda e: e.tensor_tensor(out=ysq[:], in0=ysq[:], in1=ngt[:].unsqueeze(1).to_broadcast([64, 8, 128]), op=ALU.mult),
              reads=[b_par], writes=[b_ysq])
            V(lambda e: e.tensor_tensor(out=ybf[:], in0=ysq[:], in1=zt[:], op=ALU.mult), reads=[b_ysq, b_zt], writes=[b_ybf])
            b3, b3b = banks[3]
            b3v = b3[:].bitcast(BF16)
            for c in range(8):
                P(lambda e, c=c: e.transpose(b3v[:, c * 64:(c + 1) * 64], ybf[:, c, :], identB[0:64, 0:64]), reads=[b_ybf, b_idB], writes=[b3b])
            oi = osti[0]
            osti[0] ^= 1
            A(lambda e, oi=oi: e.activation(out=yst[oi][:], in_=b3v[:, 0:512], func=AF.Copy), writes=[b_yst[oi], b3b])
            kb.dma("sp", yc_d[h * 128:(h + 1) * 128, blk * 512:(blk + 1) * 512], yst[oi][:], reads=[b_yst[oi]])
    for _ in prep_gen(0):
        pass
    for blk in range(NBL):
        g = prep_gen(blk + 1) if blk + 1 < NBL else None
        for c in range(8):
            scan_step(blk, c)
            if g is not None:
                for _ in range(PREP_SLICE):
                    if next(g, "done") == "done":
                        g = None
                        break
        if g is not None:
            for _ in g:
                pass
        output_stage(blk)
    _end(kb, own)
    return nc, kb


def build_pool(ntok=TOK, nc=None, kb=None, io=None, tag="", t0=None):
    W = 16 + ntok
    nc, kb, own = _begin(nc, kb, tag)
    p_d = _dram(nc, io, "pT", [512, W], F32, "ExternalInput")
    val_d = _dram(nc, io, "valid", [128, W], F32, "ExternalInput") if t0 is None else None
    pw_d = _dram(nc, io, "pw", [4, 128, 128], F32, "ExternalInput")
    psc_d = _dram(nc, io, "psc", [512], F32, "ExternalInput")
    yb_d = _dram(nc, io, "ybT", [512, ntok], BF16, "ExternalOutput")
    V = lambda fn, reads=(), writes=(): kb.op("dve", fn, reads, writes)
    pw = kb.T("pw_sb", [128, 4, 128], BF16)
    psc = kb.T("psc_sb", [128, 4], F32)
    b_par = Buf()
    for g in range(4):
        kb.dma("pool", pw[:, g, :], pw_d[g], writes=[b_par])
    with nc.allow_non_contiguous_dma(reason="tiny"):
        kb.dma("sp", psc[:], psc_d.rearrange("(g p) -> p g", p=128), writes=[b_par])
    vs = [kb.T(f"vs{i}", [128, W], F32) for i in range(2)]
    rc = kb.T("rc", [128, W], F32)
    xs = kb.T("xs", [128, W], F32)
    sa = [kb.T(f"sa{i}", [128, W], F32) for i in range(2)]
    yb = kb.T("yb", [128, ntok], BF16)
    ost = [kb.T(f"ost{i}", [128, 512], BF16) for i in range(2)]
    b_vs, b_rc, b_x, b_s, b_y = Buf(), Buf(), Buf(), Buf(), Buf()
    b_ost = [Buf(), Buf()]
    if t0 is None:
        kb.dma("sp", vs[0][:], val_d, writes=[b_vs])
    else:
        V(lambda e: e.memset(vs[0][:], 1.0), writes=[b_vs])
        if t0 == 0:
            V(lambda e: e.memset(vs[0][:, 0:16], 0.0), writes=[b_vs])
    vcur = 0
    oi = 0
    for g in range(4):
        w = 2 << g
        if t0 is None:
            kb.dma("sp", xs[:], p_d[g * 128:(g + 1) * 128, :], writes=[b_x])
        elif t0 == 0:
            V(lambda e: e.memset(xs[:, 0:16], 0.0), writes=[b_x])
            kb.dma("sp", xs[:, 16:W], p_d[g * 128:(g + 1) * 128, 0:ntok], writes=[b_x])
        else:
            kb.dma("sp", xs[:], p_d[g * 128:(g + 1) * 128, t0 - 16:t0 + ntok], writes=[b_x])
        step = w // 2
        vn = 1 - vcur
        kb.op("pool", lambda e, step=step, vn=vn, vcur=vcur: e.tensor_tensor(out=vs[vn][:, step:W], in0=vs[vcur][:, step:W], in1=vs[vcur][:, 0:W - step],
                                                                     op=ALU.add), writes=[b_vs])
        kb.op("pool", lambda e, step=step, vn=vn, vcur=vcur: e.tensor_copy(out=vs[vn][:, 0:step], in_=vs[vcur][:, 0:step]), writes=[b_vs])
        vcur = vn
        V(lambda e, w=w: e.memset(rc[:, 32:W], 1.0 / w), writes=[b_rc])
        V(lambda e, vcur=vcur: e.reciprocal(out=rc[:, 16:32], in_=vs[vcur][:, 16:32]), reads=[b_vs], writes=[b_rc])
        src = xs
        si = 0
        st_ = 1
        while st_ < w:
            dst = sa[si]
            V(lambda e, st_=st_, src=src, dst=dst: e.tensor_tensor(out=dst[:, st_:W], in0=src[:, st_:W], in1=src[:, 0:W - st_], op=ALU.add),
              reads=[b_x], writes=[b_s])
            V(lambda e, st_=st_, src=src, dst=dst: e.tensor_copy(out=dst[:, 0:st_], in_=src[:, 0:st_]), reads=[b_x], writes=[b_s])
            src = dst
            si ^= 1
            st_ *= 2
        V(lambda e, src=src: e.tensor_tensor(out=src[:, 16:W], in0=src[:, 16:W], in1=rc[:, 16:W], op=ALU.mult), reads=[b_rc], writes=[b_s])
        V(lambda e, src=src: e.tensor_tensor(out=yb[:], in0=src[:, 16:W], in1=xs[:, 16:W], op=ALU.subtract), reads=[b_x, b_s], writes=[b_y])
        for c in range(ntok // 512):
            ps, pb = kb.bank()
            kb.op("pe", lambda e, c=c, g=g, ps=ps: e.matmul(ps[:], lhsT=pw[:, g, :], rhs=yb[:, c * 512:(c + 1) * 512], start=True, stop=True),
                  reads=[b_par, b_y], writes=[pb])
            o_, ob = ost[oi], b_ost[oi]
            oi ^= 1
            kb.op("act", lambda e, ps=ps, o_=o_, g=g: e.activation(out=o_[:], in_=ps[:], func=AF.Copy, scale=psc[:, g:g + 1]),
                  reads=[b_par], writes=[ob, pb])
            kb.dma("sp", yb_d[g * 128:(g + 1) * 128, c * 512:(c + 1) * 512], o_[:], reads=[ob])
    _end(kb, own)
    return nc, kb


def emit_mod_rows(kb, nc, c_ap, adaw_ap, adab_ap, col0, ncol, name):
    row = kb.T(name + "_row", [128, ncol], F32)
    b = Buf()
    with kb.S(name + "_cT", [128, 8], F32) as cT, kb.S(name + "_sTb", [128, 8, 128], F32) as sTb, kb.S(name + "_bb", [128, ncol], F32) as bb, \
            kb.S(name + "_w0", [128, 8, 256], F32) as wt0, kb.S(name + "_w1", [128, 8, 256], F32) as wt1:
        with nc.allow_non_contiguous_dma(reason="tiny"):
            kb.dma("sp", cT[:], c_ap.rearrange("(k p) -> p k", p=128), writes=[b])
            kb.dma("sp", bb[:], adab_ap[col0:col0 + ncol].unsqueeze(0).to_broadcast([128, ncol]), writes=[b])
        kb.op("act", lambda e: e.activation(out=cT[:], in_=cT[:], func=AF.Silu), writes=[b])
        kb.op("dve", lambda e: e.tensor_copy(out=sTb[:], in_=cT[:].unsqueeze(2).to_broadcast([128, 8, 128])), writes=[b])
        wts = [wt0, wt1]
        wb = [Buf(), Buf()]
        for g in range(ncol // 256):
            wt, wbuf = wts[g % 2], wb[g % 2]
            kb.dma("sp", wt[:], adaw_ap[:, col0 + g * 256:col0 + (g + 1) * 256].rearrange("(k p) n -> p k n", p=128), writes=[wbuf])
            ps, pb = kb.bank()
            for k in range(8):
                kb.op("pe", lambda e, wt=wt, k=k, ps=ps: e.matmul(ps[:, 0:256], lhsT=sTb[:, k, :], rhs=wt[:, k, :], start=(k == 0), stop=(k == 7)),
                      reads=[wbuf, b], writes=[pb])
            kb.op("dve", lambda e, g=g, ps=ps: e.tensor_tensor(out=row[:, g * 256:(g + 1) * 256], in0=ps[:, 0:256], in1=bb[:, g * 256:(g + 1) * 256], op=ALU.add),
                  writes=[b, pb])
        kb.barrier()
    return row, b


def build_c1(ntok=TOK, nc=None, kb=None, io=None, tag=""):
    nc, kb, own = _begin(nc, kb, tag)
    x = _dram(nc, io, "x", [ntok, D_MODEL], F32, "ExternalInput")
    cvec = _dram(nc, io, "c", [D_MODEL], F32, "ExternalInput")
    adaw = _dram(nc, io, "adaw", [D_MODEL, 3072], F32, "ExternalInput")
    adab = _dram(nc, io, "adab", [3072], F32, "ExternalInput")
    g1 = _dram(nc, io, "g1", [D_MODEL], F32, "ExternalInput")
    wmg_d = _dram(nc, io, "wmg", [D_MODEL, 3072], F32, "ExternalInput")
    wbr_d = _dram(nc, io, "wbr", [3, 512, D_MODEL], F32, "ExternalInput")
    wout_d = _dram(nc, io, "wout", [D_MODEL, D_MODEL], F32, "ExternalInput")
    yT_d = _dram(nc, io, "yT", [3, 512, ntok], BF16, "ExternalInput")
    x1_d = _dram(nc, io, "x1", [ntok, D_MODEL], F32, "ExternalOutput")

    ident, identb = emit_identity(kb, nc)
    modT, bm = emit_mod(kb, nc, cvec, adaw, adab, 2, "mod")
    A, Bt, abb = emit_mod_AB(kb, nc, modT, bm, g1, 0, 1, "n1")
    gt_bc, b_gt = emit_mod_rows(kb, nc, cvec, adaw, adab, 2048, 1024, "gt")

    wmg = kb.T("wmg_sb", [128, 8, 3072], BF16)
    wbr = kb.T("wbr_sb", [128, 3, 4, 1024], BF16)
    wout = kb.T("wout_sb", [128, 8, 1024], BF16)
    b_wmg, b_wbr, b_wout = Buf("wmg"), Buf("wbr"), Buf("wout")
    for k in range(8):
        for c0 in range(0, 3072, 1024):
            kb.dma("pool", wmg[:, k, c0:c0 + 1024], wmg_d[k * 128:(k + 1) * 128, c0:c0 + 1024], writes=[b_wmg])
    for br in range(3):
        for k in range(4):
            kb.dma("pool", wbr[:, br, k, :], wbr_d[br, k * 128:(k + 1) * 128, :], writes=[b_wbr])
    for k in range(8):
        kb.dma("pool", wout[:, k, :], wout_d[k * 128:(k + 1) * 128, :], writes=[b_wout])

    NB = ntok // 512
    xst = [kb.T(f"xst{i}", [128, D_MODEL], F32) for i in range(4)]
    b_xst = [Buf() for _ in range(4)]
    xr = [kb.T(f"xr{i}", [128, D_MODEL], F32) for i in range(2)]
    b_xr = [Buf(), Buf()]
    nsl = norm_slots(kb, 4)
    hTs = [kb.T(f"hT{i}", [128, 8, 512], BF16) for i in range(2)]
    hbs = [Buf(), Buf()]
    yblk = [kb.T(f"yblk{i}", [128, 3, 4, 512], BF16) for i in range(1)] * 2
    b_y = [Buf()] * 2
    gsb = [kb.T(f"gsb{i}", [128, 512], F32) for i in range(2)]
    b_g = [Buf(), Buf()]
    tmp = [kb.T(f"mtmp{i}", [128, 512], F32) for i in range(2)]
    b_tmp = [Buf(), Buf()]
    macc = kb.T("macc", [128, 512], F32)
    b_macc = Buf()
    mT = kb.T("mT", [128, 8, 512], BF16)
    b_mT = Buf()
    ot = [kb.T(f"ot{i}", [128, D_MODEL], F32) for i in range(2)]
    b_ot = [Buf(), Buf()]
    gi = 0
    oi = 0

    def norm_pre(blk):
        for t in range(4):
            ti = blk * 4 + t
            kb.dma("sp", xst[t][:], x[ti * 128:(ti + 1) * 128, :], writes=[b_xst[t]])
            emit_norm_pre(kb, xst[t][:], b_xst[t], nsl[t])

    def norm_post(blk, t):
        emit_norm_post(kb, nsl[t], hTs[blk % 2][:, :, t * 128:(t + 1) * 128], hbs[blk % 2], A, Bt, abb, ident, identb)

    norm_pre(0)
    for t in range(4):
        norm_post(0, t)
    for blk in range(NB):
        hT, hb = hTs[blk % 2], hbs[blk % 2]
        yb_, byb = yblk[blk % 2], b_y[blk % 2]
        nxt = blk + 1 < NB
        for br in range(3):
            kb.dma("sp", yb_[:, br, :, :], yT_d[br, :, blk * 512:(blk + 1) * 512].rearrange("(k p) t -> p k t", p=128), writes=[byb])
        if nxt:
            norm_pre(blk + 1)
        for oc in range(8):
            for br in range(3):
                ps, pb = kb.bank()
                for k in range(8):
                    kb.op("pe", lambda e, k=k, br=br, oc=oc, ps=ps: e.matmul(ps[:], lhsT=wmg[:, k, br * 1024 + oc * 128:br * 1024 + (oc + 1) * 128],
                                                                     rhs=hT[:, k, :], start=(k == 0), stop=(k == 7)), reads=[b_wmg, hb], writes=[pb])
                g_, bg_ = gsb[gi % 2], b_g[gi % 2]
                t_, bt_ = tmp[gi % 2], b_tmp[gi % 2]
                gi += 1
                kb.op("act", lambda e, ps=ps, g_=g_: e.activation(out=g_[:], in_=ps[:], func=AF.Sigmoid), writes=[bg_, pb])
                ps2, pb2 = kb.bank()
                for k in range(4):
                    kb.op("pe", lambda e, k=k, br=br, oc=oc, ps2=ps2: e.matmul(ps2[:], lhsT=wbr[:, br, k, oc * 128:(oc + 1) * 128], rhs=yb_[:, br, k, :],
                                                                       start=(k == 0), stop=(k == 3)), reads=[b_wbr, byb], writes=[pb2])
                if br == 0:
                    kb.op("dve", lambda e, ps2=ps2, g_=g_: e.tensor_tensor(out=macc[:], in0=ps2[:], in1=g_[:], op=ALU.mult),
                          reads=[bg_], writes=[b_macc, pb2])
                else:
                    kb.op("dve", lambda e, ps2=ps2, g_=g_, t_=t_: e.tensor_tensor(out=t_[:], in0=ps2[:], in1=g_[:], op=ALU.mult),
                          reads=[bg_], writes=[bt_, pb2])
                    if br == 1:
                        kb.op("pool", lambda e, t_=t_: e.tensor_tensor(out=macc[:], in0=macc[:], in1=t_[:], op=ALU.add), reads=[bt_], writes=[b_macc])
                    else:
                        kb.op("pool", lambda e, t_=t_, oc=oc: e.tensor_tensor(out=mT[:, oc, :], in0=macc[:], in1=t_[:], op=ALU.add),
                              reads=[bt_, b_macc], writes=[b_mT])
            if nxt and oc % 2 == 1:
                norm_post(blk + 1, oc // 2)
        for t in range(4):
            o_, bo_ = ot[oi % 2], b_ot[oi % 2]
            oi += 1
            for n in range(2):
                ps, pb = kb.bank()
                for k in range(8):
                    kb.op("pe", lambda e, k=k, t=t, n=n, ps=ps: e.matmul(ps[:], lhsT=mT[:, k, t * 128:(t + 1) * 128], rhs=wout[:, k, n * 512:(n + 1) * 512],
                                                                  start=(k == 0), stop=(k == 7)), reads=[b_wout, b_mT], writes=[pb])
                kb.op("dve", lambda e, n=n, ps=ps, o_=o_: e.tensor_tensor(out=o_[:, n * 512:(n + 1) * 512], in0=ps[:], in1=gt_bc[:, n * 512:(n + 1) * 512],
                                                                   op=ALU.mult), reads=[b_gt], writes=[bo_, pb])
            xr_, bxr_ = xr[oi % 2], b_xr[oi % 2]
            kb.dma("sp", xr_[:], x[blk * 512 + t * 128:blk * 512 + (t + 1) * 128, :], writes=[bxr_])
            kb.op("pool", lambda e, o_=o_, xr_=xr_: e.tensor_tensor(out=o_[:], in0=o_[:], in1=xr_[:], op=ALU.add), reads=[bxr_], writes=[bo_])
            kb.dma("sp", x1_d[blk * 512 + t * 128:blk * 512 + (t + 1) * 128, :], o_[:], reads=[bo_])
    _end(kb, own)
    return nc, kb


def build_c2(ntok=TOK, final=False, nc=None, kb=None, io=None, tag=""):
    nc, kb, own = _begin(nc, kb, tag)
    x = _dram(nc, io, "x", [ntok, D_MODEL], F32, "ExternalInput")
    cvec = _dram(nc, io, "c", [D_MODEL], F32, "ExternalInput")
    adaw = _dram(nc, io, "adaw", [D_MODEL, 3072], F32, "ExternalInput")
    adab = _dram(nc, io, "adab", [3072], F32, "ExternalInput")
    g2 = _dram(nc, io, "g2", [D_MODEL], F32, "ExternalInput")
    w1_d = _dram(nc, io, "w1", [D_MODEL, 4096], F32, "ExternalInput")
    w2_d = _dram(nc, io, "w2", [4096, D_MODEL], F32, "ExternalInput")
    fg_d = _dram(nc, io, "fg", [D_MODEL], F32, "ExternalInput")
    x2_d = _dram(nc, io, "x2", [ntok, D_MODEL], F32, "ExternalOutput")

    ident, identb = emit_identity(kb, nc)
    modT, bm = emit_mod(kb, nc, cvec, adaw, adab, 2, "mod")
    A, Bt, abb = emit_mod_AB(kb, nc, modT, bm, g2, 0, 1, "n2")
    gt_bc, b_gt = emit_mod_rows(kb, nc, cvec, adaw, adab, 2048, 1024, "gt")
    fg_bc = None
    if final:
        fg_bc = kb.T("fg_bc", [128, D_MODEL], F32)
        with nc.allow_non_contiguous_dma(reason="bcast"):
            kb.dma("sp", fg_bc[:], fg_d.unsqueeze(0).to_broadcast([128, D_MODEL]), writes=[b_gt])

    w1 = kb.T("w1_sb", [128, 8, 4096], BF16)
    w2 = kb.T("w2_sb", [128, 32, 1024], BF16)
    b_w1 = [Buf(f"w1_{i}") for i in range(4)]
    b_w2 = Buf("w2")
    for bi, c0 in enumerate(range(0, 4096, 1024)):
        for k in range(8):
            kb.dma("pool", w1[:, k, c0:c0 + 1024], w1_d[k * 128:(k + 1) * 128, c0:c0 + 1024], writes=[b_w1[bi]])
    for k in range(32):
        kb.dma("pool", w2[:, k, :], w2_d[k * 128:(k + 1) * 128, :], writes=[b_w2])

    BT = 256
    NB = ntok // BT
    xst = [kb.T(f"xst{i}", [128, D_MODEL], F32) for i in range(2)]
    b_xst = [Buf(), Buf()]
    xr = kb.T("xr", [128, D_MODEL], F32)
    b_xr = Buf()
    nsl = norm_slots(kb, 2)
    fsc = dict(j=kb.T("fjunk", [128, D_MODEL], BF16), ss=kb.T("fss", [128, 1], F32), r=kb.T("frstd", [128, 1], F32), b=Buf()) if final else None
    hTs = [kb.T(f"hT{i}", [128, 8, BT], BF16) for i in range(2)]
    hbs = [Buf(), Buf()]
    uT = kb.T("uT", [128, 32, BT], BF16)
    b_u = Buf()
    rl = [kb.T(f"rl{i}", [128, 2, BT], F32) for i in range(2)]
    b_rl = [Buf(), Buf()]
    ot = [kb.T(f"ot{i}", [128, D_MODEL], F32) for i in range(2)]
    b_ot = [Buf(), Buf()]
    ri = 0
    oi = 0
    def norm_pre(blk):
        for t in range(2):
            ti = blk * 2 + t
            kb.dma("sp", xst[t][:], x[ti * 128:(ti + 1) * 128, :], writes=[b_xst[t]])
            emit_norm_pre(kb, xst[t][:], b_xst[t], nsl[t])

    def norm_post(blk, t):
        emit_norm_post(kb, nsl[t], hTs[blk % 2][:, :, t * 128:(t + 1) * 128], hbs[blk % 2], A, Bt, abb, ident, identb)

    norm_pre(0)
    for t in range(2):
        norm_post(0, t)
    for blk in range(NB):
        hT, hb = hTs[blk % 2], hbs[blk % 2]
        nxt = blk + 1 < NB
        if nxt:
            norm_pre(blk + 1)
        for fp in range(16):
            if nxt and fp in (6, 12):
                norm_post(blk + 1, 0 if fp == 6 else 1)
            ps, pb = kb.bank()
            for j in range(2):
                fc = fp * 2 + j
                for k in range(8):
                    kb.op("pe", lambda e, k=k, fc=fc, j=j, ps=ps: e.matmul(ps[:, j * BT:(j + 1) * BT], lhsT=w1[:, k, fc * 128:(fc + 1) * 128], rhs=hT[:, k, :],
                                                                    start=(k == 0), stop=(k == 7)), reads=[b_w1[fc // 8], hb], writes=[pb])
            r_, br_ = rl[ri % 2], b_rl[ri % 2]
            ri += 1
            kb.op("act", lambda e, ps=ps, r_=r_: e.activation(out=r_[:].rearrange("p a b -> p (a b)"), in_=ps[:], func=AF.Relu), writes=[br_, pb])
            kb.op("dve", lambda e, r_=r_, fp=fp: e.tensor_tensor(out=uT[:, fp * 2:fp * 2 + 2, :], in0=r_[:], in1=r_[:], op=ALU.mult),
                  reads=[br_], writes=[b_u])
        for t in range(2):
            o_, bo_ = ot[oi % 2], b_ot[oi % 2]
            oi += 1
            for n in range(2):
                ps, pb = kb.bank()
                for k in range(32):
                    kb.op("pe", lambda e, k=k, t=t, n=n, ps=ps: e.matmul(ps[:], lhsT=uT[:, k, t * 128:(t + 1) * 128], rhs=w2[:, k, n * 512:(n + 1) * 512],
                                                                  start=(k == 0), stop=(k == 31)), reads=[b_w2, b_u], writes=[pb])
                kb.op("dve", lambda e, n=n, ps=ps, o_=o_: e.tensor_tensor(out=o_[:, n * 512:(n + 1) * 512], in0=ps[:], in1=gt_bc[:, n * 512:(n + 1) * 512],
                                                                   op=ALU.mult), reads=[b_gt], writes=[bo_, pb])
            kb.dma("sp", xr[:], x[blk * BT + t * 128:blk * BT + (t + 1) * 128, :], writes=[b_xr])
            kb.op("pool", lambda e, o_=o_: e.tensor_tensor(out=o_[:], in0=o_[:], in1=xr[:], op=ALU.add), reads=[b_xr], writes=[bo_])
            if final:
                kb.op("act", lambda e, o_=o_: e.activation(out=fsc["j"][:], in_=o_[:], func=AF.Square, accum_out=fsc["ss"][:]), reads=[bo_], writes=[fsc["b"]])
                kb.op("dve", lambda e: e.tensor_scalar(out=fsc["ss"][:], in0=fsc["ss"][:], scalar1=1.0 / D_MODEL, scalar2=EPS, op0=ALU.mult, op1=ALU.add),
                      writes=[fsc["b"]])
                kb.op("act", lambda e: e.activation(out=fsc["ss"][:], in_=fsc["ss"][:], func=AF.Sqrt), writes=[fsc["b"]])
                kb.op("dve", lambda e: e.reciprocal(out=fsc["r"][:], in_=fsc["ss"][:]), writes=[fsc["b"]])
                kb.op("dve", lambda e, o_=o_: e.scalar_tensor_tensor(out=o_[:], in0=o_[:], scalar=fsc["r"][:, 0:1], in1=fg_bc[:], op0=ALU.mult, op1=ALU.mult),
                      reads=[fsc["b"], b_gt], writes=[bo_])
            kb.dma("sp", x2_d[blk * BT + t * 128:blk * BT + (t + 1) * 128, :], o_[:], reads=[bo_])
    _end(kb, own)
    return nc, kb


def build_fused(S=SEQ, depth=DEPTH):
    nc = bass.Bass("TRN2", target_bir_lowering=False)
    kb = KB(nc)
    kb.init_banks()
    EI, EO = "ExternalInput", "ExternalOutput"
    din = lambda name, shape, dt=F32: nc.dram_tensor(name, shape, dt, kind=EI).ap()
    dint = lambda name, shape, dt=F32: nc.dram_tensor(name, shape, dt, kind="Internal").ap()
    x = din("x", [S, D_MODEL])
    c = din("c", [D_MODEL])
    slopes = din("slopes", [2, 4])
    fg = din("fg", [D_MODEL])
    out = nc.dram_tensor("out", [S, D_MODEL], F32, kind=EO).ap()
    L = []
    for l in range(depth):
        L.append(dict(
            adaw=din(f"adaw{l}", [D_MODEL, 6144]), adab=din(f"adab{l}", [6144]), g1=din(f"g1_{l}", [D_MODEL]), g2=din(f"g2_{l}", [D_MODEL]),
            wA=din(f"wA{l}", [D_MODEL, A_NCOL]), wmg=din(f"wmg{l}", [D_MODEL, 3072]),
            phk1=din(f"phk1_{l}", [2048, 128]), phv1=din(f"phv1_{l}", [2048, 128]), phk2=din(f"phk2_{l}", [128, 64]), phv2=din(f"phv2_{l}", [128, 64]),
            posk=din(f"posk{l}", [32, 64]), posv=din(f"posv{l}", [32, 64]), pw=din(f"pw{l}", [4, 128, 128]), psc=din(f"psc{l}", [512]),
            convw=din(f"convw{l}", [4, 1536]), Alog=din(f"Alog{l}", [1, 4]), dtb=din(f"dtb{l}", [1, 4]), ng=din(f"ng{l}", [1, 128]),
            wbr=din(f"wbr{l}", [3, 512, D_MODEL]), wout=din(f"wout{l}", [D_MODEL, D_MODEL]),
            w1=din(f"w1_{l}", [D_MODEL, 4096]), w2=din(f"w2_{l}", [4096, D_MODEL])))
    fm_bf = dint("s_fm_bf", [A_FM_BF * 128, S], BF16)
    fm_f32 = dint("s_fm_f32", [A_FM_F32 * 128, S], F32)
    tm_bf = dint("s_tm_bf", [S, 256], BF16)
    tm_f32 = dint("s_tm_f32", [S, 544], F32)
    yT = dint("s_yT", [3, 512, S], BF16)
    x1 = dint("s_x1", [S, D_MODEL], F32)
    x2 = dint("s_x2", [S, D_MODEL], F32)
    rows_k = dint("s_rows_k", [4, S], BF16)
    rows_q = dint("s_rows_q", [8, 4, S], BF16)
    rows_c = dint("s_rows_c", [4, S // 16], BF16)
    build_nsa_rows(S, nc, kb, {"slopes": slopes.rearrange("g h -> (g h)").unsqueeze(0), "rows_k": rows_k, "rows_q": rows_q, "rows_c": rows_c})
    xin = x
    for l in range(depth):
        P = L[l]
        build_phaseA(S, nc, kb, tag=f"L{l}A_", io={"x": xin, "c": c, "adaw": P["adaw"][:, 0:2048], "adab": P["adab"][0:2048], "g1": P["g1"],
                                                "w": P["wA"], "fm_bf": fm_bf, "fm_f32": fm_f32, "tm_bf": tm_bf, "tm_f32": tm_f32})
        for g in range(2):
            build_nsa(S, nc, kb, tag=f"L{l}N{g}_", io={
                "qT": fm_bf[g * 256:(g + 1) * 256, :].rearrange("(h d) s -> h d s", d=64),
                "kcT": fm_bf[512 + g * 64:512 + (g + 1) * 64, :], "vcT": fm_bf[640 + g * 64:640 + (g + 1) * 64, :],
                "ksT": fm_bf[768 + g * 64:768 + (g + 1) * 64, :], "kwT": fm_bf[896 + g * 64:896 + (g + 1) * 64, :],
                "vs": tm_bf[:, g * 64:(g + 1) * 64], "vw": tm_bf[:, 128 + g * 64:128 + (g + 1) * 64],
                "gate": tm_f32[:, g * 12:(g + 1) * 12],
                "phk1": P["phk1"], "phv1": P["phv1"], "phk2": P["phk2"], "phv2": P["phv2"], "posk": P["posk"], "posv": P["posv"],
                "slopes": slopes[g:g + 1, :], "yaT": yT[0, g * 256:(g + 1) * 256, :],
                "rows_k": rows_k, "rows_q": rows_q[4 * g:4 * g + 4], "rows_c": rows_c})
        for p in range(2):
            build_dn(S, nc, kb, tag=f"L{l}D{p}_", io={
                "xq": fm_f32[512:2048, :].rearrange("(w c) s -> w c s", w=3)[:, 2 * p * 128:(2 * p + 2) * 128, :],
                "convw": P["convw"].rearrange("k (w c) -> k w c", w=3)[:, :, 2 * p * 128:(2 * p + 2) * 128],
                "z": tm_f32[:, 32 + 2 * p * 128:32 + (2 * p + 2) * 128],
                "blog": tm_f32[:, 24 + 2 * p:24 + 2 * p + 2], "alog": tm_f32[:, 28 + 2 * p:28 + 2 * p + 2],
                "Alog": P["Alog"][:, 2 * p:2 * p + 2], "dtb": P["dtb"][:, 2 * p:2 * p + 2], "ng": P["ng"],
                "ycT": yT[2, p * 256:(p + 1) * 256, :]})
        HP = min(S, 4096)
        for hh in range(S // HP):
            build_pool(HP, nc, kb, tag=f"L{l}P{hh}_", t0=hh * HP, io={
                "pT": fm_f32[0:512, :], "pw": P["pw"], "psc": P["psc"], "ybT": yT[1, :, hh * HP:(hh + 1) * HP]})
        build_c1(S, nc, kb, tag=f"L{l}C1_", io={"x": xin, "c": c, "adaw": P["adaw"][:, 0:3072], "adab": P["adab"][0:3072], "g1": P["g1"],
                                               "wmg": P["wmg"], "wbr": P["wbr"], "wout": P["wout"], "yT": yT, "x1": x1})
        last = (l == depth - 1)
        build_c2(S, last, nc, kb, tag=f"L{l}C2_", io={"x": x1, "c": c, "adaw": P["adaw"][:, 3072:6144], "adab": P["adab"][3072:6144],
                                                     "g2": P["g2"], "w1": P["w1"], "w2": P["w2"], "fg": fg, "x2": out if last else x2})
        xin = x2
    kb.finish("sp")
    return nc, kb


def fused_inputs(b, x, c, ada_w, ada_b, norm1_g, norm2_g, w_in, phi_k1, phi_k2, phi_v1, phi_v2, pos_k, pos_v,
                 pool_w, pool_scale, dn_conv_w, dn_A_log, dn_dt_bias, dn_norm_g, w_branch_nsa, w_branch_pool,
                 w_branch_dn, w_out, mlp_w1, mlp_w2, final_g, S=SEQ, depth=DEPTH):
    f32 = np.float32
    idxA, _ = phaseA_weight_perm()
    o = np.cumsum((0,) + IN_SIZES)
    A_ = lambda a: _c(np.asarray(a, f32))
    slopes = (2.0 ** (-(np.arange(8, dtype=np.float64) + 1.0))).astype(f32).reshape(2, 4)
    im = {"x": A_(x[b][:S]), "c": A_(c[b]), "slopes": slopes, "fg": A_(final_g)}
    for l in range(depth):
        w_in_l = np.asarray(w_in[l], f32)
        im.update({
            f"adaw{l}": A_(ada_w[l]), f"adab{l}": A_(ada_b[l]), f"g1_{l}": A_(norm1_g[l]), f"g2_{l}": A_(norm2_g[l]),
            f"wA{l}": _c(w_in_l[:, idxA]), f"wmg{l}": _c(w_in_l[:, o[13]:o[14]]),
            f"phk1_{l}": A_(phi_k1[l]), f"phv1_{l}": A_(phi_v1[l]), f"phk2_{l}": A_(phi_k2[l]), f"phv2_{l}": A_(phi_v2[l]),
            f"posk{l}": A_(pos_k[l]), f"posv{l}": A_(pos_v[l]), f"pw{l}": A_(pool_w[l]), f"psc{l}": A_(pool_scale[l]),
            f"convw{l}": A_(dn_conv_w[l]), f"Alog{l}": A_(dn_A_log[l]).reshape(1, 4), f"dtb{l}": A_(dn_dt_bias[l]).reshape(1, 4),
            f"ng{l}": A_(dn_norm_g[l]).reshape(1, 128),
            f"wbr{l}": _c(np.stack([np.asarray(w_branch_nsa[l], f32), np.asarray(w_branch_pool[l], f32), np.asarray(w_branch_dn[l], f32)])),
            f"wout{l}": A_(w_out[l]), f"w1_{l}": A_(mlp_w1[l]), f"w2_{l}": A_(mlp_w2[l])})
    return im


IN_SIZES = (512, 128, 128, 128, 128, 128, 128, 24, 512, 1536, 512, 4, 4, 3072)
_PROGS = {}


def _prog(name, fn, *args):
    key = (name,) + args
    if key not in _PROGS:
        _PROGS[key] = fn(*args)[0]
    return _PROGS[key]


def _run(nc, in_maps):
    res = run_bass_kernel_spmd(nc, in_maps, core_ids=list(range(N_CORES)))
    return res.results


def _c(a):
    return np.ascontiguousarray(a)


def kernel(**inputs):
    nc = _prog("F", build_fused, SEQ, DEPTH)
    base = fused_inputs(0, **inputs)
    ims = []
    for i in range(N_CORES):
        b = i % BATCH
        im = dict(base)
        im["x"] = _c(np.asarray(inputs["x"][b], np.float32))
        im["c"] = _c(np.asarray(inputs["c"][b], np.float32))
        ims.append(im)
    res = _run(nc, ims)
    return np.stack([np.asarray(res[b]["out"], np.float32) for b in range(BATCH)])


def kernel_unfused(x, c, ada_w, ada_b, norm1_g, norm2_g, w_in, phi_k1, phi_k2, phi_v1, phi_v2, pos_k, pos_v,
                   pool_w, pool_scale, dn_conv_w, dn_A_log, dn_dt_bias, dn_norm_g, w_branch_nsa, w_branch_pool,
                   w_branch_dn, w_out, mlp_w1, mlp_w2, final_g):
    f32 = np.float32
    x = np.asarray(x, f32)
    c = np.asarray(c, f32)
    S = SEQ
    H = TOK
    idxA, off = phaseA_weight_perm()
    slopes = (2.0 ** (-(np.arange(8, dtype=np.float64) + 1.0))).astype(f32)
    ncA = _prog("A", build_phaseA, TOK)
    ncN = _prog("N", build_nsa, SEQ)
    ncD = _prog("D", build_dn, SEQ)
    ncP = _prog("P", build_pool, TOK)
    ncC1 = _prog("C1", build_c1, TOK)
    xcur = x
    cores = [(ci // 2, ci % 2) for ci in range(N_CORES)]
    for l in range(DEPTH):
        adaw_l = np.asarray(ada_w[l], f32)
        adab_l = np.asarray(ada_b[l], f32)
        w_in_l = np.asarray(w_in[l], f32)
        wA = _c(w_in_l[:, idxA])
        adaw1 = _c(adaw_l[:, 0:2048])
        adab1 = _c(adab_l[0:2048])
        ims = [{"x": _c(xcur[b, p * H:(p + 1) * H]), "c": _c(c[b]), "adaw": adaw1, "adab": adab1,
                "g1": _c(np.asarray(norm1_g[l], f32)), "w": wA} for (b, p) in cores]
        rA = _run(ncA, ims)
        fm_bf = [np.concatenate([rA[2 * b]["fm_bf"], rA[2 * b + 1]["fm_bf"]], axis=1) for b in range(BATCH)]
        fm_f32 = [np.concatenate([rA[2 * b]["fm_f32"], rA[2 * b + 1]["fm_f32"]], axis=1) for b in range(BATCH)]
        tm_bf = [np.concatenate([rA[2 * b]["tm_bf"], rA[2 * b + 1]["tm_bf"]], axis=0) for b in range(BATCH)]
        tm_f32 = [np.concatenate([rA[2 * b]["tm_f32"], rA[2 * b + 1]["tm_f32"]], axis=0) for b in range(BATCH)]
        del rA
        ims = []
        for (b, g) in cores:
            fb = fm_bf[b]
            ims.append({
                "qT": _c(fb[g * 256:(g + 1) * 256].reshape(4, 64, S)),
                "kcT": _c(fb[512 + g * 64:512 + (g + 1) * 64]), "vcT": _c(fb[640 + g * 64:640 + (g + 1) * 64]),
                "ksT": _c(fb[768 + g * 64:768 + (g + 1) * 64]), "kwT": _c(fb[896 + g * 64:896 + (g + 1) * 64]),
                "vs": _c(tm_bf[b][:, g * 64:(g + 1) * 64]), "vw": _c(tm_bf[b][:, 128 + g * 64:128 + (g + 1) * 64]),
                "gate": _c(tm_f32[b][:, g * 12:(g + 1) * 12]),
                "phk1": _c(np.asarray(phi_k1[l], f32)), "phv1": _c(np.asarray(phi_v1[l], f32)),
                "phk2": _c(np.asarray(phi_k2[l], f32)), "phv2": _c(np.asarray(phi_v2[l], f32)),
                "posk": _c(np.asarray(pos_k[l], f32)), "posv": _c(np.asarray(pos_v[l], f32)),
                "slopes": _c(slopes[4 * g:4 * g + 4].reshape(1, 4)),
            })
        rN = _run(ncN, ims)
        yaT = [np.concatenate([rN[2 * b]["yaT"], rN[2 * b + 1]["yaT"]], axis=0) for b in range(BATCH)]
        del rN
        cwfull = np.asarray(dn_conv_w[l], f32)
        ims = []
        for (b, p) in cores:
            heads = [2 * p, 2 * p + 1]
            dq = fm_f32[b][512:2048]
            rows = np.concatenate([dq[w_ * 512 + h * 128:w_ * 512 + (h + 1) * 128] for w_ in range(3) for h in heads], axis=0)
            xq = _c(rows.reshape(3, 256, S))
            cw = np.concatenate([cwfull[:, w_ * 512 + h * 128:w_ * 512 + (h + 1) * 128] for w_ in range(3) for h in heads], axis=1)
            tf = tm_f32[b]
            ims.append({"xq": xq, "convw": _c(cw.reshape(4, 3, 256)), "z": _c(tf[:, 32 + heads[0] * 128:32 + (heads[1] + 1) * 128]),
                        "blog": _c(tf[:, 24 + heads[0]:24 + heads[1] + 1]), "alog": _c(tf[:, 28 + heads[0]:28 + heads[1] + 1]),
                        "Alog": _c(np.asarray(dn_A_log[l], f32)[heads].reshape(1, 2)),
                        "dtb": _c(np.asarray(dn_dt_bias[l], f32)[heads].reshape(1, 2)),
                        "ng": _c(np.asarray(dn_norm_g[l], f32).reshape(1, 128))})
        rD = _run(ncD, ims)
        ycT = [np.concatenate([rD[2 * b]["ycT"], rD[2 * b + 1]["ycT"]], axis=0) for b in range(BATCH)]
        del rD
        ims = []
        for (b, p) in cores:
            pin = fm_f32[b][0:512]
            W = 16 + H
            pT = np.zeros((512, W), f32)
            valid = np.zeros((128, W), f32)
            lo = p * H - 16
            s0 = max(lo, 0)
            pT[:, s0 - lo:] = pin[:, s0:p * H + H]
            valid[:, s0 - lo:] = 1.0
            ims.append({"pT": pT, "valid": valid, "pw": _c(np.asarray(pool_w[l], f32)), "psc": _c(np.asarray(pool_scale[l], f32))})
        rP = _run(ncP, ims)
        ybT = [np.concatenate([rP[2 * b]["ybT"], rP[2 * b + 1]["ybT"]], axis=1) for b in range(BATCH)]
        del rP, fm_bf, fm_f32, tm_bf, tm_f32
        o = np.cumsum((0,) + IN_SIZES)
        wmg = _c(w_in_l[:, o[13]:o[14]])
        wbr = _c(np.stack([np.asarray(w_branch_nsa[l], f32), np.asarray(w_branch_pool[l], f32), np.asarray(w_branch_dn[l], f32)]))
        adaw3 = _c(adaw_l[:, 0:3072])
        adab3 = _c(adab_l[0:3072])
        ims = []
        for (b, p) in cores:
            sl = slice(p * H, (p + 1) * H)
            ims.append({"x": _c(xcur[b, sl]), "c": _c(c[b]), "adaw": adaw3, "adab": adab3, "g1": _c(np.asarray(norm1_g[l], f32)),
                        "wmg": wmg, "wbr": wbr, "wout": _c(np.asarray(w_out[l], f32)),
                        "yT": _c(np.stack([yaT[b][:, sl], ybT[b][:, sl], ycT[b][:, sl]]))})
        rC1 = _run(ncC1, ims)
        x1 = np.stack([np.concatenate([rC1[2 * b]["x1"], rC1[2 * b + 1]["x1"]], axis=0) for b in range(BATCH)])
        del rC1
        ncC2 = _prog("C2", build_c2, TOK, l == DEPTH - 1)
        adaw6 = _c(adaw_l[:, 3072:6144])
        adab6 = _c(adab_l[3072:6144])
        ims = [{"x": _c(x1[b, p * H:(p + 1) * H]), "c": _c(c[b]), "adaw": adaw6, "adab": adab6, "g2": _c(np.asarray(norm2_g[l], f32)),
                "w1": _c(np.asarray(mlp_w1[l], f32)), "w2": _c(np.asarray(mlp_w2[l], f32)), "fg": _c(np.asarray(final_g, f32))}
               for (b, p) in cores]
        rC2 = _run(ncC2, ims)
        xcur = np.stack([np.concatenate([rC2[2 * b]["x2"], rC2[2 * b + 1]["x2"]], axis=0) for b in range(BATCH)])
        del rC2
    return xcur.astype(np.float32)
```

```python
import numpy as np
import ml_dtypes
import concourse.bass as bass
import concourse.mybir as mybir
from concourse.bass_utils import run_bass_kernel_spmd

F32 = mybir.dt.float32
BF16 = mybir.dt.bfloat16
I32 = mybir.dt.int32
F32R = mybir.dt.float32r
AF = mybir.ActivationFunctionType
ALU = mybir.AluOpType
AX = mybir.AxisListType

D_MODEL = 1024
IN_SIZES = (512, 128, 128, 128, 128, 128, 128, 24, 512, 1536, 512, 4, 4, 3072)
BATCH = 4
SEQ = 8192
DEPTH = 2
N_CORES = 8
TOK = SEQ // 2
EPS = 1e-6


class Buf:
    __slots__ = ("name", "w", "r")

    def __init__(self, name=""):
        self.name = name
        self.w = {}
        self.r = {}


class KB:
    def __init__(self, nc, n_dma_sems=60, n_sw=12):
        self.nc = nc
        self.E = {"pe": nc.tensor, "dve": nc.vector, "act": nc.scalar, "pool": nc.gpsimd, "sp": nc.sync}
        self.sems = []
        self.semidx = {}
        for e in ("pe", "dve", "act", "pool"):
            self.semidx[e] = len(self.sems)
            self.sems.append(nc.alloc_semaphore(name="sem_" + e))
        self.cnt = {e: 0 for e in self.semidx}
        self.seen = {e: {} for e in self.E}
        self.dslots = {"hw": [], "sw": []}
        for kind in ("hw", "sw"):
            for i in range(n_dma_sems if kind == "hw" else n_sw):
                self.dslots[kind].append([len(self.sems), 0])
                self.sems.append(nc.alloc_semaphore(name=f"dsem_{kind}{i}"))
        self.dnext = {"hw": 0, "sw": 0}
        self.n_inst = 0
        self.n_wait = 0
        self._banks = None
        self._bank_next = 0

    def _deps(self, reads, writes):
        deps = {}
        for b in reads:
            for si, v in b.w.items():
                if v > deps.get(si, 0):
                    deps[si] = v
        for b in writes:
            for si, v in b.w.items():
                if v > deps.get(si, 0):
                    deps[si] = v
            for si, v in b.r.items():
                if v > deps.get(si, 0):
                    deps[si] = v
        return deps

    def _wait(self, e, deps):
        eng = self.E[e]
        seen = self.seen[e]
        pe_si = self.semidx["pe"]
        for si, v in deps.items():
            if e == "pe" and si == pe_si:
                continue
            if seen.get(si, 0) >= v:
                continue
            eng.wait_ge(self.sems[si], v)
            seen[si] = v
            self.n_wait += 1

    def _mark(self, tok, reads, writes):
        si, v = tok
        for b in reads:
            if v > b.r.get(si, 0):
                b.r[si] = v
        for b in writes:
            if v > b.w.get(si, 0):
                b.w[si] = v

    def op(self, e, fn, reads=(), writes=()):
        self._wait(e, self._deps(reads, writes))
        inst = fn(self.E[e])
        self.cnt[e] += 1
        si = self.semidx[e]
        inst.then_inc(self.sems[si], 1)
        self._mark((si, self.cnt[e]), reads, writes)
        self.n_inst += 1
        return inst

    def dma(self, q, out, in_, reads=(), writes=(), **kw):
        deps = self._deps(reads, writes)
        kind = "sw" if q == "pool" else "hw"
        slot = self.dslots[kind][self.dnext[kind]]
        self.dnext[kind] = (self.dnext[kind] + 1) % len(self.dslots[kind])
        si, target = slot
        if target > 0 and target > deps.get(si, 0):
            deps[si] = target
        self._wait(q, deps)
        inst = self.E[q].dma_start(out=out, in_=in_, **kw)
        inst.then_inc(self.sems[si], 16)
        slot[1] = target + 16
        self._mark((si, target + 16), reads, writes)
        self.n_inst += 1
        return inst

    def finish(self, q="sp"):
        deps = {}
        for kind in ("hw", "sw"):
            for si, target in self.dslots[kind]:
                if target > 0:
                    deps[si] = target
        for e, c in self.cnt.items():
            if c > 0:
                deps[self.semidx[e]] = c
        self._wait(q, deps)

    def open_scope(self, tag):
        from contextlib import ExitStack
        self._scope = ExitStack()
        self._tag = tag
        return self._scope

    def T(self, name, shape, dtype):
        scope = getattr(self, "_scope", None)
        full = f"{getattr(self, '_tag', '')}{name}"
        if scope is None:
            return self.nc.alloc_sbuf_tensor(full, shape, dtype)
        return scope.enter_context(self.nc.sbuf_tensor(full, shape, dtype))

    def S(self, name, shape, dtype):
        return self.nc.sbuf_tensor(f"{getattr(self, '_tag', '')}{name}", shape, dtype)

    def close_scope(self):
        self.barrier()
        self._scope.close()
        self._scope = None
        self._tag = ""

    def fill_reg(self, val):
        regs = self.__dict__.setdefault("_fill_regs", {})
        if val not in regs:
            regs[val] = self.nc.gpsimd.to_reg(val)
        return regs[val]

    def barrier(self):
        for q in self.E:
            self.finish(q)

    def init_banks(self):
        if self._banks is not None:
            return
        self._banks = []
        for i in range(8):
            t = self.nc.alloc_psum_tensor(f"bank{i}", [128, 512], F32)
            self._banks.append((t, Buf(f"bank{i}")))

    def bank(self):
        t, b = self._banks[self._bank_next]
        self._bank_next = (self._bank_next + 1) % 8
        return t, b


def _begin(nc, kb, tag):
    own = nc is None
    if own:
        nc = bass.Bass("TRN2", target_bir_lowering=False)
        kb = KB(nc)
        kb.init_banks()
    else:
        kb.open_scope(tag)
    return nc, kb, own


def _end(kb, own):
    if own:
        kb.finish("sp")
    else:
        kb.close_scope()


def _dram(nc, io, name, shape, dtype, kind):
    if io is not None:
        return io[name]
    return nc.dram_tensor(name, shape, dtype, kind=kind).ap()


def _bf16(a):
    return np.ascontiguousarray(a).astype(ml_dtypes.bfloat16)


class NormCtx:
    pass


def emit_identity(kb, nc, dtype=BF16, name="ident"):
    it = kb.T(name + "_i", [128, 128], I32)
    idt = kb.T(name, [128, 128], dtype)
    b = Buf(name)
    kb.op("pool", lambda e: e.iota(it[:], pattern=[[1, 128]], base=0, channel_multiplier=-1), writes=[b])
    kb.op("dve", lambda e: e.tensor_scalar(out=idt[:], in0=it[:], scalar1=0, scalar2=None, op0=ALU.is_equal),
          reads=[b], writes=[b])
    return idt, b


def emit_mod(kb, nc, c_ap, adaw_ap, adab_ap, n_vec, name):
    ncol = n_vec * 1024
    cT = kb.T(name + "_cT", [128, 8], F32)
    sT = kb.T(name + "_sT", [128, 8], F32)
    bT = kb.T(name + "_bT", [128, n_vec * 8], F32)
    modT = kb.T(name + "_modT", [128, n_vec * 8], F32)
    bc, bb, bm = Buf(), Buf(), Buf()
    with nc.allow_non_contiguous_dma(reason="tiny transposed vector loads"):
        kb.dma("sp", cT[:], c_ap.rearrange("(k p) -> p k", p=128), writes=[bc])
        kb.dma("sp", bT[:], adab_ap[0:ncol].rearrange("(j p) -> p j", p=128), writes=[bb])
    kb.op("act", lambda e: e.activation(out=sT[:], in_=cT[:], func=AF.Silu), reads=[bc], writes=[bc])
    ps, pb = kb.bank()
    ngrp = ncol // 256
    with kb.S(name + "_w0", [128, 8, 256], F32) as wt0, kb.S(name + "_w1", [128, 8, 256], F32) as wt1:
        wts = [wt0, wt1]
        wb = [Buf(), Buf()]
        for g in range(ngrp):
            wt, wbuf = wts[g % 2], wb[g % 2]
            kb.dma("sp", wt[:], adaw_ap[:, g * 256:(g + 1) * 256].rearrange("(k p) n -> p k n", p=128), writes=[wbuf])
            for j in range(2):
                col = g * 2 + j
                for k in range(8):
                    kb.op("pe", lambda e, wt=wt, j=j, k=k, col=col: e.matmul(
                        ps[:, col:col + 1], lhsT=wt[:, k, j * 128:(j + 1) * 128], rhs=sT[:, k:k + 1],
                        start=(k == 0), stop=(k == 7)), reads=[wbuf, bc], writes=[pb])
        kb.op("dve", lambda e: e.tensor_tensor(out=modT[:], in0=ps[:, 0:n_vec * 8], in1=bT[:], op=ALU.add),
              reads=[bb], writes=[bm, pb])
        kb.barrier()
    return modT, bm


def norm_slots(kb, n, name="ns"):
    return [dict(xn=kb.T(f"{name}_xn{i}", [128, D_MODEL], BF16), ss=kb.T(f"{name}_ss{i}", [128, 1], F32),
                 rstd=kb.T(f"{name}_rstd{i}", [128, 1], F32), b=Buf()) for i in range(n)]


def emit_norm_pre(kb, x_tile, xb, sl):
    xn, ss, rstd, sb = sl["xn"], sl["ss"], sl["rstd"], sl["b"]
    kb.op("act", lambda e: e.activation(out=xn[:], in_=x_tile, func=AF.Square, accum_out=ss[:]), reads=[xb], writes=[sb])
    kb.op("dve", lambda e: e.tensor_scalar(out=ss[:], in0=ss[:], scalar1=1.0 / D_MODEL, scalar2=EPS, op0=ALU.mult, op1=ALU.add), writes=[sb])
    kb.op("act", lambda e: e.activation(out=ss[:], in_=ss[:], func=AF.Sqrt), writes=[sb])
    kb.op("dve", lambda e: e.reciprocal(out=rstd[:], in_=ss[:]), writes=[sb])
    kb.op("dve", lambda e: e.tensor_scalar(out=xn[:], in0=x_tile, scalar1=rstd[:, 0:1], scalar2=None, op0=ALU.mult), reads=[xb], writes=[sb])


def emit_norm_post(kb, sl, hT_dst, hT_buf, A_ap, B_ap, ab_buf, ident, ident_buf):
    xn, sb = sl["xn"], sl["b"]
    ps, pb = kb.bank()
    psb = ps[:].bitcast(BF16)
    for k in range(8):
        kb.op("pe", lambda e, k=k: e.transpose(psb[:, k * 128:(k + 1) * 128], xn[:, k * 128:(k + 1) * 128], ident[:]),
              reads=[sb, ident_buf], writes=[pb])
    for k in range(8):
        kb.op("act", lambda e, k=k: e.activation(out=hT_dst[:, k, :], in_=psb[:, k * 128:(k + 1) * 128], func=AF.Identity,
                                                 scale=A_ap[:, k:k + 1], bias=B_ap[:, k:k + 1]), reads=[ab_buf], writes=[hT_buf, pb])


def emit_mod_AB(kb, nc, modT, bm, g_ap, sh_idx, sc_idx, name):
    gT = kb.T(name + "_gT", [128, 8], F32)
    A = kb.T(name + "_A", [128, 8], F32)
    Bt = kb.T(name + "_B", [128, 8], F32)
    b = Buf()
    with nc.allow_non_contiguous_dma(reason="tiny transposed vector loads"):
        kb.dma("sp", gT[:], g_ap.rearrange("(k p) -> p k", p=128), writes=[b])
    kb.op("dve", lambda e: e.scalar_tensor_tensor(out=A[:], in0=modT[:, sc_idx * 8:(sc_idx + 1) * 8], scalar=1.0,
                                                  in1=gT[:], op0=ALU.add, op1=ALU.mult),
          reads=[bm, b], writes=[b])
    kb.op("dve", lambda e: e.tensor_copy(out=Bt[:], in_=modT[:, sh_idx * 8:(sh_idx + 1) * 8]), reads=[bm], writes=[b])
    return A, Bt, b


A_FM_BF = 8
A_FM_F32 = 16
A_FM = A_FM_BF + A_FM_F32
A_TM0 = A_FM * 128
A_TM1 = A_TM0 + 288
A_NCOL = A_TM1 + 512


def build_phaseA(ntok=TOK, nc=None, kb=None, io=None, tag=""):
    nc, kb, own = _begin(nc, kb, tag)
    x = _dram(nc, io, "x", [ntok, D_MODEL], F32, "ExternalInput")
    cvec = _dram(nc, io, "c", [D_MODEL], F32, "ExternalInput")
    adaw = _dram(nc, io, "adaw", [D_MODEL, 2048], F32, "ExternalInput")
    adab = _dram(nc, io, "adab", [2048], F32, "ExternalInput")
    g1 = _dram(nc, io, "g1", [D_MODEL], F32, "ExternalInput")
    w = _dram(nc, io, "w", [D_MODEL, A_NCOL], F32, "ExternalInput")
    o_fm_bf = _dram(nc, io, "fm_bf", [A_FM_BF * 128, ntok], BF16, "ExternalOutput")
    o_fm_f32 = _dram(nc, io, "fm_f32", [A_FM_F32 * 128, ntok], F32, "ExternalOutput")
    o_tm_bf = _dram(nc, io, "tm_bf", [ntok, 256], BF16, "ExternalOutput")
    o_tm_f32 = _dram(nc, io, "tm_f32", [ntok, 544], F32, "ExternalOutput")

    ident, identb = emit_identity(kb, nc)
    modT, bm = emit_mod(kb, nc, cvec, adaw, adab, 2, "mod")
    A, Bt, abb = emit_mod_AB(kb, nc, modT, bm, g1, 0, 1, "n1")

    wsb = kb.T("wsb", [128, 8, A_NCOL], BF16)
    wbufs = [Buf(f"w{i}") for i in range((A_NCOL + 1023) // 1024)]
    for bi, c0 in enumerate(range(0, A_NCOL, 1024)):
        c1 = min(A_NCOL, c0 + 1024)
        for k in range(8):
            kb.dma("pool", wsb[:, k, c0:c1], w[k * 128:(k + 1) * 128, c0:c1], writes=[wbufs[bi]])

    def wdep(c0, c1):
        return [wbufs[i] for i in range(c0 // 1024, (c1 - 1) // 1024 + 1)]

    NB = ntok // 512
    xts = [kb.T(f"xt{i}", [128, D_MODEL], F32) for i in range(4)]
    xbs = [Buf() for _ in range(4)]
    hTs = [kb.T(f"hT{i}", [128, 8, 512], BF16) for i in range(2)]
    hbs = [Buf(), Buf()]
    nsl = norm_slots(kb, 4)

    def norm_pre(blk):
        for t in range(4):
            ti = blk * 4 + t
            kb.dma("sp", xts[t][:], x[ti * 128:(ti + 1) * 128, :], writes=[xbs[t]])
            emit_norm_pre(kb, xts[t][:], xbs[t], nsl[t])

    def norm_post(blk, t):
        emit_norm_post(kb, nsl[t], hTs[blk % 2][:, :, t * 128:(t + 1) * 128], hbs[blk % 2], A, Bt, abb, ident, identb)

    norm_pre(0)
    for t in range(4):
        norm_post(0, t)
    st_bf = [kb.T(f"stbf{i}", [128, A_FM_BF, 512], BF16) for i in range(2)]
    st_f32 = [kb.T(f"stf{i}", [128, 8, 512], F32) for i in range(2)]
    stfb = [Buf(), Buf()]
    st_tb = [kb.T(f"sttb{i}", [128, 4, 256], BF16) for i in range(2)]
    st_tf = [kb.T(f"sttf{i}", [128, 4, 544], F32) for i in range(2)]
    stb = [[Buf() for _ in range(4)] for _ in range(2)]
    ev = 0
    for blk in range(NB):
        hT, hb = hTs[blk % 2], hbs[blk % 2]
        nxt = blk + 1 < NB
        if nxt:
            norm_pre(blk + 1)
        sb_, stb_, stf_ = st_bf[blk % 2], st_tb[blk % 2], st_tf[blk % 2]
        b_bf, _unused, b_tb, b_tf = stb[blk % 2]
        for c in range(A_FM):
            ps, pb = kb.bank()
            for k in range(8):
                kb.op("pe", lambda e, c=c, k=k, ps=ps: e.matmul(ps[:], lhsT=wsb[:, k, c * 128:(c + 1) * 128], rhs=hT[:, k, :],
                                                         start=(k == 0), stop=(k == 7)), reads=wdep(c * 128, (c + 1) * 128) + [hb], writes=[pb])
            eng = "act" if ev % 2 == 0 else "dve"
            ev += 1
            if c < A_FM_BF:
                dst, db = sb_[:, c, :], b_bf
                scale = 0.125 if c < 4 else 1.0
            else:
                hf = (c - A_FM_BF) // 8
                dst, db = st_f32[hf][:, (c - A_FM_BF) % 8, :], stfb[hf]
                scale = 1.0
            if eng == "act":
                kb.op("act", lambda e, dst=dst, ps=ps, scale=scale: e.activation(out=dst, in_=ps[:], func=AF.Copy, scale=scale),
                      writes=[db, pb])
            else:
                kb.op("dve", lambda e, dst=dst, ps=ps, scale=scale: e.tensor_scalar(out=dst, in0=ps[:], scalar1=scale, scalar2=None,
                                                                           op0=ALU.mult), writes=[db, pb])
            if c >= A_FM_BF and (c - A_FM_BF) % 8 == 7:
                hf = (c - A_FM_BF) // 8
                kb.dma("sp", o_fm_f32[hf * 1024:(hf + 1) * 1024, blk * 512:(blk + 1) * 512].rearrange("(c p) t -> p c t", p=128),
                       st_f32[hf][:], reads=[stfb[hf]])
            if nxt and c % 6 == 5:
                norm_post(blk + 1, c // 6)
        kb.dma("sp", o_fm_bf[:, blk * 512:(blk + 1) * 512].rearrange("(c p) t -> p c t", p=128), sb_[:], reads=[b_bf])
        for t in range(4):
            ps, pb = kb.bank()
            for k in range(8):
                kb.op("pe", lambda e, k=k, t=t, ps=ps: e.matmul(ps[:, 0:288], lhsT=hT[:, k, t * 128:(t + 1) * 128], rhs=wsb[:, k, A_TM0:A_TM1],
                                                         start=(k == 0), stop=(k == 7)), reads=wdep(A_TM0, A_TM1) + [hb], writes=[pb])
            kb.op("dve", lambda e, t=t, ps=ps: e.tensor_copy(out=stb_[:, t, :], in_=ps[:, 0:256]), writes=[b_tb, pb])
            kb.op("act", lambda e, t=t, ps=ps: e.activation(out=stf_[:, t, 0:24], in_=ps[:, 256:280], func=AF.Sigmoid),
                  writes=[b_tf, pb])
            kb.op("dve", lambda e, t=t, ps=ps: e.tensor_copy(out=stf_[:, t, 24:32], in_=ps[:, 280:288]), writes=[b_tf, pb])
            ps2, pb2 = kb.bank()
            for k in range(8):
                kb.op("pe", lambda e, k=k, t=t, ps2=ps2: e.matmul(ps2[:], lhsT=hT[:, k, t * 128:(t + 1) * 128], rhs=wsb[:, k, A_TM1:A_NCOL],
                                                           start=(k == 0), stop=(k == 7)), reads=wdep(A_TM1, A_NCOL) + [hb], writes=[pb2])
            kb.op("act", lambda e, t=t, ps2=ps2: e.activation(out=stf_[:, t, 32:544], in_=ps2[:], func=AF.Copy), writes=[b_tf, pb2])
        kb.dma("sp", o_tm_bf[blk * 512:(blk + 1) * 512, :].rearrange("(t p) c -> p t c", p=128), stb_[:], reads=[b_tb])
        kb.dma("sp", o_tm_f32[blk * 512:(blk + 1) * 512, :].rearrange("(t p) c -> p t c", p=128), stf_[:], reads=[b_tf])
    _end(kb, own)
    return nc, kb


def phaseA_weight_perm():
    off = {}
    o = 0
    for nme, n in (("nq", 512), ("kc", 128), ("vc", 128), ("ks", 128), ("vs", 128), ("kw", 128), ("vw", 128),
                   ("gate", 24), ("pool", 512), ("dqkv", 1536), ("dz", 512), ("dbeta", 4), ("da", 4), ("mg", 3072)):
        off[nme] = (o, n)
        o += n
    order = ["nq", "kc", "vc", "ks", "kw", "pool", "dqkv", "vs", "vw", "gate", "dbeta", "da", "dz"]
    idx = np.concatenate([np.arange(off[n][0], off[n][0] + off[n][1]) for n in order])
    assert idx.size == A_NCOL
    return idx, off


def emit_pos_rows(kb, nc, S, NCP, slopes_ap, nheads, put_c, put_k, put_q):
    slp = kb.T("slp", [1, nheads], F32)
    b_sl = Buf()
    kb.dma("sp", slp[:], slopes_ap, writes=[b_sl])
    CH = min(S, 2048)
    with kb.S("rowf", [1, 2, CH], F32) as rowf, kb.S("rowb", [1, 4, CH], BF16) as rowb, \
            kb.S("crow", [1, 2, NCP], F32) as crow, kb.S("crowb", [1, 2, NCP], BF16) as crowb, \
            kb.S("qrow", [1, 4, CH], BF16) as qrow:
        b_rf, b_rb, b_cb, b_qr = Buf(), Buf(), Buf(), Buf()
        kb.op("pool", lambda e: e.iota(crow[:, 0, :], pattern=[[128, NCP // 8], [0, 8]], base=0, channel_multiplier=0,
                                       allow_small_or_imprecise_dtypes=True), writes=[b_cb])
        kb.op("pool", lambda e: e.iota(crow[:, 1, :], pattern=[[0, NCP // 8], [16, 8]], base=31, channel_multiplier=0,
                                       allow_small_or_imprecise_dtypes=True), writes=[b_cb])
        kb.op("dve", lambda e: e.tensor_copy(out=crowb[:], in_=crow[:]), writes=[b_cb])
        for r in range(2):
            put_c(2 + r, crowb[:, r, :], [b_cb])
        kb.op("dve", lambda e: e.memset(crowb[:], 1.0), writes=[b_cb])
        for r in range(2):
            put_c(r, crowb[:, r, :], [b_cb])
        for c in range(S // CH):
            c0 = c * CH
            kb.op("pool", lambda e, c0=c0: e.iota(rowf[:, 0, :], pattern=[[128, CH // 128], [0, 128]], base=c0, channel_multiplier=0,
                                           allow_small_or_imprecise_dtypes=True), writes=[b_rf])
            kb.op("pool", lambda e: e.iota(rowf[:, 1, :], pattern=[[0, CH // 128], [1, 128]], base=0, channel_multiplier=0,
                                           allow_small_or_imprecise_dtypes=True), writes=[b_rf])
            kb.op("dve", lambda e: e.memset(rowb[:, 0:2, :], 1.0), writes=[b_rb])
            kb.op("dve", lambda e: e.tensor_copy(out=rowb[:, 2:4, :], in_=rowf[:, :, :]), reads=[b_rf], writes=[b_rb])
            for r in range(4):
                put_k(r, c0, CH, rowb[:, r, :], [b_rb])
            for h in range(nheads):
                kb.op("dve", lambda e, h=h: e.tensor_scalar(out=qrow[:, 0:2, :], in0=rowf[:, :, :], scalar1=slp[0:1, h:h + 1],
                                                     scalar2=-1.0, op0=ALU.mult, op1=ALU.mult),
                      reads=[b_rf, b_sl], writes=[b_qr])
                kb.op("dve", lambda e, h=h: e.tensor_scalar(out=qrow[:, 2:4, :], in0=rowb[:, 0:2, :], scalar1=slp[0:1, h:h + 1],
                                                     scalar2=None, op0=ALU.mult), reads=[b_rb, b_sl], writes=[b_qr])
                for r in range(4):
                    put_q(h, r, c0, CH, qrow[:, r, :], [b_qr])
        kb.barrier()


def build_nsa_rows(S, nc, kb, io, tag="rows_"):
    nc, kb, own = _begin(nc, kb, tag)
    rk, rq, rc = io["rows_k"], io["rows_q"], io["rows_c"]
    emit_pos_rows(kb, nc, S, S // 16, io["slopes"], 8,
                  lambda r, src, rd: kb.dma("sp", rc[r:r + 1, :], src, reads=rd),
                  lambda r, c0, CH, src, rd: kb.dma("sp", rk[r:r + 1, c0:c0 + CH], src, reads=rd),
                  lambda h, r, c0, CH, src, rd: kb.dma("sp", rq[h, r:r + 1, c0:c0 + CH], src, reads=rd))
    _end(kb, own)


NEG_BIG = -30000.0


def build_nsa(S=SEQ, nc=None, kb=None, io=None, tag=""):
    NQB = S // 512
    NT = S // 128
    NCP = S // 16
    NC_REAL = NCP - 1
    NNT = NCP // 128
    NSEL = S // 64
    VCW = 65 + NSEL
    nc, kb, own = _begin(nc, kb, tag)
    qT_d = _dram(nc, io, "qT", [4, 64, S], BF16, "ExternalInput")
    kcT_d = _dram(nc, io, "kcT", [64, S], BF16, "ExternalInput")
    vcT_d = _dram(nc, io, "vcT", [64, S], BF16, "ExternalInput")
    ksT_d = _dram(nc, io, "ksT", [64, S], BF16, "ExternalInput")
    kwT_d = _dram(nc, io, "kwT", [64, S], BF16, "ExternalInput")
    vs_d = _dram(nc, io, "vs", [S, 64], BF16, "ExternalInput")
    vw_d = _dram(nc, io, "vw", [S, 64], BF16, "ExternalInput")
    gate_d = _dram(nc, io, "gate", [S, 12], F32, "ExternalInput")
    phk1_d = _dram(nc, io, "phk1", [2048, 128], F32, "ExternalInput")
    phv1_d = _dram(nc, io, "phv1", [2048, 128], F32, "ExternalInput")
    phk2_d = _dram(nc, io, "phk2", [128, 64], F32, "ExternalInput")
    phv2_d = _dram(nc, io, "phv2", [128, 64], F32, "ExternalInput")
    posk_d = _dram(nc, io, "posk", [32, 64], F32, "ExternalInput")
    posv_d = _dram(nc, io, "posv", [32, 64], F32, "ExternalInput")
    slopes_d = _dram(nc, io, "slopes", [1, 4], F32, "ExternalInput")
    yaT_d = _dram(nc, io, "yaT", [256, S], BF16, "ExternalOutput")

    banks = kb._banks
    ident, identb = emit_identity(kb, nc)
    FILL0 = kb.fill_reg(0.0)
    FILLNEG = kb.fill_reg(-1e30)

    qTa = kb.T("qTa", [68, 4, S], BF16)
    ksTa = kb.T("ksTa", [68, S], BF16)
    kwTa = kb.T("kwTa", [68, S], BF16)
    kcTa = kb.T("kcTa", [68, NCP], BF16)
    VS = kb.T("VS", [128, NT, 65], BF16)
    VW = kb.T("VW", [128, NT, 65], BF16)
    VCa = kb.T("VCa", [128, NNT, VCW], BF16)
    Kind = kb.T("Kind", [128, S], BF16)
    gates = kb.T("gates", [128, NT, 12], F32)
    b_q, b_ks, b_kw, b_kc, b_vs, b_vw, b_vc, b_kind, b_g = (Buf() for _ in range(9))

    kb.dma("sp", qTa[0:64, :, :], qT_d.rearrange("h d s -> d h s"), writes=[b_q])
    kb.dma("sp", ksTa[0:64, :], ksT_d, writes=[b_ks])
    kb.dma("sp", kwTa[0:64, :], kwT_d, writes=[b_kw])
    kb.op("dve", lambda e: e.memset(VS[:, :, 64:65], 1.0), writes=[b_vs])
    kb.op("dve", lambda e: e.memset(VW[:, :, 64:65], 1.0), writes=[b_vw])
    kb.op("dve", lambda e: e.memset(VCa[:, :, 64:65], 1.0), writes=[b_vc])

    if io is not None and "rows_k" in io:
        kb.dma("sp", ksTa[64:68, :], io["rows_k"], writes=[b_ks])
        kb.dma("sp", kwTa[64:68, :], io["rows_k"], writes=[b_kw])
        kb.dma("sp", qTa[64:68, :, :], io["rows_q"].rearrange("h r s -> r h s"), writes=[b_q])
        kb.dma("sp", kcTa[64:68, :], io["rows_c"], writes=[b_kc])
    else:
        def put_c(r, src, rd):
            kb.dma("sp", kcTa[64 + r:65 + r, :], src, reads=rd, writes=[b_kc])

        def put_k(r, c0, CH, src, rd):
            kb.dma("sp", ksTa[64 + r:65 + r, c0:c0 + CH], src, reads=rd, writes=[b_ks])
            kb.dma("sp", kwTa[64 + r:65 + r, c0:c0 + CH], src, reads=rd, writes=[b_kw])

        def put_q(h, r, c0, CH, src, rd):
            kb.dma("sp", qTa[64 + r:65 + r, h, c0:c0 + CH], src, reads=rd, writes=[b_q])

        emit_pos_rows(kb, nc, S, NCP, slopes_d, 4, put_c, put_k, put_q)

    kb.op("dve", lambda e: e.memset(Kind[:], 1.0), writes=[b_kind])
    kb.op("pool", lambda e: e.affine_select(out=Kind[:], in_=Kind[:], pattern=[[1, S]], compare_op=ALU.is_ge, fill=FILL0,
                                            base=0, channel_multiplier=-64), writes=[b_kind])
    kb.op("pool", lambda e: e.affine_select(out=Kind[:], in_=Kind[:], pattern=[[-1, S]], compare_op=ALU.is_ge, fill=FILL0,
                                            base=63, channel_multiplier=64), writes=[b_kind])
    ovA = kb.T("ovA", [128, NSEL], BF16)
    ovB = kb.T("ovB", [128, NSEL], BF16)
    if True:
        b_ov = Buf()
        for nt in range(NNT):
            kb.op("dve", lambda e: e.memset(ovA[:], 1.0), writes=[b_ov])
            kb.op("dve", lambda e: e.memset(ovB[:], 1.0), writes=[b_ov])
            for (t_, sgn, off) in ((ovA, 1, 1), (ovA, -1, 3), (ovB, 1, 0), (ovB, -1, 2)):
                kb.op("pool", lambda e, t_=t_, sgn=sgn, off=off, nt=nt: e.affine_select(
                    out=t_[:], in_=t_[:], pattern=[[-4 * sgn, NSEL]], compare_op=ALU.is_ge, fill=FILL0,
                    base=sgn * 128 * nt + off, channel_multiplier=sgn), writes=[b_ov])
            kb.op("dve", lambda e, nt=nt: e.tensor_tensor(out=VCa[:, nt, 65:VCW], in0=ovA[:], in1=ovB[:], op=ALU.add),
                  reads=[b_ov], writes=[b_vc])

    cps, cpb = banks[7]
    with kb.S("cin_k", [64, S], BF16) as cin_k, kb.S("cin_v", [64, S], BF16) as cin_v, \
            kb.S("w1_k", [64, 32, 128], BF16) as w1_k, kb.S("w1_v", [64, 32, 128], BF16) as w1_v, \
            kb.S("w2_k", [128, 64], BF16) as w2_k, kb.S("w2_v", [128, 64], BF16) as w2_v, \
            kb.S("posT_k", [64, 32], BF16) as posT_k, kb.S("posT_v", [64, 32], BF16) as posT_v, \
            kb.S("hidT_k", [128, NCP], BF16) as hidT_k, kb.S("hidT_v", [128, NCP], BF16) as hidT_v, \
            kb.S("cvec_k", [128, 1], F32) as cvec_k, kb.S("cvec_v", [128, 1], F32) as cvec_v:
        ctens = {"k": (cin_k, w1_k, w2_k, posT_k, hidT_k, cvec_k), "v": (cin_v, w1_v, w2_v, posT_v, hidT_v, cvec_v)}
        cbufs = {w_: tuple(Buf() for _ in range(6)) for w_ in ("k", "v")}
        for which in ("k", "v"):
            cin, w1, w2, posT, hidT, cvec = ctens[which]
            b_cin, b_w1, b_w2, b_pos, b_hid, b_cv = cbufs[which]
            src, p1, p2, pp = (kcT_d, phk1_d, phk2_d, posk_d) if which == "k" else (vcT_d, phv1_d, phv2_d, posv_d)
            kb.dma("sp", cin[:], src, writes=[b_cin])
            kb.dma("pool", w1[:], p1.rearrange("(l d) h -> d l h", d=64), writes=[b_w1])
            kb.dma("pool", w2[:], p2, writes=[b_w2])
            with nc.allow_non_contiguous_dma(reason="tiny transposed load"):
                kb.dma("pool", posT[:], pp.rearrange("l d -> d l"), writes=[b_pos])
        kb.dma("sp", VS[:, :, 0:64], vs_d.rearrange("(t p) d -> p t d", p=128), writes=[b_vs])
        kb.dma("sp", VW[:, :, 0:64], vw_d.rearrange("(t p) d -> p t d", p=128), writes=[b_vw])
        kb.dma("sp", gates[:], gate_d.rearrange("(t p) c -> p t c", p=128), writes=[b_g])
        for which in ("k", "v"):
            cin, w1, w2, posT, hidT, cvec = ctens[which]
            b_cin, b_w1, b_w2, b_pos, b_hid, b_cv = cbufs[which]
            cps, cpb = banks[7] if which == "k" else banks[5]
            cview = cin[:].rearrange("d (n s) -> d n s", s=16)
            for l in range(32):
                rhs = cview[:, 0:NC_REAL, l] if l < 16 else cview[:, 1:NC_REAL + 1, l - 16]
                kb.op("pe", lambda e, l=l, rhs=rhs: e.matmul(cps[:, 0:NC_REAL], lhsT=w1[:, l, :], rhs=rhs, start=(l == 0), stop=(l == 31)),
                      reads=[b_cin, b_w1], writes=[cpb])
            ps2, pb2 = banks[6] if which == "k" else banks[4]
            for l in range(32):
                kb.op("pe", lambda e, l=l: e.matmul(ps2[:, 0:1], lhsT=w1[:, l, :], rhs=posT[:, l:l + 1], start=(l == 0), stop=(l == 31)),
                      reads=[b_pos, b_w1], writes=[pb2])
            kb.op("dve", lambda e: e.tensor_copy(out=cvec[:], in_=ps2[:, 0:1]), writes=[b_cv, pb2])
            kb.op("dve", lambda e: e.memset(hidT[:, NC_REAL:NCP], 0.0), writes=[b_hid])
            kb.op("act", lambda e: e.activation(out=hidT[:, 0:NC_REAL], in_=cps[:, 0:NC_REAL], func=AF.Silu, bias=cvec[:, 0:1]),
                  reads=[b_cv], writes=[b_hid, cpb])
            if which == "k":
                kb.op("pe", lambda e: e.matmul(ps2[0:64, 0:NCP], lhsT=w2[:], rhs=hidT[:], start=True, stop=True),
                      reads=[b_w2, b_hid], writes=[pb2])
                kb.op("dve", lambda e: e.tensor_copy(out=kcTa[0:64, :], in_=ps2[0:64, 0:NCP]), writes=[b_kc, pb2])
            else:
                for nt in range(NNT):
                    kb.op("pe", lambda e, nt=nt: e.matmul(ps2[:, nt * 64:(nt + 1) * 64], lhsT=hidT[:, nt * 128:(nt + 1) * 128], rhs=w2[:],
                                                   start=True, stop=True), reads=[b_w2, b_hid], writes=[pb2])
                kb.op("dve", lambda e: e.tensor_copy(out=VCa[:, :, 0:64], in_=ps2[:, 0:NNT * 64].rearrange("p (n d) -> p n d", d=64)),
                      writes=[b_vc, pb2])
        kb.barrier()

    NPT = 12
    pTs = [kb.T(f"pT{i}", [128, 512], BF16) for i in range(NPT)]
    pTb = [Buf() for _ in range(NPT)]
    pti = [0]
    ya = kb.T("ya", [128, 4, 256], F32)
    yab = kb.T("yab", [128, 4, 256], BF16)
    imp = kb.T("imp", [128, 4, NSEL], F32)
    score = kb.T("score", [128, NSEL], F32)
    score2 = kb.T("score2", [128, NSEL], F32)
    m8 = kb.T("m8", [128, 16], F32)
    mneg = kb.T("mneg", [128, 128], BF16)
    mnegT = kb.T("mnegT", [128, 512], BF16)
    den = kb.T("den", [128, 4], F32)
    scl = kb.T("scl", [128, 4], F32)
    yst = [kb.T(f"yst{i}", [128, 2, 512], BF16) for i in range(2)]
    b_ya, b_yab, b_imp, b_sc, b_mn, b_mnT, b_den = (Buf() for _ in range(7))
    b_yst = [Buf(), Buf()]
    kb.op("dve", lambda e: e.memset(mneg[:], 0.0), writes=[b_mn])
    sb_i = [0]

    SB = [(0, 1, 7)]

    def s_bank():
        sb = SB[0]
        i = sb_i[0] % len(sb)
        sb_i[0] = (i + 1) % len(sb)
        return banks[sb[i]]

    LAG = 5
    pend = []

    def push_pv(fn):
        pend.append(fn)
        if len(pend) > LAG:
            pend.pop(0)()

    def flush_pv():
        while pend:
            pend.pop(0)()

    def next_pT():
        i = pti[0]
        pti[0] = (i + 1) % NPT
        return pTs[i], pTb[i]

    tmpfs = [kb.T(f"tmpf{i}", [128, 512], F32) for i in range(2)]
    tmpfb = [Buf(), Buf()]
    tfi = [0]

    def emit_exp(ps, pb, pT, ptb, c0, c1, clamp):
        if clamp:
            i = tfi[0]
            tfi[0] ^= 1
            tf, tfb = tmpfs[i], tmpfb[i]
            kb.op("dve", lambda e: e.tensor_scalar(out=tf[:, c0:c1], in0=ps[:, c0:c1], scalar1=60.0, scalar2=None, op0=ALU.min),
                  writes=[tfb, pb])
            kb.op("act", lambda e: e.activation(out=pT[:, c0:c1], in_=tf[:, c0:c1], func=AF.Exp), reads=[tfb], writes=[ptb])
        else:
            kb.op("act", lambda e: e.activation(out=pT[:, c0:c1], in_=ps[:, c0:c1], func=AF.Exp), writes=[ptb, pb])

    def evac_branch(ps, pb, width, h, qb, br, first):
        accs = ps
        for i, (a, ab) in enumerate(accs):
            kb.op("dve", lambda e, a=a, i=i: e.tensor_scalar(out=den[:, i:i + 1], in0=a[:, 64:65], scalar1=1e-30, scalar2=None,
                                                             op0=ALU.max), writes=[b_den, ab])
        kb.op("dve", lambda e: e.reciprocal(out=den[:], in_=den[:]), writes=[b_den])
        kb.op("dve", lambda e: e.tensor_tensor(out=scl[:], in0=den[:], in1=gates[:, qb * 4:(qb + 1) * 4, h * 3 + br], op=ALU.mult),
              reads=[b_g], writes=[b_den])
        for i, (a, ab) in enumerate(accs):
            dst = ya[:, i, h * 64:(h + 1) * 64]
            if first:
                kb.op("dve", lambda e, a=a, i=i, dst=dst: e.tensor_scalar(out=dst, in0=a[:, 0:64], scalar1=scl[:, i:i + 1], scalar2=None,
                                                                 op0=ALU.mult), reads=[b_den], writes=[b_ya, ab])
            else:
                kb.op("dve", lambda e, a=a, i=i, dst=dst: e.scalar_tensor_tensor(out=dst, in0=a[:, 0:64], scalar=scl[:, i:i + 1], in1=dst,
                                                                        op0=ALU.mult, op1=ALU.add), reads=[b_den], writes=[b_ya, ab])

    for qb in range(NQB):
        q0 = qb * 512
        SB[0] = (0, 1, 7)
        nts = list(range(0, min(NNT, (512 * qb + 480) // 2048 + 1)))
        for h in range(4):
            bx, bxb = banks[2]
            by, byb = banks[3]
            accs = [(bx[:, 0:VCW], bxb), (bx[:, VCW:2 * VCW], bxb), (by[:, 0:VCW], byb), (by[:, VCW:2 * VCW], byb)] if 2 * VCW <= 512 else None
            for ni, nt in enumerate(nts):
                ps, pb = s_bank()
                kb.op("pe", lambda e, ps=ps, nt=nt, h=h: e.matmul(ps[:], lhsT=kcTa[:, nt * 128:(nt + 1) * 128], rhs=qTa[:, h, q0:q0 + 512],
                                                           start=True, stop=True), reads=[b_kc, b_q], writes=[pb])
                pT, ptb = next_pT()
                emit_exp(ps, pb, pT, ptb, 0, 512, qb - 4 * nt <= 4)
                if qb - 4 * nt <= 4:
                    kb.op("pool", lambda e, pT=pT, nt=nt: e.affine_select(out=pT[:], in_=pT[:], pattern=[[1, 512]], compare_op=ALU.is_ge,
                                                                   fill=FILL0, base=512 * qb - 2048 * nt - 31, channel_multiplier=-16),
                          writes=[ptb])
                def pv_c(accs=accs, pT=pT, ptb=ptb, nt=nt, ni=ni, last=(ni == len(nts) - 1)):
                    for qs in range(4):
                        a, ab = accs[qs]
                        kb.op("pe", lambda e, a=a, qs=qs: e.matmul(
                            a, lhsT=pT[:, qs * 128:(qs + 1) * 128], rhs=VCa[:, nt, :], start=(ni == 0 and qs % 2 == 0),
                            stop=last, skip_group_check=True), reads=[ptb, b_vc], writes=[ab])
                push_pv(pv_c)
            def ev_c(accs=accs, h=h):
                evac_branch(accs, None, VCW, h, qb, 0, True)
                for qs in range(4):
                    a, ab = accs[qs]
                    if h == 0:
                        kb.op("dve", lambda e, a=a, qs=qs: e.tensor_scalar(out=imp[:, qs, :], in0=a[:, 65:VCW], scalar1=den[:, qs:qs + 1], scalar2=None,
                                                                    op0=ALU.mult), reads=[b_den], writes=[b_imp, ab])
                    else:
                        kb.op("dve", lambda e, a=a, qs=qs: e.scalar_tensor_tensor(out=imp[:, qs, :], in0=a[:, 65:VCW], scalar=den[:, qs:qs + 1],
                                                                           in1=imp[:, qs, :], op0=ALU.mult, op1=ALU.add),
                              reads=[b_den], writes=[b_imp, ab])
            push_pv(ev_c)
        flush_pv()
        tp, tpb = banks[6]
        tpv = tp[:].bitcast(BF16)
        for qs in range(4):
            T = 4 * qb + qs
            kb.op("pool", lambda e, qs=qs, T=T: e.affine_select(out=score[:], in_=imp[:, qs, :], pattern=[[-64, NSEL]], compare_op=ALU.is_ge,
                                                        fill=FILLNEG, base=128 * T, channel_multiplier=1), reads=[b_imp], writes=[b_sc])
            kb.op("dve", lambda e: e.tensor_scalar(out=score[:, 0:1], in0=score[:, 0:1], scalar1=1e4, scalar2=None, op0=ALU.add), writes=[b_sc])
            kb.op("dve", lambda e, T=T: e.tensor_scalar(out=score[0:64, 2 * T:2 * T + 1], in0=score[0:64, 2 * T:2 * T + 1], scalar1=1e4,
                                                  scalar2=None, op0=ALU.add), writes=[b_sc])
            kb.op("dve", lambda e, T=T: e.tensor_scalar(out=score[64:128, 2 * T + 1:2 * T + 2], in0=score[64:128, 2 * T + 1:2 * T + 2],
                                                  scalar1=1e4, scalar2=None, op0=ALU.add), writes=[b_sc])
            kb.op("dve", lambda e: e.max(out=m8[:, 0:8], in_=score[:]), writes=[b_sc])
            kb.op("dve", lambda e: e.match_replace(out=score2[:], in_to_replace=m8[:, 0:8], in_values=score[:], imm_value=-3e38), writes=[b_sc])
            kb.op("dve", lambda e: e.max(out=m8[:, 8:16], in_=score2[:]), writes=[b_sc])
            kb.op("dve", lambda e: e.tensor_scalar(out=mneg[:, 0:NSEL], in0=score[:], scalar1=m8[:, 15:16], scalar2=NEG_BIG,
                                                   op0=ALU.is_lt, op1=ALU.mult), reads=[b_sc], writes=[b_mn])
            kb.op("pe", lambda e, qs=qs: e.transpose(tpv[:, qs * 128:(qs + 1) * 128], mneg[:], ident[:]), reads=[b_mn, identb], writes=[tpb])
        kb.op("act", lambda e: e.activation(out=mnegT[:], in_=tpv[:, 0:512], func=AF.Copy), writes=[b_mnT, tpb])
        SB[0] = (0, 1, 7, 2, 3)
        for h in range(4):
            sbk, sbb = banks[4]
            wbk, wbb = banks[5]
            acc_s = [(sbk[:, i * 65:(i + 1) * 65], sbb) for i in range(4)]
            acc_w = [(wbk[:, i * 65:(i + 1) * 65], wbb) for i in range(4)]
            first_s = True
            for kt in range(0, 4 * qb + 4):
                a_ = max(0, kt - 4 * qb)
                c0 = 128 * a_
                ps, pb = s_bank()
                kb.op("pe", lambda e, ps=ps, kt=kt, h=h, c0=c0: e.matmul(ps[:, c0:512], lhsT=ksTa[:, kt * 128:(kt + 1) * 128],
                                                                  rhs=qTa[:, h, q0 + c0:q0 + 512], start=True, stop=False),
                      reads=[b_ks, b_q], writes=[pb])
                kb.op("pe", lambda e, ps=ps, kt=kt, c0=c0: e.matmul(ps[:, c0:512], lhsT=Kind[:, kt * 128:(kt + 1) * 128], rhs=mnegT[:, c0:512],
                                                             start=False, stop=True), reads=[b_kind, b_mnT], writes=[pb])
                pT, ptb = next_pT()
                emit_exp(ps, pb, pT, ptb, c0, 512, False)
                if kt >= 4 * qb:
                    kb.op("pool", lambda e, pT=pT, c0=c0: e.affine_select(out=pT[:, c0:c0 + 128], in_=pT[:, c0:c0 + 128], pattern=[[1, 128]],
                                                                   compare_op=ALU.is_ge, fill=FILL0, base=0, channel_multiplier=-1), writes=[ptb])
                def pv_s(acc_s=acc_s, pT=pT, ptb=ptb, kt=kt, a_=a_, fs=first_s):
                    for qs in range(a_, 4):
                        a, ab = acc_s[qs]
                        kb.op("pe", lambda e, a=a, qs=qs, st_=(fs and qs == a_): e.matmul(
                            a, lhsT=pT[:, qs * 128:(qs + 1) * 128], rhs=VS[:, kt, :], start=st_, stop=(kt == 4 * qb + qs),
                            skip_group_check=True), reads=[ptb, b_vs], writes=[ab])
                push_pv(pv_s)
                first_s = False
            first_w = True
            for kt in range(max(0, 4 * qb - 4), 4 * qb + 4):
                qlo = max(0, kt - 4 * qb)
                qhi = min(3, kt + 4 - 4 * qb)
                c0, c1 = 128 * qlo, 128 * (qhi + 1)
                ps, pb = s_bank()
                kb.op("pe", lambda e, ps=ps, kt=kt, h=h, c0=c0, c1=c1: e.matmul(ps[:, c0:c1], lhsT=kwTa[:, kt * 128:(kt + 1) * 128],
                                                                         rhs=qTa[:, h, q0 + c0:q0 + c1], start=True, stop=True),
                      reads=[b_kw, b_q], writes=[pb])
                pT, ptb = next_pT()
                emit_exp(ps, pb, pT, ptb, c0, c1, False)
                if kt >= 4 * qb:
                    kb.op("pool", lambda e, pT=pT, c0=c0: e.affine_select(out=pT[:, c0:c0 + 128], in_=pT[:, c0:c0 + 128], pattern=[[1, 128]],
                                                                   compare_op=ALU.is_ge, fill=FILL0, base=0, channel_multiplier=-1), writes=[ptb])
                if kt + 4 - 4 * qb <= 3:
                    ce = 128 * (kt + 4 - 4 * qb)
                    kb.op("pool", lambda e, pT=pT, ce=ce: e.affine_select(out=pT[:, ce:ce + 128], in_=pT[:, ce:ce + 128], pattern=[[-1, 128]],
                                                                   compare_op=ALU.is_ge, fill=FILL0, base=-1, channel_multiplier=1), writes=[ptb])
                def pv_w(acc_w=acc_w, pT=pT, ptb=ptb, kt=kt, qlo=qlo, qhi=qhi, fw=first_w):
                    for qs in range(qlo, qhi + 1):
                        a, ab = acc_w[qs]
                        kb.op("pe", lambda e, a=a, qs=qs, st_=(fw and qs == qlo): e.matmul(
                            a, lhsT=pT[:, qs * 128:(qs + 1) * 128], rhs=VW[:, kt, :], start=st_, stop=(kt == 4 * qb + qs),
                            skip_group_check=True), reads=[ptb, b_vw], writes=[ab])
                push_pv(pv_w)
                first_w = False
            def ev_sw(acc_s=acc_s, acc_w=acc_w, h=h):
                evac_branch(acc_s, None, 65, h, qb, 1, False)
                evac_branch(acc_w, None, 65, h, qb, 2, False)
            push_pv(ev_sw)
        flush_pv()
        kb.op("act", lambda e: e.activation(out=yab[:], in_=ya[:], func=AF.Copy), reads=[b_ya], writes=[b_yab])
        yst_, ystb = yst[qb % 2], b_yst[qb % 2]
        for f in range(2):
            tp, tpb = banks[6]
            tpv = tp[:].bitcast(BF16)
            for qs in range(4):
                kb.op("pe", lambda e, qs=qs, f=f, tpv=tpv: e.transpose(tpv[:, qs * 128:(qs + 1) * 128], yab[:, qs, f * 128:(f + 1) * 128], ident[:]),
                      reads=[b_yab, identb], writes=[tpb])
            kb.op("dve", lambda e, f=f, tpv=tpv, yst_=yst_: e.tensor_copy(out=yst_[:, f, :], in_=tpv[:, 0:512]), writes=[ystb, tpb])
        kb.dma("sp", yaT_d[:, q0:q0 + 512].rearrange("(f p) t -> p f t", p=128), yst_[:], reads=[ystb])
    _end(kb, own)
    return nc, kb


def build_dn(S=SEQ, nc=None, kb=None, io=None, tag=""):
    PREP_SLICE = 8
    NBL = S // 512
    NCH = S // 64
    nc, kb, own = _begin(nc, kb, tag)
    x_d = _dram(nc, io, "xq", [3, 256, S], F32, "ExternalInput")
    cw_d = _dram(nc, io, "convw", [4, 3, 256], F32, "ExternalInput")
    z_d = _dram(nc, io, "z", [S, 256], F32, "ExternalInput")
    bl_d = _dram(nc, io, "blog", [S, 2], F32, "ExternalInput")
    al_d = _dram(nc, io, "alog", [S, 2], F32, "ExternalInput")
    Alog_d = _dram(nc, io, "Alog", [1, 2], F32, "ExternalInput")
    dtb_d = _dram(nc, io, "dtb", [1, 2], F32, "ExternalInput")
    ng_d = _dram(nc, io, "ng", [1, 128], F32, "ExternalInput")
    yc_d = _dram(nc, io, "ycT", [256, S], BF16, "ExternalOutput")

    banks = kb._banks
    V = lambda fn, reads=(), writes=(): kb.op("dve", fn, reads, writes)
    A = lambda fn, reads=(), writes=(): kb.op("act", fn, reads, writes)
    P = lambda fn, reads=(), writes=(): kb.op("pe", fn, reads, writes)
    G = lambda fn, reads=(), writes=(): kb.op("pool", fn, reads, writes)
    FILL0 = kb.fill_reg(0.0)

    identF, b_idF = emit_identity(kb, nc, F32, "identF")
    identB, b_idB = emit_identity(kb, nc, BF16, "identB")
    cst = Buf("consts")
    U = kb.T("U", [64, 64], F32)
    negU = kb.T("negU", [64, 64], F32)
    ones64 = kb.T("ones64", [64, 128], F32)
    neg64 = kb.T("neg64", [64, 64], F32)
    ones128 = kb.T("ones128", [128, 128], F32)
    mask_sl = kb.T("mask_sl", [64, 8, 64], F32)
    mask_ui = kb.T("mask_ui", [64, 8, 64], F32)
    V(lambda e: e.memset(U[:], 1.0), writes=[cst])
    G(lambda e: e.affine_select(out=U[:], in_=U[:], pattern=[[1, 64]], compare_op=ALU.is_ge, fill=FILL0, base=0,
                                channel_multiplier=-1), writes=[cst])
    V(lambda e: e.tensor_scalar(out=negU[:], in0=U[:], scalar1=-1.0, scalar2=None, op0=ALU.mult), writes=[cst])
    V(lambda e: e.memset(ones64[:], 1.0), writes=[cst])
    V(lambda e: e.memset(neg64[:], -1.0), writes=[cst])
    V(lambda e: e.memset(ones128[:], 1.0), writes=[cst])
    V(lambda e: e.memset(mask_sl[:], 1.0), writes=[cst])
    V(lambda e: e.memset(mask_ui[:], 1.0), writes=[cst])
    G(lambda e: e.affine_select(out=mask_sl[:], in_=mask_sl[:], pattern=[[0, 8], [-1, 64]], compare_op=ALU.is_ge, fill=FILL0,
                                base=-1, channel_multiplier=1), writes=[cst])
    G(lambda e: e.affine_select(out=mask_ui[:], in_=mask_ui[:], pattern=[[0, 8], [1, 64]], compare_op=ALU.is_ge, fill=FILL0,
                                base=0, channel_multiplier=-1), writes=[cst])
    ones128r = kb.T("ones128r", [128, 128], F32)
    identR = kb.T("identR", [64, 64], F32)
    V(lambda e: e.tensor_copy(out=ones128r[:].bitcast(F32R), in_=ones128[:]), writes=[cst])
    V(lambda e: e.tensor_copy(out=identR[:].bitcast(F32R), in_=identF[0:64, 0:64]), reads=[b_idF], writes=[cst])
    U8 = U[:].unsqueeze(1).to_broadcast([64, 8, 64])
    I8 = identF[0:64, 0:64].unsqueeze(1).to_broadcast([64, 8, 64])

    cw = kb.T("cw", [128, 6, 4], F32)
    ngt = kb.T("ngt", [64, 128], F32)
    An = kb.T("An", [64, 2], F32)
    dtb = kb.T("dtb_sb", [64, 2], F32)
    b_par = Buf("par")
    with nc.allow_non_contiguous_dma(reason="tiny parameter loads"):
        for k in range(4):
            for w_ in range(3):
                kb.dma("sp", cw[:, 2 * w_:2 * w_ + 2, k], cw_d[k, w_].rearrange("(t p) -> p t", p=128), writes=[b_par])
        kb.dma("sp", ngt[:], ng_d.to_broadcast([64, 128]), writes=[b_par])
        kb.dma("sp", An[:], Alog_d.to_broadcast([64, 2]), writes=[b_par])
        kb.dma("sp", dtb[:], dtb_d.to_broadcast([64, 2]), writes=[b_par])
    A(lambda e: e.activation(out=An[:], in_=An[:], func=AF.Exp), writes=[b_par])
    V(lambda e: e.tensor_scalar(out=An[:], in0=An[:], scalar1=-1.0, scalar2=None, op0=ALU.mult), writes=[b_par])

    beta = kb.T("beta", [64, NCH, 2], F32)
    gl = kb.T("gl", [64, NCH, 2], F32)
    eA = kb.T("eA", [64, NCH, 2], F32)
    eB = kb.T("eB", [64, NCH, 2], F32)
    bA = kb.T("bA", [64, NCH, 2], F32)
    dec = kb.T("dec", [128, NCH, 2], F32)
    b_gt = Buf("gates")
    kb.dma("sp", beta[:], bl_d.rearrange("(c i) h -> i c h", i=64), writes=[b_gt])
    kb.dma("sp", gl[:], al_d.rearrange("(c i) h -> i c h", i=64), writes=[b_gt])
    A(lambda e: e.activation(out=beta[:], in_=beta[:], func=AF.Sigmoid), writes=[b_gt])
    V(lambda e: e.tensor_tensor(out=gl[:], in0=gl[:], in1=dtb[:].unsqueeze(1).to_broadcast([64, NCH, 2]), op=ALU.add),
      reads=[b_par], writes=[b_gt])
    A(lambda e: e.activation(out=gl[:], in_=gl[:], func=AF.Exp), writes=[b_gt])
    A(lambda e: e.activation(out=gl[:], in_=gl[:], func=AF.Ln, bias=1.0), writes=[b_gt])
    V(lambda e: e.tensor_tensor(out=gl[:], in0=gl[:], in1=An[:].unsqueeze(1).to_broadcast([64, NCH, 2]), op=ALU.mult),
      reads=[b_par], writes=[b_gt])
    glf = gl[:].rearrange("p c h -> p (c h)")
    NG = NCH * 2
    for c0 in range(0, NG, 512):
        c1 = min(NG, c0 + 512)
        ps, pb = banks[0]
        P(lambda e: e.matmul(ps[0:64, 0:c1 - c0], lhsT=U[:], rhs=glf[:, c0:c1], start=True, stop=True), reads=[cst, b_gt], writes=[pb])
        ps2, pb2 = banks[1]
        P(lambda e: e.matmul(ps2[0:64, 0:c1 - c0], lhsT=ones64[:, 0:64], rhs=glf[:, c0:c1], start=True, stop=True), reads=[cst, b_gt], writes=[pb2])
        ps3, pb3 = banks[2]
        P(lambda e: e.matmul(ps3[:, 0:c1 - c0], lhsT=ones64[:, :], rhs=glf[:, c0:c1], start=True, stop=True), reads=[cst, b_gt], writes=[pb3])
        eAf = eA[:].rearrange("p c h -> p (c h)")
        eBf = eB[:].rearrange("p c h -> p (c h)")
        decf = dec[:].rearrange("p c h -> p (c h)")
        A(lambda e: e.activation(out=eAf[:, c0:c1], in_=ps[0:64, 0:c1 - c0], func=AF.Exp), writes=[b_gt, pb])
        V(lambda e: e.tensor_copy(out=eBf[:, c0:c1], in_=ps[0:64, 0:c1 - c0]), writes=[b_gt, pb])
        V(lambda e: e.tensor_tensor(out=eBf[:, c0:c1], in0=ps2[0:64, 0:c1 - c0], in1=eBf[:, c0:c1], op=ALU.subtract), writes=[b_gt, pb2])
        A(lambda e: e.activation(out=eBf[:, c0:c1], in_=eBf[:, c0:c1], func=AF.Exp), writes=[b_gt])
        A(lambda e: e.activation(out=decf[:, c0:c1], in_=ps3[:, 0:c1 - c0], func=AF.Exp), writes=[b_gt, pb3])
    V(lambda e: e.tensor_tensor(out=bA[:], in0=beta[:], in1=eA[:], op=ALU.mult), writes=[b_gt])

    xin = [kb.T("xin0", [128, 6, 515], F32)] * 2
    b_xin = [Buf()] * 2
    cs = [kb.T(f"cs{i}", [128, 6, 512], F32) for i in range(2)]
    b_cs = [Buf(), Buf()]
    acc = kb.T("acc", [128, 512], F32)
    accp = kb.T("accp", [128, 512], F32)
    b_acc, b_accp = Buf(), Buf()
    sqt = kb.T("sqt", [128, 512], F32)
    rt = kb.T("rt", [128, 512], F32)
    b_sq, b_rt = Buf(), Buf()
    tmps = []
    for h in range(2):
        tmps.append(dict(
            G1=kb.T(f"G1_{h}", [64, 8, 64], F32), G2=kb.T(f"G2_{h}", [64, 8, 64], F32),
            Ls=kb.T(f"Ls_{h}", [64, 8, 64], F32), Lt=kb.T(f"Lt_{h}", [64, 8, 64], F32),
            KKL=kb.T(f"KKL_{h}", [64, 8, 64], F32), dgb=kb.T(f"dgb_{h}", [64, 8, 64], F32),
            nM0=kb.T(f"nM0_{h}", [64, 8, 64], F32), b_nM0=Buf(),
            Nk=[kb.T(f"Nk{i}_{h}", [64, 8, 64], F32) for i in range(2)],
            Mk=[kb.T(f"Mk{i}_{h}", [64, 8, 64], F32) for i in range(6)],
            b_G=Buf(), b_Ls=Buf(), b_Lt=Buf(), b_KKL=Buf(), b_dgb=Buf(), b_Nk=[Buf(), Buf()], b_Mk=[Buf() for _ in range(6)]))
    sets = []
    for i in range(4):
        st = dict(
            k_e=kb.T(f"k_e{i}", [64, 8, 128], F32), R=kb.T(f"R{i}", [64, 8, 256], F32),
            AqkT=kb.T(f"AqkT{i}", [64, 8, 64], F32), wT=kb.T(f"wT{i}", [128, 8, 64], F32),
            o_all=(kb.T(f"o_all{i}", [64, 8, 128], F32) if i < 2 else None),
            b_ke=Buf(), b_R=[Buf() for _ in range(4)], b_Aqk=Buf(), b_wT=Buf(), b_o=Buf())
        if i >= 2:
            st["o_all"], st["b_o"] = sets[i - 2]["o_all"], sets[i - 2]["b_o"]
        sets.append(st)
    Sst = [kb.T(f"Sst{h}", [128, 128], F32) for h in range(2)]
    b_S = [Buf(), Buf()]
    for h in range(2):
        V(lambda e, h=h: e.tensor_scalar(out=Sst[h][:].bitcast(F32R), in0=ones128[:], scalar1=0.0, scalar2=None, op0=ALU.mult), reads=[cst], writes=[b_S[h]])
    vnew = [kb.T(f"vnew{h}", [64, 128], F32) for h in range(2)]
    o1 = [kb.T(f"o1{h}", [64, 128], F32) for h in range(2)]
    b_vn = [Buf(), Buf()]
    b_o1 = [Buf(), Buf()]
    zt = kb.T("zt", [64, 8, 128], F32)
    ysq = kb.T("ysq", [64, 8, 128], F32)
    ybf = kb.T("ybf", [64, 8, 128], BF16)
    ssn = kb.T("ssn", [64, 8], F32)
    yst = [kb.T(f"ycst{i}", [128, 512], BF16) for i in range(2)]
    b_zt, b_ysq, b_ybf, b_ssn = Buf(), Buf(), Buf(), Buf()
    b_yst = [Buf(), Buf()]
    osti = [0]

    def bc(ap2, n):
        return ap2.unsqueeze(2).to_broadcast([64, 8, n])

    def prep_gen(blk):
        par = blk % 2
        xi, bxi = xin[par], b_xin[par]
        c_, bcs = cs[par], b_cs[par]
        for w_ in range(3):
            if blk == 0:
                kb.dma("sp", xi[:, 2 * w_:2 * w_ + 2, 3:515], x_d[w_, :, 0:512].rearrange("(t p) s -> p t s", p=128), writes=[bxi])
            else:
                kb.dma("sp", xi[:, 2 * w_:2 * w_ + 2, :], x_d[w_, :, blk * 512 - 3:blk * 512 + 512].rearrange("(t p) s -> p t s", p=128),
                       writes=[bxi])
        if blk == 0:
            V(lambda e: e.memset(xi[:, :, 0:3], 0.0), writes=[bxi])
        for t in range(6):
            eng, ac, bac = ("dve", acc, b_acc) if t % 2 == 0 else ("dve", accp, b_accp)
            kb.op(eng, lambda e, t=t, ac=ac: e.tensor_scalar(out=ac[:], in0=xi[:, t, 3:515], scalar1=cw[:, t, 3:4], scalar2=None, op0=ALU.mult),
                  reads=[bxi, b_par], writes=[bac])
            for k in (2, 1, 0):
                kb.op("dve", lambda e, t=t, k=k, ac=ac: e.scalar_tensor_tensor(out=ac[:], in0=xi[:, t, k:k + 512], scalar=cw[:, t, k:k + 1], in1=ac[:],
                                                                   op0=ALU.mult, op1=ALU.add), reads=[bxi, b_par], writes=[bac])
            A(lambda e, t=t, ac=ac: e.activation(out=c_[:, t, :].bitcast(F32R), in_=ac[:], func=AF.Silu), reads=[bac], writes=[bcs])
            yield
        for t in range(4):
            A(lambda e, t=t: e.activation(out=sqt[:].bitcast(F32R), in_=c_[:, t, :], func=AF.Square), reads=[bcs], writes=[b_sq])
            ps, pb = banks[2]
            P(lambda e: e.matmul(ps[:], lhsT=ones128r[:].bitcast(F32R), rhs=sqt[:].bitcast(F32R), start=True, stop=True), reads=[cst, b_sq], writes=[pb])
            A(lambda e: e.activation(out=rt[:], in_=ps[:], func=AF.Ln, bias=EPS), writes=[b_rt, pb])
            A(lambda e: e.activation(out=rt[:], in_=rt[:], func=AF.Exp, scale=-0.5), writes=[b_rt])
            sc = (128.0 ** -0.5) if t < 2 else 1.0
            V(lambda e, t=t, sc=sc: e.scalar_tensor_tensor(out=c_[:, t, :].bitcast(F32R), in0=c_[:, t, :], scalar=sc, in1=rt[:], op0=ALU.mult, op1=ALU.mult),
              reads=[b_rt], writes=[bcs])
            yield
        cs8 = slice(blk * 8, blk * 8 + 8)
        cx = []
        for h in range(2):
            st = sets[par * 2 + h]
            d_ = dict(tmps[h])
            d_.update(h=h, st=st, qT=c_[:, 0 + h, :], kT=c_[:, 2 + h, :], vT=c_[:, 4 + h, :],
                      g_b=gl[:, cs8, h], be_b=beta[:, cs8, h], eB_b=eB[:, cs8, h], bA_b=bA[:, cs8, h],
                      bN=banks[0 + 2 * h], bM=banks[1 + 2 * h])
            cx.append(d_)
        fl = lambda t: t[:].rearrange("p c j -> p (c j)")
        yield
        for X in cx:
            st = X["st"]
            k_e, R, bR = st["k_e"], st["R"], st["b_R"]
            for half in range(2):
                bk, bkb = banks[4 + half]
                hs = slice(half * 4, half * 4 + 4)
                bRh = [bR[2 * half], bR[2 * half + 1]]
                for c4 in range(4):
                    c = half * 4 + c4
                    P(lambda e, c=c, c4=c4, bk=bk, X=X: e.transpose(bk[0:64, c4 * 128:(c4 + 1) * 128], X["kT"][:, c * 64:(c + 1) * 64], identF[:]),
                      reads=[bcs, b_idF], writes=[bkb])
                bkv = bk[0:64, :].rearrange("p (c d) -> p c d", d=128)
                V(lambda e, bkv=bkv, hs=hs, X=X, k_e=k_e: e.tensor_tensor(out=k_e[:, hs, :].bitcast(F32R), in0=bkv, in1=X["eB_b"][:, hs].unsqueeze(2).to_broadcast([64, 4, 128]),
                                                                 op=ALU.mult), reads=[b_gt], writes=[st["b_ke"], bkb])
                V(lambda e, bkv=bkv, hs=hs, X=X, R=R: e.tensor_tensor(out=R[:, hs, 128:256].bitcast(F32R), in0=bkv, in1=X["bA_b"][:, hs].unsqueeze(2).to_broadcast([64, 4, 128]),
                                                               op=ALU.mult), reads=[b_gt], writes=bRh + [bkb])
                for c4 in range(4):
                    c = half * 4 + c4
                    P(lambda e, c=c, c4=c4, bk=bk, X=X: e.transpose(bk[0:64, c4 * 128:(c4 + 1) * 128], X["vT"][:, c * 64:(c + 1) * 64], identF[:]),
                      reads=[bcs, b_idF], writes=[bkb])
                V(lambda e, bkv=bkv, hs=hs, X=X, R=R: e.tensor_tensor(out=R[:, hs, 0:128].bitcast(F32R), in0=bkv, in1=X["be_b"][:, hs].unsqueeze(2).to_broadcast([64, 4, 128]),
                                                               op=ALU.mult), reads=[b_gt], writes=bRh + [bkb])
        yield
        for X in cx:
            (b0, b0b), (b1, b1b) = X["bN"], X["bM"]
            for c in range(8):
                P(lambda e, c=c, X=X, b0=b0: e.matmul(b0[0:64, c * 64:(c + 1) * 64], lhsT=X["kT"][:, c * 64:(c + 1) * 64], rhs=X["kT"][:, c * 64:(c + 1) * 64],
                                               start=True, stop=True), reads=[bcs], writes=[b0b])
            for c in range(8):
                P(lambda e, c=c, X=X, b1=b1: e.matmul(b1[0:64, c * 64:(c + 1) * 64], lhsT=X["kT"][:, c * 64:(c + 1) * 64], rhs=X["qT"][:, c * 64:(c + 1) * 64],
                                               start=True, stop=True), reads=[bcs], writes=[b1b])
        yield
        for X in cx:
            V(lambda e, X=X: e.tensor_copy(out=X["G1"][:], in_=bc(X["g_b"], 64)), reads=[b_gt], writes=[X["b_G"]])
            V(lambda e, X=X: e.tensor_tensor(out=X["G2"][:], in0=X["G1"][:], in1=U8, op=ALU.mult), reads=[cst], writes=[X["b_G"]])
        yield
        for X in cx:
            (b2, b2b), (b3, b3b) = banks[4], banks[5]
            G1f, G2f, Lsf, Ltf = fl(X["G1"]), fl(X["G2"]), fl(X["Ls"]), fl(X["Lt"])
            P(lambda e, G1f=G1f: e.matmul(b2[0:64, :], lhsT=U[:], rhs=G1f, start=True, stop=False), reads=[cst, X["b_G"]], writes=[b2b])
            P(lambda e, G2f=G2f: e.matmul(b2[0:64, :], lhsT=neg64[:], rhs=G2f, start=False, stop=True), reads=[cst, X["b_G"]], writes=[b2b])
            P(lambda e, G2f=G2f: e.matmul(b3[0:64, :], lhsT=ones64[:, 0:64], rhs=G2f, start=True, stop=False), reads=[cst, X["b_G"]], writes=[b3b])
            P(lambda e, G1f=G1f: e.matmul(b3[0:64, :], lhsT=negU[:], rhs=G1f, start=False, stop=True), reads=[cst, X["b_G"]], writes=[b3b])
            V(lambda e, Lsf=Lsf: e.tensor_scalar(out=Lsf, in0=b2[0:64, :], scalar1=0.0, scalar2=None, op0=ALU.min), writes=[X["b_Ls"], b2b])
            A(lambda e, Lsf=Lsf: e.activation(out=Lsf, in_=Lsf, func=AF.Exp), writes=[X["b_Ls"]])
            V(lambda e, X=X: e.tensor_tensor(out=X["Ls"][:], in0=X["Ls"][:], in1=mask_sl[:], op=ALU.mult), reads=[cst], writes=[X["b_Ls"]])
            V(lambda e, Ltf=Ltf: e.tensor_scalar(out=Ltf, in0=b3[0:64, :], scalar1=0.0, scalar2=None, op0=ALU.min), writes=[X["b_Lt"], b3b])
            A(lambda e, Ltf=Ltf: e.activation(out=Ltf, in_=Ltf, func=AF.Exp), writes=[X["b_Lt"]])
            V(lambda e, X=X: e.tensor_tensor(out=X["Lt"][:], in0=X["Lt"][:], in1=mask_ui[:], op=ALU.mult), reads=[cst], writes=[X["b_Lt"]])
        yield
        for X in cx:
            st = X["st"]
            (b0, b0b), (b1, b1b) = X["bN"], X["bM"]
            V(lambda e, X=X, b0=b0: e.tensor_tensor(out=fl(X["KKL"]), in0=b0[0:64, :], in1=fl(X["Ls"]), op=ALU.mult), reads=[X["b_Ls"]], writes=[X["b_KKL"], b0b])
            V(lambda e, X=X: e.tensor_tensor(out=X["Nk"][0][:].bitcast(F32R), in0=X["KKL"][:], in1=bc(X["be_b"], 64), op=ALU.mult),
              reads=[X["b_KKL"], b_gt], writes=[X["b_Nk"][0]])
            V(lambda e, X=X, b1=b1, st=st: e.tensor_tensor(out=fl(st["AqkT"]).bitcast(F32R), in0=b1[0:64, :], in1=fl(X["Lt"]), op=ALU.mult),
              reads=[X["b_Lt"]], writes=[st["b_Aqk"], b1b])
            V(lambda e, X=X: e.tensor_tensor(out=X["dgb"][:], in0=I8, in1=bc(X["be_b"], 64), op=ALU.mult), reads=[b_idF, b_gt], writes=[X["b_dgb"]])
        yield
        for X in cx:
            b1, b1b = X["bM"]
            for c in range(8):
                P(lambda e, c=c, X=X, b1=b1: e.matmul(b1[0:64, c * 64:(c + 1) * 64], lhsT=X["KKL"][:, c, :], rhs=X["dgb"][:, c, :], start=True, stop=True),
                  reads=[X["b_KKL"], X["b_dgb"]], writes=[b1b])
        yield
        for X in cx:
            b1, b1b = X["bM"]
            V(lambda e, X=X, b1=b1: e.tensor_copy(out=fl(X["Mk"][0]).bitcast(F32R), in_=b1[0:64, :]), writes=[X["b_Mk"][0], b1b])
            V(lambda e, X=X, b1=b1: e.tensor_scalar(out=fl(X["nM0"]).bitcast(F32R), in0=b1[0:64, :], scalar1=-1.0, scalar2=None, op0=ALU.mult),
              writes=[X["b_nM0"], b1b])
        for lev in range(1, 6):
            yield
            for X in cx:
                (b0, b0b), (b1, b1b) = X["bN"], X["bM"]
                Np, bNp = X["Nk"][(lev - 1) % 2], X["b_Nk"][(lev - 1) % 2]
                Mp, bMp = X["Mk"][lev - 1], X["b_Mk"][lev - 1]
                if lev < 5:
                    for c in range(8):
                        P(lambda e, c=c, Mp=Mp, Np=Np, b0=b0: e.matmul(b0[0:64, c * 64:(c + 1) * 64], lhsT=Mp[:, c, :].bitcast(F32R), rhs=Np[:, c, :].bitcast(F32R), start=True, stop=True),
                          reads=[bMp, bNp], writes=[b0b])
                for c in range(8):
                    P(lambda e, c=c, Mp=Mp, Np=Np, b1=b1: e.matmul(b1[0:64, c * 64:(c + 1) * 64], lhsT=Np[:, c, :].bitcast(F32R), rhs=Mp[:, c, :].bitcast(F32R), start=True, stop=True),
                      reads=[bMp, bNp], writes=[b1b])
            yield
            for X in cx:
                (b0, b0b), (b1, b1b) = X["bN"], X["bM"]
                Nn, bNn = X["Nk"][lev % 2], X["b_Nk"][lev % 2]
                if lev < 5:
                    A(lambda e, Nn=Nn, b0=b0: e.activation(out=fl(Nn).bitcast(F32R), in_=b0[0:64, :], func=AF.Copy), writes=[bNn, b0b])
                V(lambda e, X=X, lev=lev, b1=b1: e.tensor_copy(out=fl(X["Mk"][lev]).bitcast(F32R), in_=b1[0:64, :]), writes=[X["b_Mk"][lev], b1b])
        ai = 0
        for lev in (5, 4, 3, 2, 1, 0):
            for pr in range(4):
                yield
                for X in cx:
                    st = X["st"]
                    R, bRp = st["R"], st["b_R"][pr]
                    bk, bkb = banks[4 + ai % 2]
                    ai += 1
                    Ml, bMl = (X["Mk"][lev], X["b_Mk"][lev]) if lev > 0 else (X["nM0"], X["b_nM0"])
                    for c2 in range(2):
                        c = pr * 2 + c2
                        P(lambda e, c=c, c2=c2, bk=bk, R=R: e.matmul(bk[0:64, c2 * 256:(c2 + 1) * 256], lhsT=identR[:].bitcast(F32R), rhs=R[:, c, :].bitcast(F32R),
                                                              start=True, stop=False), reads=[cst, bRp], writes=[bkb])
                        P(lambda e, c=c, c2=c2, bk=bk, Ml=Ml, R=R: e.matmul(bk[0:64, c2 * 256:(c2 + 1) * 256], lhsT=Ml[:, c, :].bitcast(F32R), rhs=R[:, c, :].bitcast(F32R),
                                                                     start=False, stop=True), reads=[bMl, bRp], writes=[bkb])
                    Rv = R[:, pr * 2:pr * 2 + 2, :].rearrange("p c d -> p (c d)")
                    if ai % 3 == 0:
                        V(lambda e, Rv=Rv, bk=bk: e.tensor_copy(out=Rv.bitcast(F32R), in_=bk[0:64, :]), writes=[bRp, bkb])
                    else:
                        A(lambda e, Rv=Rv, bk=bk: e.activation(out=Rv.bitcast(F32R), in_=bk[0:64, :], func=AF.Copy), writes=[bRp, bkb])
        yield
        for X in cx:
            st = X["st"]
            b2, b2b = X["bN"]
            for c in range(8):
                P(lambda e, c=c, st=st, b2=b2: e.transpose(b2[:, c * 64:(c + 1) * 64], st["R"][:, c, 128:256], identF[0:64, 0:64]),
                  reads=[st["b_R"][c // 2], b_idF], writes=[b2b])
            A(lambda e, st=st, b2=b2: e.activation(out=st["wT"][:].rearrange("p c i -> p (c i)").bitcast(F32R), in_=b2[:, :], func=AF.Copy), writes=[st["b_wT"], b2b])
    def scan_step(blk, c):
        par = blk % 2
        c_, bcs = cs[par], b_cs[par]
        if True:
            n = blk * 8 + c
            for h in range(2):
                st = sets[par * 2 + h]
                qT = c_[:, 0 + h, :]
                sb, sbb = banks[6 + h]
                P(lambda e, c=c, st=st, h=h, sb=sb: e.matmul(sb[0:64, 0:128], lhsT=st["wT"][:, c, :].bitcast(F32R), rhs=Sst[h][:].bitcast(F32R), start=True, stop=True),
                  reads=[st["b_wT"], b_S[h]], writes=[sbb])
                P(lambda e, c=c, qT=qT, h=h, sb=sb: e.matmul(sb[0:64, 128:256], lhsT=qT[:, c * 64:(c + 1) * 64].bitcast(F32R), rhs=Sst[h][:].bitcast(F32R), start=True, stop=True),
                  reads=[bcs, b_S[h]], writes=[sbb])
                V(lambda e, c=c, st=st, h=h, sb=sb: e.tensor_tensor(out=vnew[h][:].bitcast(F32R), in0=st["R"][:, c, 0:128], in1=sb[0:64, 0:128], op=ALU.subtract),
                  reads=[st["b_R"][c // 2]], writes=[b_vn[h], sbb])
                V(lambda e, n=n, h=h, sb=sb: e.tensor_scalar(out=o1[h][:], in0=sb[0:64, 128:256], scalar1=eA[:, n, h:h + 1], scalar2=None, op0=ALU.mult),
                  reads=[b_gt], writes=[b_o1[h], sbb])
                P(lambda e, c=c, st=st, h=h, sb=sb: e.matmul(sb[0:64, 256:384], lhsT=st["AqkT"][:, c, :].bitcast(F32R), rhs=vnew[h][:].bitcast(F32R), start=True, stop=True),
                  reads=[st["b_Aqk"], b_vn[h]], writes=[sbb])
                P(lambda e, c=c, st=st, h=h, sb=sb: e.matmul(sb[:, 384:512], lhsT=st["k_e"][:, c, :].bitcast(F32R), rhs=vnew[h][:].bitcast(F32R), start=True, stop=True),
                  reads=[st["b_ke"], b_vn[h]], writes=[sbb])
                V(lambda e, c=c, st=st, h=h, sb=sb: e.tensor_tensor(out=st["o_all"][:, c, :], in0=o1[h][:], in1=sb[0:64, 256:384], op=ALU.add),
                  reads=[b_o1[h]], writes=[st["b_o"], sbb])
                V(lambda e, n=n, h=h, sb=sb: e.scalar_tensor_tensor(out=Sst[h][:].bitcast(F32R), in0=Sst[h][:], scalar=dec[:, n, h:h + 1], in1=sb[:, 384:512],
                                                             op0=ALU.mult, op1=ALU.add), reads=[b_gt], writes=[b_S[h], sbb])
    def output_stage(blk):
        par = blk % 2
        for h in range(2):
            st = sets[par * 2 + h]
            o_all = st["o_all"]
            kb.dma("sp", zt[:], z_d[blk * 512:(blk + 1) * 512, h * 128:(h + 1) * 128].rearrange("(c i) d -> i c d", i=64), writes=[b_zt])
            A(lambda e: e.activation(out=zt[:], in_=zt[:], func=AF.Silu), writes=[b_zt])
            V(lambda e, o_all=o_all: e.tensor_tensor(out=ysq[:], in0=o_all[:], in1=o_all[:], op=ALU.mult), reads=[st["b_o"]], writes=[b_ysq])
            V(lambda e: e.tensor_reduce(out=ssn[:], in_=ysq[:], axis=AX.X, op=ALU.add), reads=[b_ysq], writes=[b_ssn])
            V(lambda e: e.tensor_scalar(out=ssn[:], in0=ssn[:], scalar1=1.0 / 128.0, scalar2=EPS, op0=ALU.mult, op1=ALU.add), writes=[b_ssn])
            A(lambda e: e.activation(out=ssn[:], in_=ssn[:], func=AF.Sqrt), writes=[b_ssn])
            V(lambda e: e.reciprocal(out=ssn[:], in_=ssn[:]), writes=[b_ssn])
            V(lambda e, o_all=o_all: e.tensor_tensor(out=ysq[:], in0=o_all[:], in1=bc(ssn[:, :], 128), op=ALU.mult), reads=[st["b_o"], b_ssn], writes=[b_ysq])
            V(lambda e: e.tensor_tensor(out=ysq[:], in0=ysq[:], in1=ngt[:].unsqueeze(1).to_broadcast([64, 8, 128]), op=ALU.mult),
              reads=[b_par], writes=[b_ysq])
            V(lambda e: e.tensor_tensor(out=ybf[:], in0=ysq[:], in1=zt[:], op=ALU.mult), reads=[b_ysq, b_zt], writes=[b_ybf])
            b3, b3b = banks[3]
            b3v = b3[:].bitcast(BF16)
            for c in range(8):
                P(lambda e, c=c: e.transpose(b3v[:, c * 64:(c + 1) * 64], ybf[:, c, :], identB[0:64, 0:64]), reads=[b_ybf, b_idB], writes=[b3b])
            oi = osti[0]
            osti[0] ^= 1
            A(lambda e, oi=oi: e.activation(out=yst[oi][:], in_=b3v[:, 0:512], func=AF.Copy), writes=[b_yst[oi], b3b])
            kb.dma("sp", yc_d[h * 128:(h + 1) * 128, blk * 512:(blk + 1) * 512], yst[oi][:], reads=[b_yst[oi]])
    for _ in prep_gen(0):
        pass
    for blk in range(NBL):
        g = prep_gen(blk + 1) if blk + 1 < NBL else None
        for c in range(8):
            scan_step(blk, c)
            if g is not None:
                for _ in range(PREP_SLICE):
                    if next(g, "done") == "done":
                        g = None
                        break
        if g is not None:
            for _ in g:
                pass
        output_stage(blk)
    _end(kb, own)
    return nc, kb


def build_pool(ntok=TOK, nc=None, kb=None, io=None, tag="", t0=None):
    W = 16 + ntok
    nc, kb, own = _begin(nc, kb, tag)
    p_d = _dram(nc, io, "pT", [512, W], F32, "ExternalInput")
    val_d = _dram(nc, io, "valid", [128, W], F32, "ExternalInput") if t0 is None else None
    pw_d = _dram(nc, io, "pw", [4, 128, 128], F32, "ExternalInput")
    psc_d = _dram(nc, io, "psc", [512], F32, "ExternalInput")
    yb_d = _dram(nc, io, "ybT", [512, ntok], BF16, "ExternalOutput")
    V = lambda fn, reads=(), writes=(): kb.op("dve", fn, reads, writes)
    pw = kb.T("pw_sb", [128, 4, 128], BF16)
    psc = kb.T("psc_sb", [128, 4], F32)
    b_par = Buf()
    for g in range(4):
        kb.dma("pool", pw[:, g, :], pw_d[g], writes=[b_par])
    with nc.allow_non_contiguous_dma(reason="tiny"):
        kb.dma("sp", psc[:], psc_d.rearrange("(g p) -> p g", p=128), writes=[b_par])
    vs = [kb.T(f"vs{i}", [128, W], F32) for i in range(2)]
    rc = kb.T("rc", [128, W], F32)
    xs = kb.T("xs", [128, W], F32)
    sa = [kb.T(f"sa{i}", [128, W], F32) for i in range(2)]
    yb = kb.T("yb", [128, ntok], BF16)
    ost = [kb.T(f"ost{i}", [128, 512], BF16) for i in range(2)]
    b_vs, b_rc, b_x, b_s, b_y = Buf(), Buf(), Buf(), Buf(), Buf()
    b_ost = [Buf(), Buf()]
    if t0 is None:
        kb.dma("sp", vs[0][:], val_d, writes=[b_vs])
    else:
        V(lambda e: e.memset(vs[0][:], 1.0), writes=[b_vs])
        if t0 == 0:
            V(lambda e: e.memset(vs[0][:, 0:16], 0.0), writes=[b_vs])
    vcur = 0
    oi = 0
    for g in range(4):
        w = 2 << g
        if t0 is None:
            kb.dma("sp", xs[:], p_d[g * 128:(g + 1) * 128, :], writes=[b_x])
        elif t0 == 0:
            V(lambda e: e.memset(xs[:, 0:16], 0.0), writes=[b_x])
            kb.dma("sp", xs[:, 16:W], p_d[g * 128:(g + 1) * 128, 0:ntok], writes=[b_x])
        else:
            kb.dma("sp", xs[:], p_d[g * 128:(g + 1) * 128, t0 - 16:t0 + ntok], writes=[b_x])
        step = w // 2
        vn = 1 - vcur
        kb.op("pool", lambda e, step=step, vn=vn, vcur=vcur: e.tensor_tensor(out=vs[vn][:, step:W], in0=vs[vcur][:, step:W], in1=vs[vcur][:, 0:W - step],
                                                                     op=ALU.add), writes=[b_vs])
        kb.op("pool", lambda e, step=step, vn=vn, vcur=vcur: e.tensor_copy(out=vs[vn][:, 0:step], in_=vs[vcur][:, 0:step]), writes=[b_vs])
        vcur = vn
        V(lambda e, w=w: e.memset(rc[:, 32:W], 1.0 / w), writes=[b_rc])
        V(lambda e, vcur=vcur: e.reciprocal(out=rc[:, 16:32], in_=vs[vcur][:, 16:32]), reads=[b_vs], writes=[b_rc])
        src = xs
        si = 0
        st_ = 1
        while st_ < w:
            dst = sa[si]
            V(lambda e, st_=st_, src=src, dst=dst: e.tensor_tensor(out=dst[:, st_:W], in0=src[:, st_:W], in1=src[:, 0:W - st_], op=ALU.add),
              reads=[b_x], writes=[b_s])
            V(lambda e, st_=st_, src=src, dst=dst: e.tensor_copy(out=dst[:, 0:st_], in_=src[:, 0:st_]), reads=[b_x], writes=[b_s])
            src = dst
            si ^= 1
            st_ *= 2
        V(lambda e, src=src: e.tensor_tensor(out=src[:, 16:W], in0=src[:, 16:W], in1=rc[:, 16:W], op=ALU.mult), reads=[b_rc], writes=[b_s])
        V(lambda e, src=src: e.tensor_tensor(out=yb[:], in0=src[:, 16:W], in1=xs[:, 16:W], op=ALU.subtract), reads=[b_x, b_s], writes=[b_y])
        for c in range(ntok // 512):
            ps, pb = kb.bank()
            kb.op("pe", lambda e, c=c, g=g, ps=ps: e.matmul(ps[:], lhsT=pw[:, g, :], rhs=yb[:, c * 512:(c + 1) * 512], start=True, stop=True),
                  reads=[b_par, b_y], writes=[pb])
            o_, ob = ost[oi], b_ost[oi]
            oi ^= 1
            kb.op("act", lambda e, ps=ps, o_=o_, g=g: e.activation(out=o_[:], in_=ps[:], func=AF.Copy, scale=psc[:, g:g + 1]),
                  reads=[b_par], writes=[ob, pb])
            kb.dma("sp", yb_d[g * 128:(g + 1) * 128, c * 512:(c + 1) * 512], o_[:], reads=[ob])
    _end(kb, own)
    return nc, kb


def emit_mod_rows(kb, nc, c_ap, adaw_ap, adab_ap, col0, ncol, name):
    row = kb.T(name + "_row", [128, ncol], F32)
    b = Buf()
    with kb.S(name + "_cT", [128, 8], F32) as cT, kb.S(name + "_sTb", [128, 8, 128], F32) as sTb, kb.S(name + "_bb", [128, ncol], F32) as bb, \
            kb.S(name + "_w0", [128, 8, 256], F32) as wt0, kb.S(name + "_w1", [128, 8, 256], F32) as wt1:
        with nc.allow_non_contiguous_dma(reason="tiny"):
            kb.dma("sp", cT[:], c_ap.rearrange("(k p) -> p k", p=128), writes=[b])
            kb.dma("sp", bb[:], adab_ap[col0:col0 + ncol].unsqueeze(0).to_broadcast([128, ncol]), writes=[b])
        kb.op("act", lambda e: e.activation(out=cT[:], in_=cT[:], func=AF.Silu), writes=[b])
        kb.op("dve", lambda e: e.tensor_copy(out=sTb[:], in_=cT[:].unsqueeze(2).to_broadcast([128, 8, 128])), writes=[b])
        wts = [wt0, wt1]
        wb = [Buf(), Buf()]
        for g in range(ncol // 256):
            wt, wbuf = wts[g % 2], wb[g % 2]
            kb.dma("sp", wt[:], adaw_ap[:, col0 + g * 256:col0 + (g + 1) * 256].rearrange("(k p) n -> p k n", p=128), writes=[wbuf])
            ps, pb = kb.bank()
            for k in range(8):
                kb.op("pe", lambda e, wt=wt, k=k, ps=ps: e.matmul(ps[:, 0:256], lhsT=sTb[:, k, :], rhs=wt[:, k, :], start=(k == 0), stop=(k == 7)),
                      reads=[wbuf, b], writes=[pb])
            kb.op("dve", lambda e, g=g, ps=ps: e.tensor_tensor(out=row[:, g * 256:(g + 1) * 256], in0=ps[:, 0:256], in1=bb[:, g * 256:(g + 1) * 256], op=ALU.add),
                  writes=[b, pb])
        kb.barrier()
    return row, b


def build_c1(ntok=TOK, nc=None, kb=None, io=None, tag=""):
    nc, kb, own = _begin(nc, kb, tag)
    x = _dram(nc, io, "x", [ntok, D_MODEL], F32, "ExternalInput")
    cvec = _dram(nc, io, "c", [D_MODEL], F32, "ExternalInput")
    adaw = _dram(nc, io, "adaw", [D_MODEL, 3072], F32, "ExternalInput")
    adab = _dram(nc, io, "adab", [3072], F32, "ExternalInput")
    g1 = _dram(nc, io, "g1", [D_MODEL], F32, "ExternalInput")
    wmg_d = _dram(nc, io, "wmg", [D_MODEL, 3072], F32, "ExternalInput")
    wbr_d = _dram(nc, io, "wbr", [3, 512, D_MODEL], F32, "ExternalInput")
    wout_d = _dram(nc, io, "wout", [D_MODEL, D_MODEL], F32, "ExternalInput")
    yT_d = _dram(nc, io, "yT", [3, 512, ntok], BF16, "ExternalInput")
    x1_d = _dram(nc, io, "x1", [ntok, D_MODEL], F32, "ExternalOutput")

    ident, identb = emit_identity(kb, nc)
    modT, bm = emit_mod(kb, nc, cvec, adaw, adab, 2, "mod")
    A, Bt, abb = emit_mod_AB(kb, nc, modT, bm, g1, 0, 1, "n1")
    gt_bc, b_gt = emit_mod_rows(kb, nc, cvec, adaw, adab, 2048, 1024, "gt")

    wmg = kb.T("wmg_sb", [128, 8, 3072], BF16)
    wbr = kb.T("wbr_sb", [128, 3, 4, 1024], BF16)
    wout = kb.T("wout_sb", [128, 8, 1024], BF16)
    b_wmg, b_wbr, b_wout = Buf("wmg"), Buf("wbr"), Buf("wout")
    for k in range(8):
        for c0 in range(0, 3072, 1024):
            kb.dma("pool", wmg[:, k, c0:c0 + 1024], wmg_d[k * 128:(k + 1) * 128, c0:c0 + 1024], writes=[b_wmg])
    for br in range(3):
        for k in range(4):
            kb.dma("pool", wbr[:, br, k, :], wbr_d[br, k * 128:(k + 1) * 128, :], writes=[b_wbr])
    for k in range(8):
        kb.dma("pool", wout[:, k, :], wout_d[k * 128:(k + 1) * 128, :], writes=[b_wout])

    NB = ntok // 512
    xst = [kb.T(f"xst{i}", [128, D_MODEL], F32) for i in range(4)]
    b_xst = [Buf() for _ in range(4)]
    xr = [kb.T(f"xr{i}", [128, D_MODEL], F32) for i in range(2)]
    b_xr = [Buf(), Buf()]
    nsl = norm_slots(kb, 4)
    hTs = [kb.T(f"hT{i}", [128, 8, 512], BF16) for i in range(2)]
    hbs = [Buf(), Buf()]
    yblk = [kb.T(f"yblk{i}", [128, 3, 4, 512], BF16) for i in range(1)] * 2
    b_y = [Buf()] * 2
    gsb = [kb.T(f"gsb{i}", [128, 512], F32) for i in range(2)]
    b_g = [Buf(), Buf()]
    tmp = [kb.T(f"mtmp{i}", [128, 512], F32) for i in range(2)]
    b_tmp = [Buf(), Buf()]
    macc = kb.T("macc", [128, 512], F32)
    b_macc = Buf()
    mT = kb.T("mT", [128, 8, 512], BF16)
    b_mT = Buf()
    ot = [kb.T(f"ot{i}", [128, D_MODEL], F32) for i in range(2)]
    b_ot = [Buf(), Buf()]
    gi = 0
    oi = 0

    def norm_pre(blk):
        for t in range(4):
            ti = blk * 4 + t
            kb.dma("sp", xst[t][:], x[ti * 128:(ti + 1) * 128, :], writes=[b_xst[t]])
            emit_norm_pre(kb, xst[t][:], b_xst[t], nsl[t])

    def norm_post(blk, t):
        emit_norm_post(kb, nsl[t], hTs[blk % 2][:, :, t * 128:(t + 1) * 128], hbs[blk % 2], A, Bt, abb, ident, identb)

    norm_pre(0)
    for t in range(4):
        norm_post(0, t)
    for blk in range(NB):
        hT, hb = hTs[blk % 2], hbs[blk % 2]
        yb_, byb = yblk[blk % 2], b_y[blk % 2]
        nxt = blk + 1 < NB
        for br in range(3):
            kb.dma("sp", yb_[:, br, :, :], yT_d[br, :, blk * 512:(blk + 1) * 512].rearrange("(k p) t -> p k t", p=128), writes=[byb])
        if nxt:
            norm_pre(blk + 1)
        for oc in range(8):
            for br in range(3):
                ps, pb = kb.bank()
                for k in range(8):
                    kb.op("pe", lambda e, k=k, br=br, oc=oc, ps=ps: e.matmul(ps[:], lhsT=wmg[:, k, br * 1024 + oc * 128:br * 1024 + (oc + 1) * 128],
                                                                     rhs=hT[:, k, :], start=(k == 0), stop=(k == 7)), reads=[b_wmg, hb], writes=[pb])
                g_, bg_ = gsb[gi % 2], b_g[gi % 2]
                t_, bt_ = tmp[gi % 2], b_tmp[gi % 2]
                gi += 1
                kb.op("act", lambda e, ps=ps, g_=g_: e.activation(out=g_[:], in_=ps[:], func=AF.Sigmoid), writes=[bg_, pb])
                ps2, pb2 = kb.bank()
                for k in range(4):
                    kb.op("pe", lambda e, k=k, br=br, oc=oc, ps2=ps2: e.matmul(ps2[:], lhsT=wbr[:, br, k, oc * 128:(oc + 1) * 128], rhs=yb_[:, br, k, :],
                                                                       start=(k == 0), stop=(k == 3)), reads=[b_wbr, byb], writes=[pb2])
                if br == 0:
                    kb.op("dve", lambda e, ps2=ps2, g_=g_: e.tensor_tensor(out=macc[:], in0=ps2[:], in1=g_[:], op=ALU.mult),
                          reads=[bg_], writes=[b_macc, pb2])
                else:
                    kb.op("dve", lambda e, ps2=ps2, g_=g_, t_=t_: e.tensor_tensor(out=t_[:], in0=ps2[:], in1=g_[:], op=ALU.mult),
                          reads=[bg_], writes=[bt_, pb2])
                    if br == 1:
                        kb.op("pool", lambda e, t_=t_: e.tensor_tensor(out=macc[:], in0=macc[:], in1=t_[:], op=ALU.add), reads=[bt_], writes=[b_macc])
                    else:
                        kb.op("pool", lambda e, t_=t_, oc=oc: e.tensor_tensor(out=mT[:, oc, :], in0=macc[:], in1=t_[:], op=ALU.add),
                              reads=[bt_, b_macc], writes=[b_mT])
            if nxt and oc % 2 == 1:
                norm_post(blk + 1, oc // 2)
        for t in range(4):
            o_, bo_ = ot[oi % 2], b_ot[oi % 2]
            oi += 1
            for n in range(2):
                ps, pb = kb.bank()
                for k in range(8):
                    kb.op("pe", lambda e, k=k, t=t, n=n, ps=ps: e.matmul(ps[:], lhsT=mT[:, k, t * 128:(t + 1) * 128], rhs=wout[:, k, n * 512:(n + 1) * 512],
                                                                  start=(k == 0), stop=(k == 7)), reads=[b_wout, b_mT], writes=[pb])
                kb.op("dve", lambda e, n=n, ps=ps, o_=o_: e.tensor_tensor(out=o_[:, n * 512:(n + 1) * 512], in0=ps[:], in1=gt_bc[:, n * 512:(n + 1) * 512],
                                                                   op=ALU.mult), reads=[b_gt], writes=[bo_, pb])
            xr_, bxr_ = xr[oi % 2], b_xr[oi % 2]
            kb.dma("sp", xr_[:], x[blk * 512 + t * 128:blk * 512 + (t + 1) * 128, :], writes=[bxr_])
            kb.op("pool", lambda e, o_=o_, xr_=xr_: e.tensor_tensor(out=o_[:], in0=o_[:], in1=xr_[:], op=ALU.add), reads=[bxr_], writes=[bo_])
            kb.dma("sp", x1_d[blk * 512 + t * 128:blk * 512 + (t + 1) * 128, :], o_[:], reads=[bo_])
    _end(kb, own)
    return nc, kb


def build_c2(ntok=TOK, final=False, nc=None, kb=None, io=None, tag=""):
    nc, kb, own = _begin(nc, kb, tag)
    x = _dram(nc, io, "x", [ntok, D_MODEL], F32, "ExternalInput")
    cvec = _dram(nc, io, "c", [D_MODEL], F32, "ExternalInput")
    adaw = _dram(nc, io, "adaw", [D_MODEL, 3072], F32, "ExternalInput")
    adab = _dram(nc, io, "adab", [3072], F32, "ExternalInput")
    g2 = _dram(nc, io, "g2", [D_MODEL], F32, "ExternalInput")
    w1_d = _dram(nc, io, "w1", [D_MODEL, 4096], F32, "ExternalInput")
    w2_d = _dram(nc, io, "w2", [4096, D_MODEL], F32, "ExternalInput")
    fg_d = _dram(nc, io, "fg", [D_MODEL], F32, "ExternalInput")
    x2_d = _dram(nc, io, "x2", [ntok, D_MODEL], F32, "ExternalOutput")

    ident, identb = emit_identity(kb, nc)
    modT, bm = emit_mod(kb, nc, cvec, adaw, adab, 2, "mod")
    A, Bt, abb = emit_mod_AB(kb, nc, modT, bm, g2, 0, 1, "n2")
    gt_bc, b_gt = emit_mod_rows(kb, nc, cvec, adaw, adab, 2048, 1024, "gt")
    fg_bc = None
    if final:
        fg_bc = kb.T("fg_bc", [128, D_MODEL], F32)
        with nc.allow_non_contiguous_dma(reason="bcast"):
            kb.dma("sp", fg_bc[:], fg_d.unsqueeze(0).to_broadcast([128, D_MODEL]), writes=[b_gt])

    w1 = kb.T("w1_sb", [128, 8, 4096], BF16)
    w2 = kb.T("w2_sb", [128, 32, 1024], BF16)
    b_w1 = [Buf(f"w1_{i}") for i in range(4)]
    b_w2 = Buf("w2")
    for bi, c0 in enumerate(range(0, 4096, 1024)):
        for k in range(8):
            kb.dma("pool", w1[:, k, c0:c0 + 1024], w1_d[k * 128:(k + 1) * 128, c0:c0 + 1024], writes=[b_w1[bi]])
    for k in range(32):
        kb.dma("pool", w2[:, k, :], w2_d[k * 128:(k + 1) * 128, :], writes=[b_w2])

    BT = 256
    NB = ntok // BT
    xst = [kb.T(f"xst{i}", [128, D_MODEL], F32) for i in range(2)]
    b_xst = [Buf(), Buf()]
    xr = kb.T("xr", [128, D_MODEL], F32)
    b_xr = Buf()
    nsl = norm_slots(kb, 2)
    fsc = dict(j=kb.T("fjunk", [128, D_MODEL], BF16), ss=kb.T("fss", [128, 1], F32), r=kb.T("frstd", [128, 1], F32), b=Buf()) if final else None
    hTs = [kb.T(f"hT{i}", [128, 8, BT], BF16) for i in range(2)]
    hbs = [Buf(), Buf()]
    uT = kb.T("uT", [128, 32, BT], BF16)
    b_u = Buf()
    rl = [kb.T(f"rl{i}", [128, 2, BT], F32) for i in range(2)]
    b_rl = [Buf(), Buf()]
    ot = [kb.T(f"ot{i}", [128, D_MODEL], F32) for i in range(2)]
    b_ot = [Buf(), Buf()]
    ri = 0
    oi = 0
    def norm_pre(blk):
        for t in range(2):
            ti = blk * 2 + t
            kb.dma("sp", xst[t][:], x[ti * 128:(ti + 1) * 128, :], writes=[b_xst[t]])
            emit_norm_pre(kb, xst[t][:], b_xst[t], nsl[t])

    def norm_post(blk, t):
        emit_norm_post(kb, nsl[t], hTs[blk % 2][:, :, t * 128:(t + 1) * 128], hbs[blk % 2], A, Bt, abb, ident, identb)

    norm_pre(0)
    for t in range(2):
        norm_post(0, t)
    for blk in range(NB):
        hT, hb = hTs[blk % 2], hbs[blk % 2]
        nxt = blk + 1 < NB
        if nxt:
            norm_pre(blk + 1)
        for fp in range(16):
            if nxt and fp in (6, 12):
                norm_post(blk + 1, 0 if fp == 6 else 1)
            ps, pb = kb.bank()
            for j in range(2):
                fc = fp * 2 + j
                for k in range(8):
                    kb.op("pe", lambda e, k=k, fc=fc, j=j, ps=ps: e.matmul(ps[:, j * BT:(j + 1) * BT], lhsT=w1[:, k, fc * 128:(fc + 1) * 128], rhs=hT[:, k, :],
                                                                    start=(k == 0), stop=(k == 7)), reads=[b_w1[fc // 8], hb], writes=[pb])
            r_, br_ = rl[ri % 2], b_rl[ri % 2]
            ri += 1
            kb.op("act", lambda e, ps=ps, r_=r_: e.activation(out=r_[:].rearrange("p a b -> p (a b)"), in_=ps[:], func=AF.Relu), writes=[br_, pb])
            kb.op("dve", lambda e, r_=r_, fp=fp: e.tensor_tensor(out=uT[:, fp * 2:fp * 2 + 2, :], in0=r_[:], in1=r_[:], op=ALU.mult),
                  reads=[br_], writes=[b_u])
        for t in range(2):
            o_, bo_ = ot[oi % 2], b_ot[oi % 2]
            oi += 1
            for n in range(2):
                ps, pb = kb.bank()
                for k in range(32):
                    kb.op("pe", lambda e, k=k, t=t, n=n, ps=ps: e.matmul(ps[:], lhsT=uT[:, k, t * 128:(t + 1) * 128], rhs=w2[:, k, n * 512:(n + 1) * 512],
                                                                  start=(k == 0), stop=(k == 31)), reads=[b_w2, b_u], writes=[pb])
                kb.op("dve", lambda e, n=n, ps=ps, o_=o_: e.tensor_tensor(out=o_[:, n * 512:(n + 1) * 512], in0=ps[:], in1=gt_bc[:, n * 512:(n + 1) * 512],
                                                                   op=ALU.mult), reads=[b_gt], writes=[bo_, pb])
            kb.dma("sp", xr[:], x[blk * BT + t * 128:blk * BT + (t + 1) * 128, :], writes=[b_xr])
            kb.op("pool", lambda e, o_=o_: e.tensor_tensor(out=o_[:], in0=o_[:], in1=xr[:], op=ALU.add), reads=[b_xr], writes=[bo_])
            if final:
                kb.op("act", lambda e, o_=o_: e.activation(out=fsc["j"][:], in_=o_[:], func=AF.Square, accum_out=fsc["ss"][:]), reads=[bo_], writes=[fsc["b"]])
                kb.op("dve", lambda e: e.tensor_scalar(out=fsc["ss"][:], in0=fsc["ss"][:], scalar1=1.0 / D_MODEL, scalar2=EPS, op0=ALU.mult, op1=ALU.add),
                      writes=[fsc["b"]])
                kb.op("act", lambda e: e.activation(out=fsc["ss"][:], in_=fsc["ss"][:], func=AF.Sqrt), writes=[fsc["b"]])
                kb.op("dve", lambda e: e.reciprocal(out=fsc["r"][:], in_=fsc["ss"][:]), writes=[fsc["b"]])
                kb.op("dve", lambda e, o_=o_: e.scalar_tensor_tensor(out=o_[:], in0=o_[:], scalar=fsc["r"][:, 0:1], in1=fg_bc[:], op0=ALU.mult, op1=ALU.mult),
                      reads=[fsc["b"], b_gt], writes=[bo_])
            kb.dma("sp", x2_d[blk * BT + t * 128:blk * BT + (t + 1) * 128, :], o_[:], reads=[bo_])
    _end(kb, own)
    return nc, kb


def build_fused(S=SEQ, depth=DEPTH):
    nc = bass.Bass("TRN2", target_bir_lowering=False)
    kb = KB(nc)
    kb.init_banks()
    EI, EO = "ExternalInput", "ExternalOutput"
    din = lambda name, shape, dt=F32: nc.dram_tensor(name, shape, dt, kind=EI).ap()
    dint = lambda name, shape, dt=F32: nc.dram_tensor(name, shape, dt, kind="Internal").ap()
    x = din("x", [S, D_MODEL])
    c = din("c", [D_MODEL])
    slopes = din("slopes", [2, 4])
    fg = din("fg", [D_MODEL])
    out = nc.dram_tensor("out", [S, D_MODEL], F32, kind=EO).ap()
    L = []
    for l in range(depth):
        L.append(dict(
            adaw=din(f"adaw{l}", [D_MODEL, 6144]), adab=din(f"adab{l}", [6144]), g1=din(f"g1_{l}", [D_MODEL]), g2=din(f"g2_{l}", [D_MODEL]),
            wA=din(f"wA{l}", [D_MODEL, A_NCOL]), wmg=din(f"wmg{l}", [D_MODEL, 3072]),
            phk1=din(f"phk1_{l}", [2048, 128]), phv1=din(f"phv1_{l}", [2048, 128]), phk2=din(f"phk2_{l}", [128, 64]), phv2=din(f"phv2_{l}", [128, 64]),
            posk=din(f"posk{l}", [32, 64]), posv=din(f"posv{l}", [32, 64]), pw=din(f"pw{l}", [4, 128, 128]), psc=din(f"psc{l}", [512]),
            convw=din(f"convw{l}", [4, 1536]), Alog=din(f"Alog{l}", [1, 4]), dtb=din(f"dtb{l}", [1, 4]), ng=din(f"ng{l}", [1, 128]),
            wbr=din(f"wbr{l}", [3, 512, D_MODEL]), wout=din(f"wout{l}", [D_MODEL, D_MODEL]),
            w1=din(f"w1_{l}", [D_MODEL, 4096]), w2=din(f"w2_{l}", [4096, D_MODEL])))
    fm_bf = dint("s_fm_bf", [A_FM_BF * 128, S], BF16)
    fm_f32 = dint("s_fm_f32", [A_FM_F32 * 128, S], F32)
    tm_bf = dint("s_tm_bf", [S, 256], BF16)
    tm_f32 = dint("s_tm_f32", [S, 544], F32)
    yT = dint("s_yT", [3, 512, S], BF16)
    x1 = dint("s_x1", [S, D_MODEL], F32)
    x2 = dint("s_x2", [S, D_MODEL], F32)
    rows_k = dint("s_rows_k", [4, S], BF16)
    rows_q = dint("s_rows_q", [8, 4, S], BF16)
    rows_c = dint("s_rows_c", [4, S // 16], BF16)
    build_nsa_rows(S, nc, kb, {"slopes": slopes.rearrange("g h -> (g h)").unsqueeze(0), "rows_k": rows_k, "rows_q": rows_q, "rows_c": rows_c})
    xin = x
    for l in range(depth):
        P = L[l]
        build_phaseA(S, nc, kb, tag=f"L{l}A_", io={"x": xin, "c": c, "adaw": P["adaw"][:, 0:2048], "adab": P["adab"][0:2048], "g1": P["g1"],
                                                "w": P["wA"], "fm_bf": fm_bf, "fm_f32": fm_f32, "tm_bf": tm_bf, "tm_f32": tm_f32})
        for g in range(2):
            build_nsa(S, nc, kb, tag=f"L{l}N{g}_", io={
                "qT": fm_bf[g * 256:(g + 1) * 256, :].rearrange("(h d) s -> h d s", d=64),
                "kcT": fm_bf[512 + g * 64:512 + (g + 1) * 64, :], "vcT": fm_bf[640 + g * 64:640 + (g + 1) * 64, :],
                "ksT": fm_bf[768 + g * 64:768 + (g + 1) * 64, :], "kwT": fm_bf[896 + g * 64:896 + (g + 1) * 64, :],
                "vs": tm_bf[:, g * 64:(g + 1) * 64], "vw": tm_bf[:, 128 + g * 64:128 + (g + 1) * 64],
                "gate": tm_f32[:, g * 12:(g + 1) * 12],
                "phk1": P["phk1"], "phv1": P["phv1"], "phk2": P["phk2"], "phv2": P["phv2"], "posk": P["posk"], "posv": P["posv"],
                "slopes": slopes[g:g + 1, :], "yaT": yT[0, g * 256:(g + 1) * 256, :],
                "rows_k": rows_k, "rows_q": rows_q[4 * g:4 * g + 4], "rows_c": rows_c})
        for p in range(2):
            build_dn(S, nc, kb, tag=f"L{l}D{p}_", io={
                "xq": fm_f32[512:2048, :].rearrange("(w c) s -> w c s", w=3)[:, 2 * p * 128:(2 * p + 2) * 128, :],
                "convw": P["convw"].rearrange("k (w c) -> k w c", w=3)[:, :, 2 * p * 128:(2 * p + 2) * 128],
                "z": tm_f32[:, 32 + 2 * p * 128:32 + (2 * p + 2) * 128],
                "blog": tm_f32[:, 24 + 2 * p:24 + 2 * p + 2], "alog": tm_f32[:, 28 + 2 * p:28 + 2 * p + 2],
                "Alog": P["Alog"][:, 2 * p:2 * p + 2], "dtb": P["dtb"][:, 2 * p:2 * p + 2], "ng": P["ng"],
                "ycT": yT[2, p * 256:(p + 1) * 256, :]})
        HP = min(S, 4096)
        for hh in range(S // HP):
            build_pool(HP, nc, kb, tag=f"L{l}P{hh}_", t0=hh * HP, io={
                "pT": fm_f32[0:512, :], "pw": P["pw"], "psc": P["psc"], "ybT": yT[1, :, hh * HP:(hh + 1) * HP]})
        build_c1(S, nc, kb, tag=f"L{l}C1_", io={"x": xin, "c": c, "adaw": P["adaw"][:, 0:3072], "adab": P["adab"][0:3072], "g1": P["g1"],
                                               "wmg": P["wmg"], "wbr": P["wbr"], "wout": P["wout"], "yT": yT, "x1": x1})
        last = (l == depth - 1)
        build_c2(S, last, nc, kb, tag=f"L{l}C2_", io={"x": x1, "c": c, "adaw": P["adaw"][:, 3072:6144], "adab": P["adab"][3072:6144],
                                                     "g2": P["g2"], "w1": P["w1"], "w2": P["w2"], "fg": fg, "x2": out if last else x2})
        xin = x2
    kb.finish("sp")
    return nc, kb


def fused_inputs(b, x, c, ada_w, ada_b, norm1_g, norm2_g, w_in, phi_k1, phi_k2, phi_v1, phi_v2, pos_k, pos_v,
                 pool_w, pool_scale, dn_conv_w, dn_A_log, dn_dt_bias, dn_norm_g, w_branch_nsa, w_branch_pool,
                 w_branch_dn, w_out, mlp_w1, mlp_w2, final_g, S=SEQ, depth=DEPTH):
    f32 = np.float32
    idxA, _ = phaseA_weight_perm()
    o = np.cumsum((0,) + IN_SIZES)
    A_ = lambda a: _c(np.asarray(a, f32))
    slopes = (2.0 ** (-(np.arange(8, dtype=np.float64) + 1.0))).astype(f32).reshape(2, 4)
    im = {"x": A_(x[b][:S]), "c": A_(c[b]), "slopes": slopes, "fg": A_(final_g)}
    for l in range(depth):
        w_in_l = np.asarray(w_in[l], f32)
        im.update({
            f"adaw{l}": A_(ada_w[l]), f"adab{l}": A_(ada_b[l]), f"g1_{l}": A_(norm1_g[l]), f"g2_{l}": A_(norm2_g[l]),
            f"wA{l}": _c(w_in_l[:, idxA]), f"wmg{l}": _c(w_in_l[:, o[13]:o[14]]),
            f"phk1_{l}": A_(phi_k1[l]), f"phv1_{l}": A_(phi_v1[l]), f"phk2_{l}": A_(phi_k2[l]), f"phv2_{l}": A_(phi_v2[l]),
            f"posk{l}": A_(pos_k[l]), f"posv{l}": A_(pos_v[l]), f"pw{l}": A_(pool_w[l]), f"psc{l}": A_(pool_scale[l]),
            f"convw{l}": A_(dn_conv_w[l]), f"Alog{l}": A_(dn_A_log[l]).reshape(1, 4), f"dtb{l}": A_(dn_dt_bias[l]).reshape(1, 4),
            f"ng{l}": A_(dn_norm_g[l]).reshape(1, 128),
            f"wbr{l}": _c(np.stack([np.asarray(w_branch_nsa[l], f32), np.asarray(w_branch_pool[l], f32), np.asarray(w_branch_dn[l], f32)])),
            f"wout{l}": A_(w_out[l]), f"w1_{l}": A_(mlp_w1[l]), f"w2_{l}": A_(mlp_w2[l])})
    return im


IN_SIZES = (512, 128, 128, 128, 128, 128, 128, 24, 512, 1536, 512, 4, 4, 3072)
_PROGS = {}


def _prog(name, fn, *args):
    key = (name,) + args
    if key not in _PROGS:
        _PROGS[key] = fn(*args)[0]
    return _PROGS[key]


def _run(nc, in_maps):
    res = run_bass_kernel_spmd(nc, in_maps, core_ids=list(range(N_CORES)))
    return res.results


def _c(a):
    return np.ascontiguousarray(a)


def kernel(**inputs):
    nc = _prog("F", build_fused, SEQ, DEPTH)
    base = fused_inputs(0, **inputs)
    ims = []
    for i in range(N_CORES):
        b = i % BATCH
        im = dict(base)
        im["x"] = _c(np.asarray(inputs["x"][b], np.float32))
        im["c"] = _c(np.asarray(inputs["c"][b], np.float32))
        ims.append(im)
    res = _run(nc, ims)
    return np.stack([np.asarray(res[b]["out"], np.float32) for b in range(BATCH)])


def kernel_unfused(x, c, ada_w, ada_b, norm1_g, norm2_g, w_in, phi_k1, phi_k2, phi_v1, phi_v2, pos_k, pos_v,
                   pool_w, pool_scale, dn_conv_w, dn_A_log, dn_dt_bias, dn_norm_g, w_branch_nsa, w_branch_pool,
                   w_branch_dn, w_out, mlp_w1, mlp_w2, final_g):
    f32 = np.float32
    x = np.asarray(x, f32)
    c = np.asarray(c, f32)
    S = SEQ
    H = TOK
    idxA, off = phaseA_weight_perm()
    slopes = (2.0 ** (-(np.arange(8, dtype=np.float64) + 1.0))).astype(f32)
    ncA = _prog("A", build_phaseA, TOK)
    ncN = _prog("N", build_nsa, SEQ)
    ncD = _prog("D", build_dn, SEQ)
    ncP = _prog("P", build_pool, TOK)
    ncC1 = _prog("C1", build_c1, TOK)
    xcur = x
    cores = [(ci // 2, ci % 2) for ci in range(N_CORES)]
    for l in range(DEPTH):
        adaw_l = np.asarray(ada_w[l], f32)
        adab_l = np.asarray(ada_b[l], f32)
        w_in_l = np.asarray(w_in[l], f32)
        wA = _c(w_in_l[:, idxA])
        adaw1 = _c(adaw_l[:, 0:2048])
        adab1 = _c(adab_l[0:2048])
        ims = [{"x": _c(xcur[b, p * H:(p + 1) * H]), "c": _c(c[b]), "adaw": adaw1, "adab": adab1,
                "g1": _c(np.asarray(norm1_g[l], f32)), "w": wA} for (b, p) in cores]
        rA = _run(ncA, ims)
        fm_bf = [np.concatenate([rA[2 * b]["fm_bf"], rA[2 * b + 1]["fm_bf"]], axis=1) for b in range(BATCH)]
        fm_f32 = [np.concatenate([rA[2 * b]["fm_f32"], rA[2 * b + 1]["fm_f32"]], axis=1) for b in range(BATCH)]
        tm_bf = [np.concatenate([rA[2 * b]["tm_bf"], rA[2 * b + 1]["tm_bf"]], axis=0) for b in range(BATCH)]
        tm_f32 = [np.concatenate([rA[2 * b]["tm_f32"], rA[2 * b + 1]["tm_f32"]], axis=0) for b in range(BATCH)]
        del rA
        ims = []
        for (b, g) in cores:
            fb = fm_bf[b]
            ims.append({
                "qT": _c(fb[g * 256:(g + 1) * 256].reshape(4, 64, S)),
                "kcT": _c(fb[512 + g * 64:512 + (g + 1) * 64]), "vcT": _c(fb[640 + g * 64:640 + (g + 1) * 64]),
                "ksT": _c(fb[768 + g * 64:768 + (g + 1) * 64]), "kwT": _c(fb[896 + g * 64:896 + (g + 1) * 64]),
                "vs": _c(tm_bf[b][:, g * 64:(g + 1) * 64]), "vw": _c(tm_bf[b][:, 128 + g * 64:128 + (g + 1) * 64]),
                "gate": _c(tm_f32[b][:, g * 12:(g + 1) * 12]),
                "phk1": _c(np.asarray(phi_k1[l], f32)), "phv1": _c(np.asarray(phi_v1[l], f32)),
                "phk2": _c(np.asarray(phi_k2[l], f32)), "phv2": _c(np.asarray(phi_v2[l], f32)),
                "posk": _c(np.asarray(pos_k[l], f32)), "posv": _c(np.asarray(pos_v[l], f32)),
                "slopes": _c(slopes[4 * g:4 * g + 4].reshape(1, 4)),
            })
        rN = _run(ncN, ims)
        yaT = [np.concatenate([rN[2 * b]["yaT"], rN[2 * b + 1]["yaT"]], axis=0) for b in range(BATCH)]
        del rN
        cwfull = np.asarray(dn_conv_w[l], f32)
        ims = []
        for (b, p) in cores:
            heads = [2 * p, 2 * p + 1]
            dq = fm_f32[b][512:2048]
            rows = np.concatenate([dq[w_ * 512 + h * 128:w_ * 512 + (h + 1) * 128] for w_ in range(3) for h in heads], axis=0)
            xq = _c(rows.reshape(3, 256, S))
            cw = np.concatenate([cwfull[:, w_ * 512 + h * 128:w_ * 512 + (h + 1) * 128] for w_ in range(3) for h in heads], axis=1)
            tf = tm_f32[b]
            ims.append({"xq": xq, "convw": _c(cw.reshape(4, 3, 256)), "z": _c(tf[:, 32 + heads[0] * 128:32 + (heads[1] + 1) * 128]),
                        "blog": _c(tf[:, 24 + heads[0]:24 + heads[1] + 1]), "alog": _c(tf[:, 28 + heads[0]:28 + heads[1] + 1]),
                        "Alog": _c(np.asarray(dn_A_log[l], f32)[heads].reshape(1, 2)),
                        "dtb": _c(np.asarray(dn_dt_bias[l], f32)[heads].reshape(1, 2)),
                        "ng": _c(np.asarray(dn_norm_g[l], f32).reshape(1, 128))})
        rD = _run(ncD, ims)
        ycT = [np.concatenate([rD[2 * b]["ycT"], rD[2 * b + 1]["ycT"]], axis=0) for b in range(BATCH)]
        del rD
        ims = []
        for (b, p) in cores:
            pin = fm_f32[b][0:512]
            W = 16 + H
            pT = np.zeros((512, W), f32)
            valid = np.zeros((128, W), f32)
            lo = p * H - 16
            s0 = max(lo, 0)
            pT[:, s0 - lo:] = pin[:, s0:p * H + H]
            valid[:, s0 - lo:] = 1.0
            ims.append({"pT": pT, "valid": valid, "pw": _c(np.asarray(pool_w[l], f32)), "psc": _c(np.asarray(pool_scale[l], f32))})
        rP = _run(ncP, ims)
        ybT = [np.concatenate([rP[2 * b]["ybT"], rP[2 * b + 1]["ybT"]], axis=1) for b in range(BATCH)]
        del rP, fm_bf, fm_f32, tm_bf, tm_f32
        o = np.cumsum((0,) + IN_SIZES)
        wmg = _c(w_in_l[:, o[13]:o[14]])
        wbr = _c(np.stack([np.asarray(w_branch_nsa[l], f32), np.asarray(w_branch_pool[l], f32), np.asarray(w_branch_dn[l], f32)]))
        adaw3 = _c(adaw_l[:, 0:3072])
        adab3 = _c(adab_l[0:3072])
        ims = []
        for (b, p) in cores:
            sl = slice(p * H, (p + 1) * H)
            ims.append({"x": _c(xcur[b, sl]), "c": _c(c[b]), "adaw": adaw3, "adab": adab3, "g1": _c(np.asarray(norm1_g[l], f32)),
                        "wmg": wmg, "wbr": wbr, "wout": _c(np.asarray(w_out[l], f32)),
                        "yT": _c(np.stack([yaT[b][:, sl], ybT[b][:, sl], ycT[b][:, sl]]))})
        rC1 = _run(ncC1, ims)
        x1 = np.stack([np.concatenate([rC1[2 * b]["x1"], rC1[2 * b + 1]["x1"]], axis=0) for b in range(BATCH)])
        del rC1
        ncC2 = _prog("C2", build_c2, TOK, l == DEPTH - 1)
        adaw6 = _c(adaw_l[:, 3072:6144])
        adab6 = _c(adab_l[3072:6144])
        ims = [{"x": _c(x1[b, p * H:(p + 1) * H]), "c": _c(c[b]), "adaw": adaw6, "adab": adab6, "g2": _c(np.asarray(norm2_g[l], f32)),
                "w1": _c(np.asarray(mlp_w1[l], f32)), "w2": _c(np.asarray(mlp_w2[l], f32)), "fg": _c(np.asarray(final_g, f32))}
               for (b, p) in cores]
        rC2 = _run(ncC2, ims)
        xcur = np.stack([np.concatenate([rC2[2 * b]["x2"], rC2[2 * b + 1]["x2"]], axis=0) for b in range(BATCH)])
        del rC2
    return xcur.astype(np.float32)
```
